# Optimizing a Trainium2 kernel written in Bass

```python
import jax, jax.numpy as jnp
from jax import lax
import numpy as np

D_MODEL = 1024
BATCH = 8
SEQ = 2048
DEPTH = 4

HEAD_DIM = 64
N_HEADS = D_MODEL // HEAD_DIM
ATTN_WIDTH = N_HEADS * HEAD_DIM
SWA_KV_HEADS = max(1, N_HEADS // 8)
SWA_GROUP = N_HEADS // SWA_KV_HEADS
SWA_WINDOW = 128
ROPE_THETA = 500000.0
ROPE_DIM = HEAD_DIM // 4
D_FF = 4 * D_MODEL
PLE_DIM = 256
Q_BLOCK = 128
N_MIXERS = 3
RMS_EPS = 1e-6
NEG_INF = -1e30
IN_COLS = (3 * ATTN_WIDTH,
           ATTN_WIDTH + 2 * SWA_KV_HEADS * HEAD_DIM,
           3 * ATTN_WIDTH + N_HEADS)

kernel_name = "interleaved_sb_swa_fox_hybrid"


def rms_norm(x, g):
    xf = x.astype(jnp.float32)
    y = xf * lax.rsqrt(jnp.mean(xf * xf, axis=-1, keepdims=True) + RMS_EPS)
    return (y * g.astype(jnp.float32)).astype(x.dtype)


def partial_rope(x, pos):
    half = ROPE_DIM // 2
    inv_freq = ROPE_THETA ** (-jnp.arange(half, dtype=jnp.float32) / half)
    ang = pos.astype(jnp.float32)[:, None] * inv_freq[None, :]
    cos = jnp.cos(ang)[None, :, None, :]
    sin = jnp.sin(ang)[None, :, None, :]
    xr = x[..., :ROPE_DIM].astype(jnp.float32)
    x1, x2 = xr[..., :half], xr[..., half:]
    rot = jnp.concatenate([x1 * cos - x2 * sin, x2 * cos + x1 * sin], axis=-1).astype(x.dtype)
    return jnp.concatenate([rot, x[..., ROPE_DIM:]], axis=-1)


def stick_breaking_attention(h, w_in):
    B, S, _ = h.shape
    proj = h @ w_in
    q = proj[..., :ATTN_WIDTH].reshape(B, S, N_HEADS, HEAD_DIM)
    k = proj[..., ATTN_WIDTH:2 * ATTN_WIDTH].reshape(B, S, N_HEADS, HEAD_DIM)
    v = proj[..., 2 * ATTN_WIDTH:].reshape(B, S, N_HEADS, HEAD_DIM)
    scale = HEAD_DIM ** -0.5
    outs = []
    for blk in range(S // Q_BLOCK):
        q0, q1 = blk * Q_BLOCK, (blk + 1) * Q_BLOCK
        z = jnp.einsum('bqhd,bkhd->bhqk', q[:, q0:q1], k[:, :q1]).astype(jnp.float32) * scale
        t = q0 + jnp.arange(Q_BLOCK)
        s = jnp.arange(q1)
        strict = s[None, :] < t[:, None]
        log_beta = jax.nn.log_sigmoid(z)
        log_one_minus = jnp.where(strict, jax.nn.log_sigmoid(-z), 0.0)
        tail = lax.cumsum(log_one_minus, axis=3, reverse=True) - log_one_minus
        w = jnp.where(strict, jnp.exp(log_beta + tail), 0.0)
        outs.append(jnp.einsum('bhqk,bkhd->bqhd', w.astype(v.dtype), v[:, :q1]))
    return jnp.concatenate(outs, axis=1).reshape(B, S, ATTN_WIDTH)


def sliding_window_sink_attention(h, w_in, sinks, pos):
    B, S, _ = h.shape
    nkv = SWA_KV_HEADS * HEAD_DIM
    proj = h @ w_in
    q = proj[..., :ATTN_WIDTH].reshape(B, S, N_HEADS, HEAD_DIM)
    k = proj[..., ATTN_WIDTH:ATTN_WIDTH + nkv].reshape(B, S, SWA_KV_HEADS, HEAD_DIM)
    v = proj[..., ATTN_WIDTH + nkv:].reshape(B, S, SWA_KV_HEADS, HEAD_DIM)
    q = partial_rope(q, pos)
    k = partial_rope(k, pos)
    nb = S // Q_BLOCK
    pad = ((0, 0), (Q_BLOCK, 0), (0, 0), (0, 0))
    kp = jnp.pad(k, pad).reshape(B, nb + 1, Q_BLOCK, SWA_KV_HEADS, HEAD_DIM)
    vp = jnp.pad(v, pad).reshape(B, nb + 1, Q_BLOCK, SWA_KV_HEADS, HEAD_DIM)
    kb = jnp.concatenate([kp[:, :-1], kp[:, 1:]], axis=2)
    vb = jnp.concatenate([vp[:, :-1], vp[:, 1:]], axis=2)
    qb = q.reshape(B, nb, Q_BLOCK, SWA_KV_HEADS, SWA_GROUP, HEAD_DIM)
    logits = jnp.einsum('bnqkgd,bnckd->bnkgqc', qb, kb).astype(jnp.float32) * (HEAD_DIM ** -0.5)
    blk = jnp.arange(nb)[:, None, None]
    q_pos = blk * Q_BLOCK + jnp.arange(Q_BLOCK)[None, :, None]
    k_pos = blk * Q_BLOCK + jnp.arange(2 * Q_BLOCK)[None, None, :] - Q_BLOCK
    diff = q_pos - k_pos
    mask = (diff >= 0) & (diff < SWA_WINDOW) & (k_pos >= 0)
    logits = jnp.where(mask[None, :, None, None], logits, NEG_INF)
    sink = sinks.astype(jnp.float32).reshape(SWA_KV_HEADS, SWA_GROUP)[None, None, :, :, None, None]
    m = jnp.maximum(jnp.max(logits, axis=-1, keepdims=True), sink)
    e = jnp.exp(logits - m)
    probs = e / (jnp.sum(e, axis=-1, keepdims=True) + jnp.exp(sink - m))
    o = jnp.einsum('bnkgqc,bnckd->bnqkgd', probs.astype(vb.dtype), vb)
    return o.reshape(B, S, ATTN_WIDTH)


def forgetting_attention(h, w_in, b_forget):
    B, S, _ = h.shape
    proj = h @ w_in
    q = proj[..., :ATTN_WIDTH].reshape(B, S, N_HEADS, HEAD_DIM)
    k = proj[..., ATTN_WIDTH:2 * ATTN_WIDTH].reshape(B, S, N_HEADS, HEAD_DIM)
    v = proj[..., 2 * ATTN_WIDTH:3 * ATTN_WIDTH].reshape(B, S, N_HEADS, HEAD_DIM)
    log_f = jax.nn.log_sigmoid(proj[..., 3 * ATTN_WIDTH:].astype(jnp.float32)
                               + b_forget.astype(jnp.float32))
    cum = lax.cumsum(log_f, axis=1).transpose(0, 2, 1)
    scale = HEAD_DIM ** -0.5
    outs = []
    for blk in range(S // Q_BLOCK):
        q0, q1 = blk * Q_BLOCK, (blk + 1) * Q_BLOCK
        logits = jnp.einsum('bqhd,bkhd->bhqk', q[:, q0:q1], k[:, :q1]).astype(jnp.float32) * scale
        logits = logits + cum[:, :, q0:q1, None] - cum[:, :, None, :q1]
        t = q0 + jnp.arange(Q_BLOCK)
        s = jnp.arange(q1)
        causal = s[None, :] <= t[:, None]
        probs = jax.nn.softmax(jnp.where(causal, logits, NEG_INF), axis=-1)
        outs.append(jnp.einsum('bhqk,bkhd->bqhd', probs.astype(v.dtype), v[:, :q1]))
    return jnp.concatenate(outs, axis=1).reshape(B, S, ATTN_WIDTH)


def squared_relu_mlp(h, w_up, w_down):
    return jnp.square(jax.nn.relu(h @ w_up)) @ w_down


def per_layer_input(x, p_i, ple_norm, w_gate, w_proj):
    gate = jax.nn.sigmoid((rms_norm(x, ple_norm) @ w_gate).astype(jnp.float32)).astype(x.dtype)
    return (p_i @ w_proj) * gate


def setup_inputs(seed: int = 0) -> dict:
    key = jax.random.key(seed)
    keys = iter(jax.random.split(key, 64))

    def nrm(shape, scale):
        return jax.random.normal(next(keys), shape, jnp.float32) * scale

    def gain():
        return 1.0 + nrm((D_MODEL,), 0.02)

    inp = {"x": nrm((BATCH, SEQ, D_MODEL), 1.0),
           "p": nrm((DEPTH, BATCH, SEQ, PLE_DIM), 1.0)}
    for i in range(DEPTH):
        kind = i % N_MIXERS
        inp[f"attn_norm_{i}"] = gain()
        inp[f"w_in_{i}"] = nrm((D_MODEL, IN_COLS[kind]), D_MODEL ** -0.5)
        inp[f"w_out_{i}"] = nrm((ATTN_WIDTH, D_MODEL), ATTN_WIDTH ** -0.5)
        if kind == 1:
            inp[f"sinks_{i}"] = nrm((N_HEADS,), 0.5)
        if kind == 2:
            inp[f"b_forget_{i}"] = jax.random.uniform(next(keys), (N_HEADS,), jnp.float32, 1.0, 4.0)
        inp[f"mlp_norm_{i}"] = gain()
        inp[f"w_up_{i}"] = nrm((D_MODEL, D_FF), D_MODEL ** -0.5)
        inp[f"w_down_{i}"] = nrm((D_FF, D_MODEL), D_FF ** -0.5)
        inp[f"ple_norm_{i}"] = gain()
        inp[f"w_ple_gate_{i}"] = nrm((D_MODEL, D_MODEL), D_MODEL ** -0.5)
        inp[f"w_ple_proj_{i}"] = nrm((PLE_DIM, D_MODEL), PLE_DIM ** -0.5)
    inp["final_norm"] = gain()
    return inp


def reference(x, p,
              attn_norm_0, w_in_0, w_out_0, mlp_norm_0, w_up_0, w_down_0, ple_norm_0, w_ple_gate_0, w_ple_proj_0,
              attn_norm_1, w_in_1, w_out_1, sinks_1, mlp_norm_1, w_up_1, w_down_1, ple_norm_1, w_ple_gate_1, w_ple_proj_1,
              attn_norm_2, w_in_2, w_out_2, b_forget_2, mlp_norm_2, w_up_2, w_down_2, ple_norm_2, w_ple_gate_2, w_ple_proj_2,
              attn_norm_3, w_in_3, w_out_3, mlp_norm_3, w_up_3, w_down_3, ple_norm_3, w_ple_gate_3, w_ple_proj_3,
              final_norm):
    layers = [
        (attn_norm_0, w_in_0, w_out_0, None, mlp_norm_0, w_up_0, w_down_0, ple_norm_0, w_ple_gate_0, w_ple_proj_0),
        (attn_norm_1, w_in_1, w_out_1, sinks_1, mlp_norm_1, w_up_1, w_down_1, ple_norm_1, w_ple_gate_1, w_ple_proj_1),
        (attn_norm_2, w_in_2, w_out_2, b_forget_2, mlp_norm_2, w_up_2, w_down_2, ple_norm_2, w_ple_gate_2, w_ple_proj_2),
        (attn_norm_3, w_in_3, w_out_3, None, mlp_norm_3, w_up_3, w_down_3, ple_norm_3, w_ple_gate_3, w_ple_proj_3),
    ]
    pos = jnp.arange(x.shape[1], dtype=jnp.int32)
    for i in range(DEPTH):
        an, wi, wo, extra, mn, wu, wd, pn, wg, wp = layers[i]
        kind = i % N_MIXERS
        h = rms_norm(x, an)
        if kind == 0:
            a = stick_breaking_attention(h, wi)
        elif kind == 1:
            a = sliding_window_sink_attention(h, wi, extra, pos)
        else:
            a = forgetting_attention(h, wi, extra)
        x = x + a @ wo
        x = x + squared_relu_mlp(rms_norm(x, mn), wu, wd)
        x = x + per_layer_input(x, p[i], pn, wg, wp)
    return rms_norm(x, final_norm)
```

```python
import numpy as np
from contextlib import ExitStack
import concourse.bass as bass
import concourse.mybir as mybir
from concourse.bass_utils import run_bass_kernel_spmd

F32 = mybir.dt.float32
BF16 = mybir.dt.bfloat16
AF = mybir.ActivationFunctionType
ALU = mybir.AluOpType

S_LEN = 2048
D = 1024
DEPTH = 4
NCORES = 8
KINDS = (0, 1, 2, 0)
EPS = 1e-6
FUSED = True
DBG_I = 3


class Sched:
    def __init__(self, nc, stack):
        self.nc = nc
        self.stack = stack
        self.engs = {"pe": nc.tensor, "act": nc.scalar, "dve": nc.vector, "pool": nc.gpsimd, "sp": nc.sync}
        self.sem = {k: stack.enter_context(nc.semaphore("s_" + k)) for k in self.engs}
        self.cnt = {k: 0 for k in self.engs}
        self.known = {k: {} for k in self.engs}
        self.wr = {}
        self.rd = {}
        self.semobj = dict(self.sem)
        self.dmacnt = {}

    def new_dma_sem(self, name):
        s = self.stack.enter_context(self.nc.semaphore(name))
        self.semobj[name] = s
        self.dmacnt[name] = 0
        return name

    def _need(self, eng, reads, writes):
        need = {}

        def add(c):
            if c is None:
                return
            k, v = c
            if need.get(k, 0) < v:
                need[k] = v
        for r in reads:
            add(self.wr.get(r))
        for r in writes:
            add(self.wr.get(r))
            for k, v in self.rd.get(r, {}).items():
                add((k, v))
        for k, v in need.items():
            if k == eng and eng == "pe":
                continue
            if self.known[eng].get(k, 0) >= v:
                continue
            self.engs[eng].wait_ge(self.semobj[k], v)
            self.known[eng][k] = v

    def _mark(self, clock, reads, writes):
        k, v = clock
        for r in reads:
            self.rd.setdefault(r, {})[k] = v
        for r in writes:
            self.wr[r] = clock
            self.rd[r] = {}

    def op(self, eng, fn, reads=(), writes=()):
        self._need(eng, reads, writes)
        inst = fn()
        self.cnt[eng] += 1
        inst.then_inc(self.sem[eng], 1)
        self._mark((eng, self.cnt[eng]), reads, writes)
        return inst

    def dma(self, queue, semname, out, in_, reads=(), writes=(), **kw):
        self._need(queue, reads, writes)
        inst = self.engs[queue].dma_start(out=out, in_=in_, **kw)
        self.dmacnt[semname] += 16
        inst.then_inc(self.semobj[semname], 16)
        self._mark((semname, self.dmacnt[semname]), reads, writes)
        return inst

    def wait_all(self, eng, semnames):
        for s in semnames:
            if self.dmacnt[s] > 0:
                self.engs[eng].wait_ge(self.semobj[s], self.dmacnt[s])


def host_consts():
    a = np.arange(128)
    ident = np.eye(128, dtype=np.float32)
    ge = (a[:, None] >= a[None, :]).astype(np.float32)
    lt = (a[:, None] < a[None, :]).astype(np.float32)
    le = (a[:, None] <= a[None, :]).astype(np.float32)
    f = np.arange(256)
    band = ((f[None, :] - a[:, None] >= 0) & (f[None, :] - a[:, None] < 128)).astype(np.float32)
    cst = np.concatenate([ident, ge, lt, le, band], axis=1)
    half = 8
    inv_freq = (np.float32(500000.0) ** (-np.arange(half, dtype=np.float32) / np.float32(half))).astype(np.float32)
    ang = np.arange(S_LEN, dtype=np.float32)[:, None] * inv_freq[None, :]
    cos = np.cos(ang).astype(np.float32).T
    sin = np.sin(ang).astype(np.float32).T
    rope = np.zeros((128, 2, S_LEN), np.float32)
    rope[:, 0, :] = 1.0
    for hh in range(2):
        b = 64 * hh
        rope[b:b + 8, 0] = cos
        rope[b + 8:b + 16, 0] = cos
        rope[b:b + 8, 1] = -sin
        rope[b + 8:b + 16, 1] = sin
    return cst, rope


class _Stop(Exception):
    pass


def build(layers, final, dbg=None):
    nc = bass.Bass("TRN2", target_bir_lowering=False)
    dt_in = lambda n, s: nc.dram_tensor(n, list(s), F32, kind="ExternalInput").ap()
    x_in = dt_in("x", (S_LEN, D))
    p_in = dt_in("p", (DEPTH, S_LEN, 256))
    cst_in = dt_in("cst", (128, 768))
    rope_in = dt_in("rope", (128, 2, S_LEN))
    gains_in = dt_in("gains", (104, 128))
    sinks_in = dt_in("sinks", (1, 16))
    bfor_in = dt_in("bfor", (1, 16))
    W = {}
    for l in layers:
        k = KINDS[l]
        W[l] = dict(
            w_in=dt_in(f"w_in_{l}", (D, (3072, 1280, 3088)[k])),
            w_out=dt_in(f"w_out_{l}", (D, D)),
            w_up=dt_in(f"w_up_{l}", (D, 4096)),
            w_down=dt_in(f"w_down_{l}", (4096, D)),
            w_gate=dt_in(f"w_ple_gate_{l}", (D, D)),
            w_proj=dt_in(f"w_ple_proj_{l}", (256, D)),
        )
    y_out = nc.dram_tensor("y", [S_LEN, D], F32, kind="ExternalOutput").ap()

    with ExitStack() as st:
        S = Sched(nc, st)
        sb = lambda n, s, d: st.enter_context(nc.sbuf_tensor("sb_" + n, list(s), d))
        xT = sb("xT", (128, 8, S_LEN), F32)
        hT = sb("hT", (128, 8, S_LEN), BF16)
        big = sb("big", (128, 8, S_LEN), BF16)
        vp = sb("vp", (128, 16, 128), BF16)
        qk = sb("qk", (128, 2, S_LEN), BF16)
        wsl = [sb(f"ws{i}", (128, 4096), BF16) for i in range(3)]
        ebuf = sb("ebuf", (128, 2, 512), F32)
        gbuf = sb("gbuf", (128, 2, 512), F32)
        spb = sb("spb", (128, 2, 512), BF16)
        wb = sb("wb", (128, 2, 512), BF16)
        special = sb("special", (128, 4608), F32)
        cstf = sb("cstf", (128, 768), F32)
        cstb = sb("cstb", (128, 768), BF16)
        gains = sb("gains", (128, 104), F32)
        gstage = sb("gstage", (104, 128), F32)
        onesf = sb("onesf", (128, 128), F32)
        onesb = sb("onesb", (128, 128), BF16)
        small = sb("small", (128, 64), F32)
        rowst = sb("rowst", (1, 32), F32)
        psum = st.enter_context(nc.psum_tensor("psum", [128, 8, 512], F32))
        banks = [psum[:, i, :] for i in range(8)]
        wb2 = sb("wb2", (128, 2, 512), BF16)
        rstd = gbuf[:, 1, :]
        vp2 = sb("vp2", (128, 16, 128), BF16)
        ebuf3 = special[:, 1536:2560].rearrange("p (a b) -> p a b", a=2)
        ebuf2 = special[:, 0:1024].rearrange("p (a b) -> p a b", a=2)
        spb2 = special[:, 1024:1536].bitcast(BF16).rearrange("p (a b) -> p a b", a=2)
        BK = lambda i: ("bank", i)

        identf = cstf[:, 0:128]
        lef = cstf[:, 384:512]
        geb, ltb, leb, bandb = cstb[:, 128:256], cstb[:, 256:384], cstb[:, 384:512], cstb[:, 512:768]
        stage = [ebuf, gbuf]
        qT, kT = qk[:, 0, :], qk[:, 1, :]
        QKV = {"cur": 0}

        def qkv_views(kind, par):
            if par == 0:
                return qk[:, 0, :], qk[:, 1, :], vp
            if kind == 0:
                a, b = 2560, 3584
            else:
                a, b = 768, 3072
            return (special[:, a:a + 1024].bitcast(BF16), special[:, b:b + 1024].bitcast(BF16), vp2)
        CH = lambda n: slice(n * 512, (n + 1) * 512)

        d_st = [S.new_dma_sem("d_st0"), S.new_dma_sem("d_st1")]
        d_out = [S.new_dma_sem("d_out0"), S.new_dma_sem("d_out1")]
        d_w = [S.new_dma_sem(f"d_w{i}") for i in range(3)]

        S.dma("sp", S.new_dma_sem("d_c0"), cstf[:, :], cst_in, writes=["cstf"])
        S.dma("sp", S.new_dma_sem("d_c1"), gstage[:, :], gains_in, writes=["gstage"])
        S.dma("sp", S.new_dma_sem("d_c2"), rowst[:, 0:16], sinks_in, writes=["rowst0"])
        S.dma("sp", S.new_dma_sem("d_c3"), rowst[:, 16:32], bfor_in, writes=["rowst1"])
        S.op("dve", lambda: nc.vector.memset(onesf[:, :], 1.0), writes=["onesf"])
        S.op("dve", lambda: nc.vector.memset(onesb[:, :], 1.0), writes=["onesb"])
        S.op("dve", lambda: nc.vector.tensor_copy(out=cstb[:, :], in_=cstf[:, :]), reads=["cstf"], writes=["cstb"])
        S.op("pe", lambda: nc.tensor.transpose(banks[7][:, 0:104], gstage[:, :], identf[0:104, 0:104]),
             reads=["gstage", "cstf"], writes=[BK(7)])
        S.op("act", lambda: nc.scalar.copy(out=gains[:, :], in_=banks[7][:, 0:104]), reads=[BK(7)], writes=["gains"])

        blocks = []

        class WS:
            issued = 0
            cur = -1

        def ws_issue(upto):
            while WS.issued <= min(upto, len(blocks) - 1):
                b = WS.issued
                slot = b % 3
                for (dstfn, src) in blocks[b]:
                    S.dma("pool", d_w[slot], dstfn(wsl[slot]), src, writes=[("ws", slot)])
                WS.issued += 1

        def ws_next(hold=False):
            WS.cur += 1
            ws_issue(WS.cur if hold else WS.cur + 2)
            return wsl[WS.cur % 3], ("ws", WS.cur % 3)

        def v3(t, a, b):
            return t[:, 0:a * b].rearrange("p (a b) -> p a b", a=a)

        def wview(wd, c0, ncols):
            return wd.rearrange("(kc p) n -> p kc n", p=128)[:, :, c0:c0 + ncols]

        for l in layers:
            k = KINDS[l]
            wd = W[l]
            if k == 2:
                blocks.append([(lambda t: v3(t, 8, 512)[:, :, 0:16], wview(wd["w_in"], 3072, 16))])
            for p in range(8):
                if k in (0, 2):
                    blocks.append([
                        ((lambda t, o=o: v3(t, 8, 512)[:, :, o * 128:(o + 1) * 128]),
                         wview(wd["w_in"], o * 1024 + p * 128, 128)) for o in range(3)])
                else:
                    def kv_block(g):
                        kc0 = 1024 + 64 * g
                        ent = [(lambda t, o=o: v3(t, 8, 512)[:, :, o:o + 64], wview(wd["w_in"], kc0, 64)) for o in (0, 64)]
                        ent.append((lambda t: v3(t, 8, 512)[:, :, 256:320], wview(wd["w_in"], 1152 + 64 * g, 64)))
                        blocks.append(ent)
                    if p == 0:
                        kv_block(0)
                    blocks.append([(lambda t: v3(t, 8, 512)[:, :, 0:128], wview(wd["w_in"], p * 128, 128))])
                    if p == 4:
                        kv_block(1)
            for hf in range(2):
                blocks.append([(lambda t: v3(t, 8, 512), wview(wd["w_out"], hf * 512, 512))])
            for g in range(8):
                blocks.append([(lambda t: v3(t, 8, 512), wview(wd["w_up"], g * 512, 512))])
                blocks.append([(lambda t: v3(t, 4, 1024),
                               wd["w_down"][g * 512:(g + 1) * 512, :].rearrange("(kc p) n -> p kc n", p=128))])
            for hf in range(2):
                blocks.append([(lambda t: v3(t, 8, 512), wview(wd["w_gate"], hf * 512, 512))])
            blocks.append([(lambda t: v3(t, 2, 1024), wd["w_proj"].rearrange("(kc p) n -> p kc n", p=128))])

        def mm_group(bank_ap, pairs, first=True, skip=False):
            last = None
            n = len(pairs)
            for i, (l_, r_) in enumerate(pairs):
                last = nc.tensor.matmul(bank_ap, lhsT=l_, rhs=r_, start=(first and i == 0), stop=(i == n - 1),
                                        skip_group_check=skip)
            return last

        HT = lambda n: [("hT", c, n) for c in range(8)]
        bank_rr = [0]

        def next_bank(lo=0, hi=4):
            b = lo + bank_rr[0] % (hi - lo)
            bank_rr[0] += 1
            return b

        SR = lambda j: [("ebuf" if j == 0 else "gbuf", 0), ("ebuf" if j == 0 else "gbuf", 1)]

        def load_x():
            for i in range(16):
                stg = stage[i % 2]
                sv = stg[:, :, :].rearrange("p a b -> p (a b)")
                S.dma("sp", d_st[i % 2], sv, x_in[i * 128:(i + 1) * 128, :], writes=SR(i % 2))
                for half in range(2):
                    b = next_bank()

                    def f(b=b, half=half, sv=sv):
                        last = None
                        for cc in range(4):
                            c = half * 4 + cc
                            last = nc.tensor.transpose(banks[b][:, cc * 128:(cc + 1) * 128],
                                                       sv[:, c * 128:(c + 1) * 128], identf)
                        return last
                    S.op("pe", f, reads=SR(i % 2) + ["cstf"], writes=[BK(b)])
                    dst = xT[:, half * 4:half * 4 + 4, i * 128:(i + 1) * 128]
                    src = banks[b][:, :].rearrange("p (c t) -> p c t", c=4)
                    wr = [("xT", c, i // 4) for c in range(half * 4, half * 4 + 4)]
                    if half == 0:
                        S.op("act", lambda dst=dst, src=src: nc.scalar.copy(out=dst, in_=src), reads=[BK(b)], writes=wr)
                    else:
                        S.op("dve", lambda dst=dst, src=src: nc.vector.tensor_copy(out=dst, in_=src), reads=[BK(b)], writes=wr)

        def store_x():
            for i in range(16):
                stg = stage[i % 2]
                sv = stg[:, :, :].rearrange("p a b -> p (a b)")
                for half in range(2):
                    b = next_bank()

                    def f(b=b, half=half):
                        last = None
                        for cc in range(4):
                            c = half * 4 + cc
                            last = nc.tensor.transpose(banks[b][:, cc * 128:(cc + 1) * 128],
                                                       xT[:, c, i * 128:(i + 1) * 128], identf)
                        return last
                    S.op("pe", f, reads=[("xT", c, i // 4) for c in range(half * 4, half * 4 + 4)] + ["cstf"],
                         writes=[BK(b)])
                    dst = sv[:, half * 512:(half + 1) * 512]
                    if half == 0:
                        S.op("act", lambda dst=dst, b=b: nc.scalar.copy(out=dst, in_=banks[b][:, :]),
                             reads=[BK(b)], writes=[SR(i % 2)[0]])
                    else:
                        S.op("dve", lambda dst=dst, b=b: nc.vector.tensor_copy(out=dst, in_=banks[b][:, :]),
                             reads=[BK(b)], writes=[SR(i % 2)[1]])
                S.dma("sp", d_out[i % 2], y_out[i * 128:(i + 1) * 128, :], sv,
                      reads=SR(i % 2))

        def stage_guard():
            pass

        def norm_chunk(gidx, n, inplace=False):
            for c in range(8):
                sq = spb[:, c % 2, :]
                S.op("act", lambda sq=sq, c=c: nc.scalar.activation(out=sq, in_=xT[:, c, CH(n)], func=AF.Square),
                     reads=[("xT", c, n)], writes=[("spb", c % 2)])
                S.op("pe", lambda sq=sq, c=c: nc.tensor.matmul(banks[7][:, :], lhsT=onesb[:, :], rhs=sq,
                                                               start=(c == 0), stop=(c == 7)),
                     reads=[("spb", c % 2), "onesb"], writes=[BK(7)])
            S.op("act", lambda: nc.scalar.activation(out=rstd, in_=banks[7][:, :], func=AF.Ln,
                                                     scale=1.0 / D, bias=epsb[:, 0:1]),
                 reads=[BK(7), "epsb"], writes=[("gbuf", 1)])
            S.op("act", lambda: nc.scalar.activation(out=rstd, in_=rstd, func=AF.Exp, scale=-0.5),
                 reads=[("gbuf", 1)], writes=[("gbuf", 1)])
            for c in range(8):
                dst = xT[:, c, CH(n)] if inplace else hT[:, c, CH(n)]
                wr = [("xT", c, n)] if inplace else [("hT", c, n)]
                S.op("dve", lambda dst=dst, c=c: nc.vector.scalar_tensor_tensor(
                    out=dst, in0=xT[:, c, CH(n)], scalar=gains[:, gidx * 8 + c:gidx * 8 + c + 1], in1=rstd,
                    op0=ALU.mult, op1=ALU.mult), reads=[("xT", c, n), "gains", ("gbuf", 1)], writes=wr)


        def norm(gidx, inplace=False):
            for n in range(4):
                norm_chunk(gidx, n, inplace)

        epsb = sb("epsb", (128, 1), F32)
        S.op("dve", lambda: nc.vector.memset(epsb[:, :], EPS), writes=["epsb"])

        def resid_add(ft, n, b):
            S.op("dve", lambda: nc.vector.tensor_tensor(out=xT[:, ft, CH(n)], in0=banks[b][:, :], in1=xT[:, ft, CH(n)],
                                                        op=ALU.add), reads=[BK(b), ("xT", ft, n)], writes=[("xT", ft, n)])

        def proj_fm(wt, wres, col0, dst, dres, copy_eng="act", dpar=0, bank=None):
            wv = v3(wt, 8, 512)
            pcs = []
            for n in range(4):
                st_ = {}

                def half(h, n=n, st_=st_):
                    if h == 0:
                        st_["b"] = next_bank(0, 2) if bank is None else bank
                    b = st_["b"]
                    S.op("pe", lambda: mm_group(banks[b][:, :], [(wv[:, kc, col0:col0 + 128], hT[:, kc, CH(n)])
                                                                 for kc in range(4 * h, 4 * h + 4)], first=(h == 0), skip=True),
                         reads=[wres] + HT(n), writes=[BK(b)])

                def evac(n=n, st_=st_):
                    b = st_["b"]
                    if copy_eng == "act":
                        S.op("act", lambda: nc.scalar.copy(out=dst[:, CH(n)], in_=banks[b][:, :]),
                             reads=[BK(b)], writes=[(dres, dpar, n)])
                    else:
                        S.op("dve", lambda: nc.vector.tensor_copy(out=dst[:, CH(n)], in_=banks[b][:, :]),
                             reads=[BK(b)], writes=[(dres, dpar, n)])
                pcs += [lambda half=half: half(0), lambda half=half: half(1), evac]
            return pcs

        def proj_v(wt, wres, col0, ncol, vdst=None, dpar=0, bank=None, copy_eng="act"):
            wv = v3(wt, 8, 512)
            vdst = vp if vdst is None else vdst
            pcs = []
            for i4 in range(4):
                st_ = {}

                def half(h, i4=i4, st_=st_):
                    if h == 0:
                        st_["b"] = next_bank(0, 2) if bank is None else bank
                    b = st_["b"]

                    def f():
                        last = None
                        for ii in range(2 * h, 2 * h + 2):
                            i = i4 * 4 + ii
                            last = mm_group(banks[b][:, ii * 128:ii * 128 + ncol],
                                            [(hT[:, kc, i * 128:(i + 1) * 128], wv[:, kc, col0:col0 + ncol]) for kc in range(8)])
                        return last
                    S.op("pe", f, reads=[wres] + HT(i4), writes=[BK(b)])

                def evac(i4=i4, st_=st_):
                    b = st_["b"]
                    src = banks[b][:, :].rearrange("p (a c) -> p a c", a=4)[:, :, 0:ncol]
                    if copy_eng == "act":
                        S.op("act", lambda: nc.scalar.copy(out=vdst[:, i4 * 4:i4 * 4 + 4, 0:ncol], in_=src),
                             reads=[BK(b)], writes=[("vp", dpar)])
                    else:
                        S.op("dve", lambda: nc.vector.tensor_copy(out=vdst[:, i4 * 4:i4 * 4 + 4, 0:ncol], in_=src),
                             reads=[BK(b)], writes=[("vp", dpar)])
                pcs += [lambda half=half: half(0), lambda half=half: half(1), evac]
            return pcs

        def swap_cols(wt, wres):
            wv = v3(wt, 8, 512)
            src = wv[:, :, 0:128].rearrange("p k (h d) -> p k h d", h=2)
            dstv = wv[:, :, 128:256].rearrange("p k (h d) -> p k h d", h=2)
            for (a, b, n) in ((0, 8, 8), (8, 0, 8), (16, 16, 48)):
                S.op("act", lambda a=a, b=b, n=n: nc.scalar.copy(out=dstv[:, :, :, a:a + n], in_=src[:, :, :, b:b + n]),
                     reads=[wres], writes=[wres])

        def proj_rope(wt, wres, col0, dst, dres, dpar=0, bks=(0, 1), tmp=None, tmpn="gbuf"):
            wv = v3(wt, 8, 512)
            tmp = gbuf if tmp is None else tmp
            rp = special[:, 0:4096].rearrange("p (a t) -> p a t", a=2)
            pcs = []
            for n in range(4):
                def pa(n=n):
                    S.op("pe", lambda: mm_group(banks[bks[0]][:, :], [(wv[:, kc, col0:col0 + 128], hT[:, kc, CH(n)])
                                                                      for kc in range(8)]),
                         reads=[wres] + HT(n), writes=[BK(bks[0])])

                def pb(n=n):
                    S.op("pe", lambda: mm_group(banks[bks[1]][:, :], [(wv[:, kc, col0 + 128:col0 + 256], hT[:, kc, CH(n)])
                                                                      for kc in range(8)]),
                         reads=[wres] + HT(n), writes=[BK(bks[1])])

                def pc(n=n):
                    S.op("dve", lambda: nc.vector.tensor_tensor(out=tmp[:, 0, :], in0=banks[bks[0]][:, :], in1=rp[:, 0, CH(n)],
                                                                op=ALU.mult), reads=[BK(bks[0]), "special"], writes=[(tmpn, 0)])
                    S.op("dve", lambda: nc.vector.tensor_tensor(out=tmp[:, 1, :], in0=banks[bks[1]][:, :], in1=rp[:, 1, CH(n)],
                                                                op=ALU.mult), reads=[BK(bks[1]), "special"], writes=[(tmpn, 1)])
                    S.op("dve", lambda: nc.vector.tensor_tensor(out=dst[:, CH(n)], in0=tmp[:, 0, :], in1=tmp[:, 1, :],
                                                                op=ALU.add),
                         reads=[(tmpn, 0), (tmpn, 1)], writes=[(dres, dpar, n)])
                pcs += [pa, pb, pc]
            return pcs

        HP = (slice(0, 64), slice(64, 128))

        def zmm(i, t0, off, nq):
            def f():
                last = None
                for hh in range(2):
                    last = nc.tensor.matmul(banks[2 + hh][:, off:off + nq], lhsT=kT[HP[hh], i * 128:(i + 1) * 128],
                                            rhs=qT[HP[hh], t0 + off:t0 + off + nq], start=True, stop=True)
                return last
            return f

        def attn_sb(p, par=0, pieces=()):
            pieces = list(pieces)
            qTv, kTv, vpv = qkv_views(0, par)
            tiles = [(cq, i) for cq in range(4) for i in range(4 * cq + 3, -1, -1)]
            NT = len(tiles)
            EBT = [ebuf, ebuf2, ebuf3]
            SPT = [spb, spb2]
            EB = [[("ebuf", 0), ("ebuf", 1)], [("eb2", 0), ("eb2", 1)], [("eb3", 0), ("eb3", 1)]]
            SP = [[("spb", 0), ("spb", 1)], [("sp2", 0), ("sp2", 1)]]
            GB = [("gbuf", 0), ("gbuf", 1)]
            WB = [("wb", 0), ("wb", 1)]
            ZB = [2, 0]

            def geom(n):
                cq, i = tiles[n]
                off = max(0, (i - 4 * cq) * 128)
                return cq, i, off, slice(off, 512)

            def P1(n):
                cq, i, off, cs = geom(n)
                zb = ZB[n % 2]

                def f():
                    last = None
                    for hh in range(2):
                        last = nc.tensor.matmul(banks[zb + hh][:, cs], lhsT=kTv[HP[hh], i * 128:(i + 1) * 128],
                                                rhs=qTv[HP[hh], cq * 512 + off:(cq + 1) * 512], start=True, stop=True)
                    return last
                S.op("pe", f, reads=[("qT", par, cq), ("kT", par, i // 4)], writes=[BK(zb), BK(zb + 1)])

            def S1(n):
                cq, i, off, cs = geom(n)
                zb = ZB[n % 2]
                e = EBT[n % 3]
                S.op("act", lambda: nc.scalar.activation(out=e[:, :, cs], in_=psum[:, zb:zb + 2, cs], func=AF.Exp, scale=0.125),
                     reads=[BK(zb), BK(zb + 1)], writes=EB[n % 3])
                if i >= 4 * cq:
                    S.op("dve", lambda: nc.vector.tensor_tensor(
                        out=e[:, :, off:off + 128], in0=e[:, :, off:off + 128],
                        in1=ltb.unsqueeze(1).to_broadcast([128, 2, 128]), op=ALU.mult),
                        reads=EB[n % 3] + ["cstb"], writes=EB[n % 3])

            def S2(n):
                cq, i, off, cs = geom(n)
                S.op("act", lambda: nc.scalar.activation(out=SPT[n % 2][:, :, cs], in_=EBT[n % 3][:, :, cs], func=AF.Ln,
                                                         bias=1.0, scale=1.0), reads=EB[n % 3], writes=SP[n % 2])

            def P2(n):
                cq, i, off, cs = geom(n)
                first = (i == 4 * cq + 3)

                def f():
                    last = None
                    for hh in range(2):
                        last = nc.tensor.matmul(banks[4 + hh][:, cs], lhsT=geb, rhs=SPT[n % 2][:, hh, cs], start=first,
                                                stop=True, skip_group_check=True)
                    return last
                S.op("pe", f, reads=SP[n % 2] + ["cstb"], writes=[BK(4), BK(5)])

            def S3(n):
                cq, i, off, cs = geom(n)
                S.op("act", lambda: nc.scalar.activation(out=gbuf[:, :, cs], in_=psum[:, 4:6, cs], func=AF.Exp, scale=-1.0),
                     reads=[BK(4), BK(5)], writes=GB)

            def D2(n):
                cq, i, off, cs = geom(n)
                S.op("dve", lambda: nc.vector.tensor_tensor(out=wb[:, :, cs], in0=EBT[n % 3][:, :, cs], in1=gbuf[:, :, cs],
                                                            op=ALU.mult), reads=EB[n % 3] + GB, writes=WB)

            def LTB(n):
                cq, i, off, cs = geom(n)
                if i > 0:
                    def f():
                        last = None
                        for hh in range(2):
                            last = nc.tensor.matmul(banks[4 + hh][:, cs], lhsT=ltb, rhs=SPT[n % 2][:, hh, cs], start=False,
                                                    stop=True, skip_group_check=True)
                        return last
                    S.op("pe", f, reads=SP[n % 2] + ["cstb"], writes=[BK(4), BK(5)])

            def PV(n):
                cq, i, off, cs = geom(n)
                first = (i == 4 * cq + 3)

                def f2():
                    last = None
                    for hh in range(2):
                        last = nc.tensor.matmul(banks[6][HP[hh], cs], lhsT=vpv[:, i, HP[hh]], rhs=wb[:, hh, cs], start=first,
                                                stop=True, skip_group_check=True)
                    return last
                S.op("pe", f2, reads=WB + [("vp", par)], writes=[BK(6)])
                if i == 0:
                    S.op("dve", lambda: nc.vector.tensor_copy(out=big[:, p, CH(cq)], in_=banks[6][:, :]),
                         reads=[BK(6)], writes=[("big", p)])

            P1(0)
            S1(0)
            S2(0)
            P1(1)
            P2(0)
            for n in range(NT):
                if n + 1 < NT:
                    S1(n + 1)
                S3(n)
                if n + 2 < NT:
                    P1(n + 2)
                D2(n)
                if n + 1 < NT:
                    S2(n + 1)
                LTB(n)
                if n + 1 < NT:
                    P2(n + 1)
                PV(n)
                if pieces:
                    pieces.pop(0)()
            while pieces:
                pieces.pop(0)()

        def attn_fox(p, par=0, pieces=()):
            pieces = list(pieces)
            qTv, kTv, vpv = qkv_views(2, par)
            fx = special[:, :]
            nend = fx[:, 256:512].rearrange("p (j h) -> p j h", j=16)
            ncum = fx[:, 512:768].rearrange("p (j h) -> p j h", j=16)
            nmid = fx[:, 1792:2048].rearrange("p (j h) -> p j h", j=16)
            Rb = special[0:64, 2048:3072].bitcast(BF16)
            for hh in range(2):
                h = 2 * p + hh
                r0 = 32 * hh
                S.op("dve", lambda h=h, r0=r0: nc.vector.tensor_scalar(
                    out=Rb[r0:r0 + 1, :].rearrange("p (j c) -> p j c", j=16),
                    in0=nmid[r0:r0 + 1, :, h:h + 1].to_broadcast([1, 16, 128]), scalar1=-8.0, scalar2=None, op0=ALU.mult),
                    reads=["special"], writes=[("Rb", hh)])
            tiles = [(cq, i) for cq in range(4) for i in range(0, 4 * cq + 4)]
            NT = len(tiles)
            WT = [wb, wb2]
            WB = [[("wb", 0), ("wb", 1)], [("wb2", 0), ("wb2", 1)]]
            ZB = [2, 0]

            def geom(n):
                cq, i = tiles[n]
                off = max(0, (i - 4 * cq) * 128)
                return cq, i, off, slice(off, 512)

            def P1(n):
                cq, i, off, cs = geom(n)
                zb = ZB[n % 2]

                def f():
                    last = None
                    for hh in range(2):
                        nc.tensor.matmul(banks[zb + hh][:, cs], lhsT=kTv[HP[hh], i * 128:(i + 1) * 128],
                                         rhs=qTv[HP[hh], cq * 512 + off:(cq + 1) * 512], start=True, stop=False,
                                         skip_group_check=True)
                    for hh in range(2):
                        r0 = 32 * hh
                        last = nc.tensor.matmul(banks[zb + hh][:, cs], lhsT=onesb[r0:r0 + 1, :],
                                                rhs=Rb[r0:r0 + 1, cq * 512 + off:(cq + 1) * 512], start=False, stop=True,
                                                skip_group_check=True)
                    return last
                S.op("pe", f, reads=[("qT", par, cq), ("kT", par, i // 4), ("Rb", 0), ("Rb", 1), "onesb"], writes=[BK(zb), BK(zb + 1)])

            def SE(n):
                cq, i, off, cs = geom(n)
                zb = ZB[n % 2]
                w_ = WT[n % 2]
                for hh in range(2):
                    h = 2 * p + hh
                    S.op("act", lambda hh=hh, h=h: nc.scalar.activation(out=w_[:, hh, cs], in_=banks[zb + hh][:, cs], func=AF.Exp,
                                                                        scale=0.125, bias=ncum[:, i, h:h + 1]),
                         reads=[BK(zb + hh), "special"], writes=[WB[n % 2][hh]])
                if i >= 4 * cq:
                    S.op("dve", lambda: nc.vector.tensor_tensor(
                        out=w_[:, :, off:off + 128], in0=w_[:, :, off:off + 128],
                        in1=leb.unsqueeze(1).to_broadcast([128, 2, 128]), op=ALU.mult),
                        reads=WB[n % 2] + ["cstb"], writes=WB[n % 2])

            def P3(n):
                cq, i, off, cs = geom(n)
                first = (i == 0)
                w_ = WT[n % 2]

                def f():
                    last = None
                    for hh in range(2):
                        last = nc.tensor.matmul(banks[6][HP[hh], cs], lhsT=vpv[:, i, HP[hh]], rhs=w_[:, hh, cs], start=first,
                                                stop=True, skip_group_check=True)
                    for hh in range(2):
                        last = nc.tensor.matmul(banks[4][HP[hh], cs], lhsT=onesb[:, HP[hh]], rhs=w_[:, hh, cs], start=first,
                                                stop=True, skip_group_check=True)
                    return last
                S.op("pe", f, reads=WB[n % 2] + [("vp", par), "onesb"], writes=[BK(6), BK(4)])
                if i == 4 * cq + 3:
                    normalize(p, cq, None)

            P1(0)
            SE(0)
            for n in range(NT):
                if n + 1 < NT:
                    P1(n + 1)
                    SE(n + 1)
                P3(n)
                if pieces:
                    pieces.pop(0)()
            while pieces:
                pieces.pop(0)()

        def normalize(p, cq, sink_cols):
            if sink_cols is None:
                S.op("act", lambda: nc.scalar.activation(out=gbuf[:, 0, :], in_=banks[4][:, :], func=AF.Ln),
                     reads=[BK(4)], writes=[("gbuf", 0)])
            else:
                S.op("act", lambda: nc.scalar.activation(out=gbuf[:, 0, :], in_=banks[4][:, :], func=AF.Ln,
                                                         bias=small[:, 16 + p:17 + p], scale=1.0),
                     reads=[BK(4), "small"], writes=[("gbuf", 0)])
            S.op("act", lambda: nc.scalar.activation(out=gbuf[:, 0, :], in_=gbuf[:, 0, :], func=AF.Exp, scale=-1.0),
                 reads=[("gbuf", 0)], writes=[("gbuf", 0)])
            S.op("dve", lambda: nc.vector.tensor_tensor(out=big[:, p, CH(cq)], in0=banks[6][:, :], in1=gbuf[:, 0, :], op=ALU.mult),
                 reads=[BK(6), ("gbuf", 0)], writes=[("big", p)])

        def attn_swa(p, par=0, pieces=()):
            pieces = list(pieces)
            qTv = qT if par == 0 else vp2[:, :, :].rearrange("p a b -> p (a b)")
            WT = [wb, wb2]
            WB = [[("wb", 0), ("wb", 1)], [("wb2", 0), ("wb2", 1)]]
            ZB = [2, 0]

            def P1(i):
                nq = 256 if i < 15 else 128
                zb = ZB[i % 2]

                def f():
                    last = None
                    for hh in range(2):
                        last = nc.tensor.matmul(banks[zb + hh][:, 0:nq], lhsT=kT[HP[hh], i * 128:(i + 1) * 128],
                                                rhs=qTv[HP[hh], i * 128:i * 128 + nq], start=True, stop=True)
                    return last
                S.op("pe", f, reads=[("qT", par, n) for n in {i // 4, (i * 128 + nq - 1) // 512}] + [("kT", 0, i // 4)],
                     writes=[BK(zb), BK(zb + 1)])

            def SE(i):
                nq = 256 if i < 15 else 128
                zb = ZB[i % 2]
                w_ = WT[i % 2]
                S.op("act", lambda: nc.scalar.activation(out=w_[:, :, 0:nq], in_=psum[:, zb:zb + 2, 0:nq], func=AF.Exp, scale=0.125),
                     reads=[BK(zb), BK(zb + 1)], writes=WB[i % 2])
                S.op("dve", lambda: nc.vector.tensor_tensor(out=w_[:, :, 0:nq], in0=w_[:, :, 0:nq],
                                                            in1=bandb[:, 0:nq].unsqueeze(1).to_broadcast([128, 2, nq]), op=ALU.mult),
                     reads=WB[i % 2] + ["cstb"], writes=WB[i % 2])

            def pv(i, e0, c0, n, first):
                w_ = WT[i % 2]

                def f():
                    last = None
                    for hh in range(2):
                        last = nc.tensor.matmul(banks[6][HP[hh], c0:c0 + n], lhsT=vp[:, i, 0:64], rhs=w_[:, hh, e0:e0 + n],
                                                start=first, stop=True, skip_group_check=True)
                    for hh in range(2):
                        last = nc.tensor.matmul(banks[4][HP[hh], c0:c0 + n], lhsT=onesb[:, HP[hh]], rhs=w_[:, hh, e0:e0 + n],
                                                start=first, stop=True, skip_group_check=True)
                    return last
                S.op("pe", f, reads=WB[i % 2] + [("vp", 0), "onesb"], writes=[BK(6), BK(4)])

            def P3(i):
                cq = i // 4
                c0 = (i % 4) * 128
                if i % 4 != 3:
                    pv(i, 0, c0, 256, i == 0)
                else:
                    pv(i, 0, c0, 128, False)
                    normalize(p, cq, True)
                    if i < 15:
                        pv(i, 128, 0, 128, True)

            P1(0)
            SE(0)
            for i in range(16):
                if i + 1 < 16:
                    P1(i + 1)
                    SE(i + 1)
                P3(i)
                if pieces:
                    pieces.pop(0)()
            while pieces:
                pieces.pop(0)()

        def fox_prep(wt, wres):
            wv = v3(wt, 8, 512)
            fx = special[:, :]
            nlf = fx[:, 0:256]

            def f():
                last = None
                for i in range(16):
                    last = mm_group(banks[2][:, i * 16:(i + 1) * 16],
                                    [(hT[:, kc, i * 128:(i + 1) * 128], wv[:, kc, 0:16]) for kc in range(8)])
                    last = nc.tensor.matmul(banks[2][:, i * 16:(i + 1) * 16], lhsT=onesf[0:1, :], rhs=rowst[0:1, 16:32],
                                            start=False, stop=True, skip_group_check=True)
                return last
            S.op("pe", f, reads=[wres, "rowst1", "onesf"] + [("hT", c, n) for c in range(8) for n in range(4)],
                 writes=[BK(2)])
            S.op("act", lambda: nc.scalar.activation(out=nlf, in_=banks[2][:, 0:256], func=AF.Exp, scale=-1.0),
                 reads=[BK(2)], writes=["special"])
            S.op("act", lambda: nc.scalar.activation(out=nlf, in_=nlf, func=AF.Ln, bias=1.0, scale=1.0),
                 reads=["special"], writes=["special"])

            def f2():
                last = None
                for j in range(16):
                    for i2 in range(j + 1):
                        last = nc.tensor.matmul(banks[3][:, j * 16:(j + 1) * 16], lhsT=onesf[:, :],
                                                rhs=nlf[:, i2 * 16:(i2 + 1) * 16], start=(i2 == 0), stop=(i2 == j))
                for i in range(16):
                    last = nc.tensor.matmul(banks[4][:, i * 16:(i + 1) * 16], lhsT=lef, rhs=nlf[:, i * 16:(i + 1) * 16],
                                            start=True, stop=True)
                return last
            S.op("pe", f2, reads=["special", "onesf", "cstf"], writes=[BK(3), BK(4)])
            S.op("dve", lambda: nc.vector.tensor_copy(out=fx[:, 256:512], in_=banks[3][:, 0:256]), reads=[BK(3)],
                 writes=["special"])
            S.op("dve", lambda: nc.vector.tensor_copy(out=fx[:, 512:528], in_=banks[4][:, 0:16]), reads=[BK(4)],
                 writes=["special"])
            S.op("dve", lambda: nc.vector.tensor_tensor(out=fx[:, 528:768], in0=banks[4][:, 16:256], in1=fx[:, 256:496],
                                                        op=ALU.add), reads=[BK(4), "special"], writes=["special"])
            S.op("dve", lambda: nc.vector.tensor_scalar(out=fx[:, 1792:1808], in0=fx[:, 256:272], scalar1=0.5, scalar2=None,
                                                        op0=ALU.mult), reads=["special"], writes=["special"])
            S.op("dve", lambda: nc.vector.tensor_tensor(out=fx[:, 1808:2048], in0=fx[:, 256:496], in1=fx[:, 272:512],
                                                        op=ALU.add), reads=["special"], writes=["special"])
            S.op("dve", lambda: nc.vector.tensor_scalar(out=fx[:, 1808:2048], in0=fx[:, 1808:2048], scalar1=0.5, scalar2=None,
                                                        op0=ALU.mult), reads=["special"], writes=["special"])

        def dump_bf(src):
            for c in range(8):
                for n in range(4):
                    S.op("dve", lambda c=c, n=n: nc.vector.tensor_copy(out=xT[:, c, CH(n)], in_=src[:, c, CH(n)]),
                         reads=[("hT", c, n), ("big", c), ("qT", 0, n), ("kT", 0, n), ("vp", 0)], writes=[("xT", c, n)])
            raise _Stop()

        def layer(l, prenormed, next_norm):
            k = KINDS[l]
            if not prenormed:
                norm(3 * l + 0)
            if dbg == "h":
                dump_bf(hT)
            if k == 1:
                rp = special[:, 0:4096].rearrange("p (a t) -> p a t", a=2)
                S.dma("sp", S.new_dma_sem("d_rope"), rp, rope_in, writes=["special"])
                S.op("pe", lambda: nc.tensor.matmul(banks[0][:, 0:16], lhsT=onesf[0:1, :], rhs=rowst[0:1, 0:16],
                                                    start=True, stop=True), reads=["onesf", "rowst0"], writes=[BK(0)])
                S.op("act", lambda: nc.scalar.activation(out=small[:, 0:16], in_=banks[0][:, 0:16], func=AF.Exp),
                     reads=[BK(0)], writes=["small"])
                for hh in range(2):
                    S.op("dve", lambda hh=hh: nc.vector.tensor_copy(
                        out=small[HP[hh], 16:24], in_=small[HP[hh], 0:16].rearrange("p (a b) -> p a b", b=2)[:, :, hh]),
                        reads=["small"], writes=["small"])
            if k == 2:
                wt, wres = ws_next()
                fox_prep(wt, wres)
            def make_pieces(pp, par, overlapped):
                wt, wres = ws_next()
                qd, kd, vd = qkv_views(k, par)
                bank = 7 if overlapped else None
                eng = "dve" if overlapped else "act"
                return (proj_fm(wt, wres, 0, qd, "qT", copy_eng=eng, dpar=par, bank=bank)
                        + proj_fm(wt, wres, 128, kd, "kT", copy_eng="dve", dpar=par, bank=bank)
                        + proj_v(wt, wres, 256, 128, vdst=vd, dpar=par, bank=bank, copy_eng=eng))

            if k in (0, 2):
                for pc in make_pieces(0, 0, False):
                    pc()
            for p in range(8):
                if k in (0, 2):
                    nxt = make_pieces(p + 1, (p + 1) % 2, True) if p + 1 < 8 else []
                    (attn_sb if k == 0 else attn_fox)(p, p % 2, nxt)
                    continue
                if True:
                    if p % 4 == 0:
                        wt, wres = ws_next()
                        swap_cols(wt, wres)
                        for pc in proj_rope(wt, wres, 0, kT, "kT") + proj_v(wt, wres, 256, 64):
                            pc()
                    if p == 0:
                        wt, wres = ws_next()
                        swap_cols(wt, wres)
                        for pc in proj_rope(wt, wres, 0, qT, "qT"):
                            pc()
                    nxt = []
                    if p + 1 < 8:
                        wt, wres = ws_next()
                        swap_cols(wt, wres)
                        npar = (p + 1) % 2
                        qd = qT if npar == 0 else vp2[:, :, :].rearrange("p a b -> p (a b)")
                        nxt = proj_rope(wt, wres, 0, qd, "qT", dpar=npar, bks=(5, 7), tmp=ebuf, tmpn="ebuf")
                    attn_swa(p, p % 2, nxt)
            if dbg == "a":
                dump_bf(big)
            for hf in range(2):
                wt, wres = ws_next()
                wv = v3(wt, 8, 512)
                order = [(ftl, n) for ftl in range(4) for n in range(4)] if hf == 0 else \
                        [(ftl, n) for n in range(4) for ftl in range(4)]
                for (ftl, n) in order:
                    ft = hf * 4 + ftl
                    b = next_bank()
                    S.op("pe", lambda b=b, n=n, ftl=ftl, wv=wv: mm_group(
                        banks[b][:, :], [(wv[:, kc, ftl * 128:(ftl + 1) * 128], big[:, kc, CH(n)]) for kc in range(8)]),
                        reads=[wres] + [("big", kc) for kc in range(8)], writes=[BK(b)])
                    resid_add(ft, n, b)
                    if hf == 1 and ftl == 3 and n >= 1 and dbg is None:
                        norm_chunk(3 * l + 1, n - 1)
            if dbg is None:
                norm_chunk(3 * l + 1, 3)
            if dbg == "xa":
                raise _Stop()
            pT = qk
            PST = [wb[:, :, :].rearrange("p a b -> p (a b)").bitcast(F32), wb2[:, :, :].rearrange("p a b -> p (a b)").bitcast(F32)]
            PSR = [[("wb", 0), ("wb", 1)], [("wb2", 0), ("wb2", 1)]]
            def p_dma(i):
                S.dma("sp", d_st[i % 2], PST[i % 2][:, 0:256], p_in[l, i * 128:(i + 1) * 128, :], writes=PSR[i % 2])

            def p_xpose(i):
                stg = PST[i % 2]

                def f():
                    last = None
                    for kc in range(2):
                        last = nc.tensor.transpose(banks[7][:, kc * 128:(kc + 1) * 128], stg[:, kc * 128:(kc + 1) * 128], identf)
                    return last
                S.op("pe", f, reads=PSR[i % 2] + ["cstf"], writes=[BK(7)])
                src = banks[7][:, 0:256].rearrange("p (a t) -> p a t", a=2)
                S.op("act", lambda: nc.scalar.copy(out=pT[:, :, i * 128:(i + 1) * 128], in_=src),
                     reads=[BK(7)], writes=[("qT", 0, i // 4), ("kT", 0, i // 4)])
            p_dma(0)
            p_dma(1)
            if dbg is not None:
                norm(3 * l + 1)
            for g in range(8):
                wt, wres = ws_next()
                wv = v3(wt, 8, 512)
                ub = g % 2
                uT = big[:, ub * 4:(ub + 1) * 4, :]
                for ftl in range(4):
                    for n in range(4):
                        b = next_bank()
                        S.op("pe", lambda b=b, n=n, ftl=ftl, wv=wv: mm_group(
                            banks[b][:, :], [(wv[:, kc, ftl * 128:(ftl + 1) * 128], hT[:, kc, CH(n)]) for kc in range(8)]),
                            reads=[wres] + HT(n), writes=[BK(b)])
                        tb = ebuf if (ftl * 4 + n) % 2 == 0 else gbuf
                        tr = "ebuf" if (ftl * 4 + n) % 2 == 0 else "gbuf"
                        S.op("act", lambda b=b, tb=tb: nc.scalar.activation(out=tb[:, 0, :], in_=banks[b][:, :], func=AF.Relu),
                             reads=[BK(b)], writes=[(tr, 0)])
                        S.op("dve", lambda tb=tb, n=n, ftl=ftl, uT=uT: nc.vector.tensor_tensor(
                            out=uT[:, ftl, CH(n)], in0=tb[:, 0, :], in1=tb[:, 0, :], op=ALU.mult),
                            reads=[(tr, 0)], writes=[("big", ub * 4 + ftl)])
                for i in (2 * g, 2 * g + 1):
                    p_xpose(i)
                    if i + 2 < 16:
                        p_dma(i + 2)
                wt, wres = ws_next()
                wv = v3(wt, 4, 1024)
                order = [(ft, n) for ft in range(8) for n in range(4)] if g < 7 else \
                        [(ft, n) for n in range(4) for ft in range(8)]
                for (ft, n) in order:
                    b = next_bank()
                    S.op("pe", lambda b=b, n=n, ft=ft, wv=wv, uT=uT: mm_group(
                        banks[b][:, :], [(wv[:, kc, ft * 128:(ft + 1) * 128], uT[:, kc, CH(n)]) for kc in range(4)]),
                        reads=[wres] + [("big", ub * 4 + kc) for kc in range(4)], writes=[BK(b)])
                    resid_add(ft, n, b)
                    if g == 7 and ft == 7 and n >= 1 and dbg is None:
                        norm_chunk(3 * l + 2, n - 1)
            if dbg is None:
                norm_chunk(3 * l + 2, 3)
            if dbg == "xm":
                raise _Stop()
            pT = qk
            if dbg is not None:
                norm(3 * l + 2)
            wgs = [ws_next(), ws_next(hold=True)]
            wpt, wpres = ws_next(hold=True)
            wpv = v3(wpt, 2, 1024)
            gtmp = ebuf[:, 1, :]
            for n in range(4):
                for ft in range(8):
                    wt, wres = wgs[ft // 4]
                    wv = v3(wt, 8, 512)
                    ftl = ft % 4
                    bg = next_bank()
                    S.op("pe", lambda bg=bg, n=n, ftl=ftl, wv=wv: mm_group(
                        banks[bg][:, :], [(wv[:, kc, ftl * 128:(ftl + 1) * 128], hT[:, kc, CH(n)]) for kc in range(8)]),
                        reads=[wres] + HT(n), writes=[BK(bg)])
                    bp = next_bank()
                    S.op("pe", lambda bp=bp, n=n, ft=ft: mm_group(
                        banks[bp][:, :], [(wpv[:, kc, ft * 128:(ft + 1) * 128], pT[:, kc, CH(n)]) for kc in range(2)]),
                        reads=[wpres, ("qT", 0, n), ("kT", 0, n)], writes=[BK(bp)])
                    S.op("act", lambda bg=bg: nc.scalar.activation(out=gtmp, in_=banks[bg][:, :], func=AF.Sigmoid),
                         reads=[BK(bg)], writes=[("ebuf", 1)])
                    S.op("dve", lambda bp=bp: nc.vector.tensor_tensor(out=gtmp, in0=banks[bp][:, :], in1=gtmp,
                                                                      op=ALU.mult), reads=[BK(bp), ("ebuf", 1)], writes=[("ebuf", 1)])
                    S.op("dve", lambda n=n, ft=ft: nc.vector.tensor_tensor(out=xT[:, ft, CH(n)], in0=gtmp,
                                                                           in1=xT[:, ft, CH(n)], op=ALU.add),
                         reads=[("ebuf", 1), ("xT", ft, n)], writes=[("xT", ft, n)])
                if next_norm is not None and n >= 1 and dbg is None:
                    norm_chunk(next_norm[0], n - 1, next_norm[1])
            if next_norm is not None and dbg is None:
                norm_chunk(next_norm[0], 3, next_norm[1])

        def alias_sync(names_from, names_to):
            for a in names_from:
                for b in names_to:
                    if a in S.wr:
                        cur = S.wr.get(b)
                        S.rd.setdefault(b, {})
                        k, v = S.wr[a]
                        S.rd[b][k] = max(S.rd[b].get(k, 0), v)
                    for k, v in S.rd.get(a, {}).items():
                        S.rd.setdefault(b, {})
                        S.rd[b][k] = max(S.rd[b].get(k, 0), v)

        EG = [("ebuf", 0), ("ebuf", 1), ("gbuf", 0), ("gbuf", 1)]
        STG = [("stage", 0), ("stage", 1), ("stage", 0, 0), ("stage", 0, 1), ("stage", 1, 0), ("stage", 1, 1)]
        load_x()
        try:
            for idx, l in enumerate(layers):
                if dbg is not None:
                    nxt = None
                elif idx + 1 < len(layers):
                    nxt = (3 * layers[idx + 1], False)
                else:
                    nxt = (12, True) if final else None
                layer(l, prenormed=(idx > 0 and dbg is None), next_norm=nxt)
        except _Stop:
            pass
        if final and (not layers or dbg is not None):
            norm(12, inplace=True)
        store_x()
        S.wait_all("sp", d_out)
    return nc


_CONSTS = None


def _run(layers, final, x, inputs):
    global _CONSTS
    if _CONSTS is None:
        _CONSTS = host_consts()
    cst, rope = _CONSTS
    nc = build(layers, final)
    gl = []
    for l in range(DEPTH):
        gl += [inputs[f"attn_norm_{l}"], inputs[f"mlp_norm_{l}"], inputs[f"ple_norm_{l}"]]
    gl.append(inputs["final_norm"])
    gains = np.ascontiguousarray(np.stack(gl).astype(np.float32).reshape(13 * 8, 128))
    shared = dict(cst=cst, rope=rope, gains=gains,
                  sinks=np.ascontiguousarray(inputs["sinks_1"].reshape(1, 16)),
                  bfor=np.ascontiguousarray(inputs["b_forget_2"].reshape(1, 16)))
    for l in layers:
        shared[f"w_in_{l}"] = inputs[f"w_in_{l}"]
        shared[f"w_out_{l}"] = inputs[f"w_out_{l}"]
        shared[f"w_up_{l}"] = inputs[f"w_up_{l}"]
        shared[f"w_down_{l}"] = inputs[f"w_down_{l}"]
        shared[f"w_ple_gate_{l}"] = inputs[f"w_ple_gate_{l}"]
        shared[f"w_ple_proj_{l}"] = inputs[f"w_ple_proj_{l}"]
    p = inputs["p"]
    in_maps = []
    for c in range(NCORES):
        m = dict(shared)
        m["x"] = np.ascontiguousarray(x[c])
        m["p"] = np.ascontiguousarray(p[:, c])
        in_maps.append(m)
    res = run_bass_kernel_spmd(nc, in_maps, core_ids=list(range(NCORES)))
    return np.stack([np.asarray(r["y"]) for r in res.results], axis=0).astype(np.float32)


def kernel(**inputs):
    inputs = {k: np.asarray(v) for k, v in inputs.items()}
    x = np.ascontiguousarray(inputs["x"], dtype=np.float32)
    if FUSED:
        return _run(list(range(DEPTH)), True, x, inputs)
    for l in range(DEPTH):
        x = _run([l], l == DEPTH - 1, x, inputs)
    return x
```

```python
import numpy as np
from contextlib import ExitStack
import concourse.bass as bass
import concourse.mybir as mybir
from concourse.bass_utils import run_bass_kernel_spmd

F32 = mybir.dt.float32
BF16 = mybir.dt.bfloat16
AF = mybir.ActivationFunctionType
ALU = mybir.AluOpType

S_LEN = 2048
D = 1024
DEPTH = 4
NCORES = 8
KINDS = (0, 1, 2, 0)
EPS = 1e-6
FUSED = True
DBG_I = 3


class Sched:
    def __init__(self, nc, stack):
        self.nc = nc
        self.stack = stack
        self.engs = {"pe": nc.tensor, "act": nc.scalar, "dve": nc.vector, "pool": nc.gpsimd, "sp": nc.sync}
        self.sem = {k: stack.enter_context(nc.semaphore("s_" + k)) for k in self.engs}
        self.cnt = {k: 0 for k in self.engs}
        self.known = {k: {} for k in self.engs}
        self.wr = {}
        self.rd = {}
        self.semobj = dict(self.sem)
        self.dmacnt = {}

    def new_dma_sem(self, name):
        s = self.stack.enter_context(self.nc.semaphore(name))
        self.semobj[name] = s
        self.dmacnt[name] = 0
        return name

    def _need(self, eng, reads, writes):
        need = {}

        def add(c):
            if c is None:
                return
            k, v = c
            if need.get(k, 0) < v:
                need[k] = v
        for r in reads:
            add(self.wr.get(r))
        for r in writes:
            add(self.wr.get(r))
            for k, v in self.rd.get(r, {}).items():
                add((k, v))
        for k, v in need.items():
            if k == eng and eng == "pe":
                continue
            if self.known[eng].get(k, 0) >= v:
                continue
            self.engs[eng].wait_ge(self.semobj[k], v)
            self.known[eng][k] = v

    def _mark(self, clock, reads, writes):
        k, v = clock
        for r in reads:
            self.rd.setdefault(r, {})[k] = v
        for r in writes:
            self.wr[r] = clock
            self.rd[r] = {}

    def op(self, eng, fn, reads=(), writes=()):
        self._need(eng, reads, writes)
        inst = fn()
        self.cnt[eng] += 1
        inst.then_inc(self.sem[eng], 1)
        self._mark((eng, self.cnt[eng]), reads, writes)
        return inst

    def dma(self, queue, semname, out, in_, reads=(), writes=(), **kw):
        self._need(queue, reads, writes)
        inst = self.engs[queue].dma_start(out=out, in_=in_, **kw)
        self.dmacnt[semname] += 16
        inst.then_inc(self.semobj[semname], 16)
        self._mark((semname, self.dmacnt[semname]), reads, writes)
        return inst

    def wait_all(self, eng, semnames):
        for s in semnames:
            if self.dmacnt[s] > 0:
                self.engs[eng].wait_ge(self.semobj[s], self.dmacnt[s])


def host_consts():
    a = np.arange(128)
    ident = np.eye(128, dtype=np.float32)
    ge = (a[:, None] >= a[None, :]).astype(np.float32)
    lt = (a[:, None] < a[None, :]).astype(np.float32)
    le = (a[:, None] <= a[None, :]).astype(np.float32)
    f = np.arange(256)
    band = ((f[None, :] - a[:, None] >= 0) & (f[None, :] - a[:, None] < 128)).astype(np.float32)
    cst = np.concatenate([ident, ge, lt, le, band], axis=1)
    half = 8
    inv_freq = (np.float32(500000.0) ** (-np.arange(half, dtype=np.float32) / np.float32(half))).astype(np.float32)
    ang = np.arange(S_LEN, dtype=np.float32)[:, None] * inv_freq[None, :]
    cos = np.cos(ang).astype(np.float32).T
    sin = np.sin(ang).astype(np.float32).T
    rope = np.zeros((128, 2, S_LEN), np.float32)
    rope[:, 0, :] = 1.0
    for hh in range(2):
        b = 64 * hh
        rope[b:b + 8, 0] = cos
        rope[b + 8:b + 16, 0] = cos
        rope[b:b + 8, 1] = -sin
        rope[b + 8:b + 16, 1] = sin
    return cst, rope


class _Stop(Exception):
    pass


def build(layers, final, dbg=None):
    nc = bass.Bass("TRN2", target_bir_lowering=False)
    dt_in = lambda n, s: nc.dram_tensor(n, list(s), F32, kind="ExternalInput").ap()
    x_in = dt_in("x", (S_LEN, D))
    p_in = dt_in("p", (DEPTH, S_LEN, 256))
    cst_in = dt_in("cst", (128, 768))
    rope_in = dt_in("rope", (128, 2, S_LEN))
    gains_in = dt_in("gains", (104, 128))
    sinks_in = dt_in("sinks", (1, 16))
    bfor_in = dt_in("bfor", (1, 16))
    W = {}
    for l in layers:
        k = KINDS[l]
        W[l] = dict(
            w_in=dt_in(f"w_in_{l}", (D, (3072, 1280, 3088)[k])),
            w_out=dt_in(f"w_out_{l}", (D, D)),
            w_up=dt_in(f"w_up_{l}", (D, 4096)),
            w_down=dt_in(f"w_down_{l}", (4096, D)),
            w_gate=dt_in(f"w_ple_gate_{l}", (D, D)),
            w_proj=dt_in(f"w_ple_proj_{l}", (256, D)),
        )
    y_out = nc.dram_tensor("y", [S_LEN, D], F32, kind="ExternalOutput").ap()

    with ExitStack() as st:
        S = Sched(nc, st)
        sb = lambda n, s, d: st.enter_context(nc.sbuf_tensor("sb_" + n, list(s), d))
        xT = sb("xT", (128, 8, S_LEN), F32)
        hT = sb("hT", (128, 8, S_LEN), BF16)
        big = sb("big", (128, 8, S_LEN), BF16)
        vp = sb("vp", (128, 16, 128), BF16)
        qk = sb("qk", (128, 2, S_LEN), BF16)
        wsl = [sb(f"ws{i}", (128, 4096), BF16) for i in range(3)]
        ebuf = sb("ebuf", (128, 2, 512), F32)
        gbuf = sb("gbuf", (128, 2, 512), F32)
        spb = sb("spb", (128, 2, 512), BF16)
        wb = sb("wb", (128, 2, 512), BF16)
        special = sb("special", (128, 4608), F32)
        cstf = sb("cstf", (128, 768), F32)
        cstb = sb("cstb", (128, 768), BF16)
        gains = sb("gains", (128, 104), F32)
        gstage = sb("gstage", (104, 128), F32)
        onesf = sb("onesf", (128, 128), F32)
        onesb = sb("onesb", (128, 128), BF16)
        small = sb("small", (128, 64), F32)
        rowst = sb("rowst", (1, 32), F32)
        psum = st.enter_context(nc.psum_tensor("psum", [128, 8, 512], F32))
        banks = [psum[:, i, :] for i in range(8)]
        wb2 = sb("wb2", (128, 2, 512), BF16)
        rstd = gbuf[:, 1, :]
        vp2 = sb("vp2", (128, 16, 128), BF16)
        ebuf3 = special[:, 1536:2560].rearrange("p (a b) -> p a b", a=2)
        ebuf2 = special[:, 0:1024].rearrange("p (a b) -> p a b", a=2)
        spb2 = special[:, 1024:1536].bitcast(BF16).rearrange("p (a b) -> p a b", a=2)
        BK = lambda i: ("bank", i)

        identf = cstf[:, 0:128]
        lef = cstf[:, 384:512]
        geb, ltb, leb, bandb = cstb[:, 128:256], cstb[:, 256:384], cstb[:, 384:512], cstb[:, 512:768]
        stage = [ebuf, gbuf]
        qT, kT = qk[:, 0, :], qk[:, 1, :]
        QKV = {"cur": 0}

        def qkv_views(kind, par):
            if par == 0:
                return qk[:, 0, :], qk[:, 1, :], vp
            if kind == 0:
                a, b = 2560, 3584
            else:
                a, b = 768, 3072
            return (special[:, a:a + 1024].bitcast(BF16), special[:, b:b + 1024].bitcast(BF16), vp2)
        CH = lambda n: slice(n * 512, (n + 1) * 512)

        d_st = [S.new_dma_sem("d_st0"), S.new_dma_sem("d_st1")]
        d_out = [S.new_dma_sem("d_out0"), S.new_dma_sem("d_out1")]
        d_w = [S.new_dma_sem(f"d_w{i}") for i in range(3)]

        S.dma("sp", S.new_dma_sem("d_c0"), cstf[:, :], cst_in, writes=["cstf"])
        S.dma("sp", S.new_dma_sem("d_c1"), gstage[:, :], gains_in, writes=["gstage"])
        S.dma("sp", S.new_dma_sem("d_c2"), rowst[:, 0:16], sinks_in, writes=["rowst0"])
        S.dma("sp", S.new_dma_sem("d_c3"), rowst[:, 16:32], bfor_in, writes=["rowst1"])
        S.op("dve", lambda: nc.vector.memset(onesf[:, :], 1.0), writes=["onesf"])
        S.op("dve", lambda: nc.vector.memset(onesb[:, :], 1.0), writes=["onesb"])
        S.op("dve", lambda: nc.vector.tensor_copy(out=cstb[:, :], in_=cstf[:, :]), reads=["cstf"], writes=["cstb"])
        S.op("pe", lambda: nc.tensor.transpose(banks[7][:, 0:104], gstage[:, :], identf[0:104, 0:104]),
             reads=["gstage", "cstf"], writes=[BK(7)])
        S.op("act", lambda: nc.scalar.copy(out=gains[:, :], in_=banks[7][:, 0:104]), reads=[BK(7)], writes=["gains"])

        blocks = []

        class WS:
            issued = 0
            cur = -1

        def ws_issue(upto):
            while WS.issued <= min(upto, len(blocks) - 1):
                b = WS.issued
                slot = b % 3
                for (dstfn, src) in blocks[b]:
                    S.dma("pool", d_w[slot], dstfn(wsl[slot]), src, writes=[("ws", slot)])
                WS.issued += 1

        def ws_next(hold=False):
            WS.cur += 1
            ws_issue(WS.cur if hold else WS.cur + 2)
            return wsl[WS.cur % 3], ("ws", WS.cur % 3)

        def v3(t, a, b):
            return t[:, 0:a * b].rearrange("p (a b) -> p a b", a=a)

        def wview(wd, c0, ncols):
            return wd.rearrange("(kc p) n -> p kc n", p=128)[:, :, c0:c0 + ncols]

        for l in layers:
            k = KINDS[l]
            wd = W[l]
            if k == 2:
                blocks.append([(lambda t: v3(t, 8, 512)[:, :, 0:16], wview(wd["w_in"], 3072, 16))])
            for p in range(8):
                if k in (0, 2):
                    blocks.append([
                        ((lambda t, o=o: v3(t, 8, 512)[:, :, o * 128:(o + 1) * 128]),
                         wview(wd["w_in"], o * 1024 + p * 128, 128)) for o in range(3)])
                else:
                    def kv_block(g):
                        kc0 = 1024 + 64 * g
                        ent = [(lambda t, o=o: v3(t, 8, 512)[:, :, o:o + 64], wview(wd["w_in"], kc0, 64)) for o in (0, 64)]
                        ent.append((lambda t: v3(t, 8, 512)[:, :, 256:320], wview(wd["w_in"], 1152 + 64 * g, 64)))
                        blocks.append(ent)
                    if p == 0:
                        kv_block(0)
                    blocks.append([(lambda t: v3(t, 8, 512)[:, :, 0:128], wview(wd["w_in"], p * 128, 128))])
                    if p == 4:
                        kv_block(1)
            for hf in range(2):
                blocks.append([(lambda t: v3(t, 8, 512), wview(wd["w_out"], hf * 512, 512))])
            for g in range(8):
                blocks.append([(lambda t: v3(t, 8, 512), wview(wd["w_up"], g * 512, 512))])
                blocks.append([(lambda t: v3(t, 4, 1024),
                               wd["w_down"][g * 512:(g + 1) * 512, :].rearrange("(kc p) n -> p kc n", p=128))])
            for hf in range(2):
                blocks.append([(lambda t: v3(t, 8, 512), wview(wd["w_gate"], hf * 512, 512))])
            blocks.append([(lambda t: v3(t, 2, 1024), wd["w_proj"].rearrange("(kc p) n -> p kc n", p=128))])

        def mm_group(bank_ap, pairs, first=True, skip=False):
            last = None
            n = len(pairs)
            for i, (l_, r_) in enumerate(pairs):
                last = nc.tensor.matmul(bank_ap, lhsT=l_, rhs=r_, start=(first and i == 0), stop=(i == n - 1),
                                        skip_group_check=skip)
            return last

        HT = lambda n: [("hT", c, n) for c in range(8)]
        bank_rr = [0]

        def next_bank(lo=0, hi=4):
            b = lo + bank_rr[0] % (hi - lo)
            bank_rr[0] += 1
            return b

        SR = lambda j: [("ebuf" if j == 0 else "gbuf", 0), ("ebuf" if j == 0 else "gbuf", 1)]

        def load_x():
            for i in range(16):
                stg = stage[i % 2]
                sv = stg[:, :, :].rearrange("p a b -> p (a b)")
                S.dma("sp", d_st[i % 2], sv, x_in[i * 128:(i + 1) * 128, :], writes=SR(i % 2))
                for half in range(2):
                    b = next_bank()

                    def f(b=b, half=half, sv=sv):
                        last = None
                        for cc in range(4):
                            c = half * 4 + cc
                            last = nc.tensor.transpose(banks[b][:, cc * 128:(cc + 1) * 128],
                                                       sv[:, c * 128:(c + 1) * 128], identf)
                        return last
                    S.op("pe", f, reads=SR(i % 2) + ["cstf"], writes=[BK(b)])
                    dst = xT[:, half * 4:half * 4 + 4, i * 128:(i + 1) * 128]
                    src = banks[b][:, :].rearrange("p (c t) -> p c t", c=4)
                    wr = [("xT", c, i // 4) for c in range(half * 4, half * 4 + 4)]
                    if half == 0:
                        S.op("act", lambda dst=dst, src=src: nc.scalar.copy(out=dst, in_=src), reads=[BK(b)], writes=wr)
                    else:
                        S.op("dve", lambda dst=dst, src=src: nc.vector.tensor_copy(out=dst, in_=src), reads=[BK(b)], writes=wr)

        def store_x():
            for i in range(16):
                stg = stage[i % 2]
                sv = stg[:, :, :].rearrange("p a b -> p (a b)")
                for half in range(2):
                    b = next_bank()

                    def f(b=b, half=half):
                        last = None
                        for cc in range(4):
                            c = half * 4 + cc
                            last = nc.tensor.transpose(banks[b][:, cc * 128:(cc + 1) * 128],
                                                       xT[:, c, i * 128:(i + 1) * 128], identf)
                        return last
                    S.op("pe", f, reads=[("xT", c, i // 4) for c in range(half * 4, half * 4 + 4)] + ["cstf"],
                         writes=[BK(b)])
                    dst = sv[:, half * 512:(half + 1) * 512]
                    if half == 0:
                        S.op("act", lambda dst=dst, b=b: nc.scalar.copy(out=dst, in_=banks[b][:, :]),
                             reads=[BK(b)], writes=[SR(i % 2)[0]])
                    else:
                        S.op("dve", lambda dst=dst, b=b: nc.vector.tensor_copy(out=dst, in_=banks[b][:, :]),
                             reads=[BK(b)], writes=[SR(i % 2)[1]])
                S.dma("sp", d_out[i % 2], y_out[i * 128:(i + 1) * 128, :], sv,
                      reads=SR(i % 2))

        def stage_guard():
            pass

        def norm_chunk(gidx, n, inplace=False):
            for c in range(8):
                sq = spb[:, c % 2, :]
                S.op("act", lambda sq=sq, c=c: nc.scalar.activation(out=sq, in_=xT[:, c, CH(n)], func=AF.Square),
                     reads=[("xT", c, n)], writes=[("spb", c % 2)])
                S.op("pe", lambda sq=sq, c=c: nc.tensor.matmul(banks[7][:, :], lhsT=onesb[:, :], rhs=sq,
                                                               start=(c == 0), stop=(c == 7)),
                     reads=[("spb", c % 2), "onesb"], writes=[BK(7)])
            S.op("act", lambda: nc.scalar.activation(out=rstd, in_=banks[7][:, :], func=AF.Ln,
                                                     scale=1.0 / D, bias=epsb[:, 0:1]),
                 reads=[BK(7), "epsb"], writes=[("gbuf", 1)])
            S.op("act", lambda: nc.scalar.activation(out=rstd, in_=rstd, func=AF.Exp, scale=-0.5),
                 reads=[("gbuf", 1)], writes=[("gbuf", 1)])
            for c in range(8):
                dst = xT[:, c, CH(n)] if inplace else hT[:, c, CH(n)]
                wr = [("xT", c, n)] if inplace else [("hT", c, n)]
                S.op("dve", lambda dst=dst, c=c: nc.vector.scalar_tensor_tensor(
                    out=dst, in0=xT[:, c, CH(n)], scalar=gains[:, gidx * 8 + c:gidx * 8 + c + 1], in1=rstd,
                    op0=ALU.mult, op1=ALU.mult), reads=[("xT", c, n), "gains", ("gbuf", 1)], writes=wr)


        def norm(gidx, inplace=False):
            for n in range(4):
                norm_chunk(gidx, n, inplace)

        epsb = sb("epsb", (128, 1), F32)
        S.op("dve", lambda: nc.vector.memset(epsb[:, :], EPS), writes=["epsb"])

        def resid_add(ft, n, b):
            S.op("dve", lambda: nc.vector.tensor_tensor(out=xT[:, ft, CH(n)], in0=banks[b][:, :], in1=xT[:, ft, CH(n)],
                                                        op=ALU.add), reads=[BK(b), ("xT", ft, n)], writes=[("xT", ft, n)])

        def proj_fm(wt, wres, col0, dst, dres, copy_eng="act", dpar=0, bank=None):
            wv = v3(wt, 8, 512)
            pcs = []
            for n in range(4):
                st_ = {}

                def half(h, n=n, st_=st_):
                    if h == 0:
                        st_["b"] = next_bank(0, 2) if bank is None else bank
                    b = st_["b"]
                    S.op("pe", lambda: mm_group(banks[b][:, :], [(wv[:, kc, col0:col0 + 128], hT[:, kc, CH(n)])
                                                                 for kc in range(4 * h, 4 * h + 4)], first=(h == 0), skip=True),
                         reads=[wres] + HT(n), writes=[BK(b)])

                def evac(n=n, st_=st_):
                    b = st_["b"]
                    if copy_eng == "act":
                        S.op("act", lambda: nc.scalar.copy(out=dst[:, CH(n)], in_=banks[b][:, :]),
                             reads=[BK(b)], writes=[(dres, dpar, n)])
                    else:
                        S.op("dve", lambda: nc.vector.tensor_copy(out=dst[:, CH(n)], in_=banks[b][:, :]),
                             reads=[BK(b)], writes=[(dres, dpar, n)])
                pcs += [lambda half=half: half(0), lambda half=half: half(1), evac]
            return pcs

        def proj_v(wt, wres, col0, ncol, vdst=None, dpar=0, bank=None, copy_eng="act"):
            wv = v3(wt, 8, 512)
            vdst = vp if vdst is None else vdst
            pcs = []
            for i4 in range(4):
                st_ = {}

                def half(h, i4=i4, st_=st_):
                    if h == 0:
                        st_["b"] = next_bank(0, 2) if bank is None else bank
                    b = st_["b"]

                    def f():
                        last = None
                        for ii in range(2 * h, 2 * h + 2):
                            i = i4 * 4 + ii
                            last = mm_group(banks[b][:, ii * 128:ii * 128 + ncol],
                                            [(hT[:, kc, i * 128:(i + 1) * 128], wv[:, kc, col0:col0 + ncol]) for kc in range(8)])
                        return last
                    S.op("pe", f, reads=[wres] + HT(i4), writes=[BK(b)])

                def evac(i4=i4, st_=st_):
                    b = st_["b"]
                    src = banks[b][:, :].rearrange("p (a c) -> p a c", a=4)[:, :, 0:ncol]
                    if copy_eng == "act":
                        S.op("act", lambda: nc.scalar.copy(out=vdst[:, i4 * 4:i4 * 4 + 4, 0:ncol], in_=src),
                             reads=[BK(b)], writes=[("vp", dpar)])
                    else:
                        S.op("dve", lambda: nc.vector.tensor_copy(out=vdst[:, i4 * 4:i4 * 4 + 4, 0:ncol], in_=src),
                             reads=[BK(b)], writes=[("vp", dpar)])
                pcs += [lambda half=half: half(0), lambda half=half: half(1), evac]
            return pcs

        def swap_cols(wt, wres):
            wv = v3(wt, 8, 512)
            src = wv[:, :, 0:128].rearrange("p k (h d) -> p k h d", h=2)
            dstv = wv[:, :, 128:256].rearrange("p k (h d) -> p k h d", h=2)
            for (a, b, n) in ((0, 8, 8), (8, 0, 8), (16, 16, 48)):
                S.op("act", lambda a=a, b=b, n=n: nc.scalar.copy(out=dstv[:, :, :, a:a + n], in_=src[:, :, :, b:b + n]),
                     reads=[wres], writes=[wres])

        def proj_rope(wt, wres, col0, dst, dres, dpar=0, bks=(0, 1), tmp=None, tmpn="gbuf"):
            wv = v3(wt, 8, 512)
            tmp = gbuf if tmp is None else tmp
            rp = special[:, 0:4096].rearrange("p (a t) -> p a t", a=2)
            pcs = []
            for n in range(4):
                def pa(n=n):
                    S.op("pe", lambda: mm_group(banks[bks[0]][:, :], [(wv[:, kc, col0:col0 + 128], hT[:, kc, CH(n)])
                                                                      for kc in range(8)]),
                         reads=[wres] + HT(n), writes=[BK(bks[0])])

                def pb(n=n):
                    S.op("pe", lambda: mm_group(banks[bks[1]][:, :], [(wv[:, kc, col0 + 128:col0 + 256], hT[:, kc, CH(n)])
                                                                      for kc in range(8)]),
                         reads=[wres] + HT(n), writes=[BK(bks[1])])

                def pc(n=n):
                    S.op("dve", lambda: nc.vector.tensor_tensor(out=tmp[:, 0, :], in0=banks[bks[0]][:, :], in1=rp[:, 0, CH(n)],
                                                                op=ALU.mult), reads=[BK(bks[0]), "special"], writes=[(tmpn, 0)])
                    S.op("dve", lambda: nc.vector.tensor_tensor(out=tmp[:, 1, :], in0=banks[bks[1]][:, :], in1=rp[:, 1, CH(n)],
                                                                op=ALU.mult), reads=[BK(bks[1]), "special"], writes=[(tmpn, 1)])
                    S.op("dve", lambda: nc.vector.tensor_tensor(out=dst[:, CH(n)], in0=tmp[:, 0, :], in1=tmp[:, 1, :],
                                                                op=ALU.add),
                         reads=[(tmpn, 0), (tmpn, 1)], writes=[(dres, dpar, n)])
                pcs += [pa, pb, pc]
            return pcs

        HP = (slice(0, 64), slice(64, 128))

        def zmm(i, t0, off, nq):
            def f():
                last = None
                for hh in range(2):
                    last = nc.tensor.matmul(banks[2 + hh][:, off:off + nq], lhsT=kT[HP[hh], i * 128:(i + 1) * 128],
                                            rhs=qT[HP[hh], t0 + off:t0 + off + nq], start=True, stop=True)
                return last
            return f

        def attn_sb(p, par=0, pieces=()):
            pieces = list(pieces)
            qTv, kTv, vpv = qkv_views(0, par)
            tiles = [(cq, i) for cq in range(4) for i in range(4 * cq + 3, -1, -1)]
            NT = len(tiles)
            EBT = [ebuf, ebuf2, ebuf3]
            SPT = [spb, spb2]
            EB = [[("ebuf", 0), ("ebuf", 1)], [("eb2", 0), ("eb2", 1)], [("eb3", 0), ("eb3", 1)]]
            SP = [[("spb", 0), ("spb", 1)], [("sp2", 0), ("sp2", 1)]]
            GB = [("gbuf", 0), ("gbuf", 1)]
            WB = [("wb", 0), ("wb", 1)]
            ZB = [2, 0]

            def geom(n):
                cq, i = tiles[n]
                off = max(0, (i - 4 * cq) * 128)
                return cq, i, off, slice(off, 512)

            def P1(n):
                cq, i, off, cs = geom(n)
                zb = ZB[n % 2]

                def f():
                    last = None
                    for hh in range(2):
                        last = nc.tensor.matmul(banks[zb + hh][:, cs], lhsT=kTv[HP[hh], i * 128:(i + 1) * 128],
                                                rhs=qTv[HP[hh], cq * 512 + off:(cq + 1) * 512], start=True, stop=True)
                    return last
                S.op("pe", f, reads=[("qT", par, cq), ("kT", par, i // 4)], writes=[BK(zb), BK(zb + 1)])

            def S1(n):
                cq, i, off, cs = geom(n)
                zb = ZB[n % 2]
                e = EBT[n % 3]
                S.op("act", lambda: nc.scalar.activation(out=e[:, :, cs], in_=psum[:, zb:zb + 2, cs], func=AF.Exp, scale=0.125),
                     reads=[BK(zb), BK(zb + 1)], writes=EB[n % 3])
                if i >= 4 * cq:
                    S.op("dve", lambda: nc.vector.tensor_tensor(
                        out=e[:, :, off:off + 128], in0=e[:, :, off:off + 128],
                        in1=ltb.unsqueeze(1).to_broadcast([128, 2, 128]), op=ALU.mult),
                        reads=EB[n % 3] + ["cstb"], writes=EB[n % 3])

            def S2(n):
                cq, i, off, cs = geom(n)
                S.op("act", lambda: nc.scalar.activation(out=SPT[n % 2][:, :, cs], in_=EBT[n % 3][:, :, cs], func=AF.Ln,
                                                         bias=1.0, scale=1.0), reads=EB[n % 3], writes=SP[n % 2])

            def P2(n):
                cq, i, off, cs = geom(n)
                first = (i == 4 * cq + 3)

                def f():
                    last = None
                    for hh in range(2):
                        last = nc.tensor.matmul(banks[4 + hh][:, cs], lhsT=geb, rhs=SPT[n % 2][:, hh, cs], start=first,
                                                stop=True, skip_group_check=True)
                    return last
                S.op("pe", f, reads=SP[n % 2] + ["cstb"], writes=[BK(4), BK(5)])

            def S3(n):
                cq, i, off, cs = geom(n)
                S.op("act", lambda: nc.scalar.activation(out=gbuf[:, :, cs], in_=psum[:, 4:6, cs], func=AF.Exp, scale=-1.0),
                     reads=[BK(4), BK(5)], writes=GB)

            def D2(n):
                cq, i, off, cs = geom(n)
                S.op("dve", lambda: nc.vector.tensor_tensor(out=wb[:, :, cs], in0=EBT[n % 3][:, :, cs], in1=gbuf[:, :, cs],
                                                            op=ALU.mult), reads=EB[n % 3] + GB, writes=WB)

            def LTB(n):
                cq, i, off, cs = geom(n)
                if i > 0:
                    def f():
                        last = None
                        for hh in range(2):
                            last = nc.tensor.matmul(banks[4 + hh][:, cs], lhsT=ltb, rhs=SPT[n % 2][:, hh, cs], start=False,
                                                    stop=True, skip_group_check=True)
                        return last
                    S.op("pe", f, reads=SP[n % 2] + ["cstb"], writes=[BK(4), BK(5)])

            def PV(n):
                cq, i, off, cs = geom(n)
                first = (i == 4 * cq + 3)

                def f2():
                    last = None
                    for hh in range(2):
                        last = nc.tensor.matmul(banks[6][HP[hh], cs], lhsT=vpv[:, i, HP[hh]], rhs=wb[:, hh, cs], start=first,
                                                stop=True, skip_group_check=True)
                    return last
                S.op("pe", f2, reads=WB + [("vp", par)], writes=[BK(6)])
                if i == 0:
                    S.op("dve", lambda: nc.vector.tensor_copy(out=big[:, p, CH(cq)], in_=banks[6][:, :]),
                         reads=[BK(6)], writes=[("big", p)])

            P1(0)
            S1(0)
            S2(0)
            P1(1)
            P2(0)
            for n in range(NT):
                if n + 1 < NT:
                    S1(n + 1)
                S3(n)
                if n + 2 < NT:
                    P1(n + 2)
                D2(n)
                if n + 1 < NT:
                    S2(n + 1)
                LTB(n)
                if n + 1 < NT:
                    P2(n + 1)
                PV(n)
                if pieces:
                    pieces.pop(0)()
            while pieces:
                pieces.pop(0)()

        def attn_fox(p, par=0, pieces=()):
            pieces = list(pieces)
            qTv, kTv, vpv = qkv_views(2, par)
            fx = special[:, :]
            nend = fx[:, 256:512].rearrange("p (j h) -> p j h", j=16)
            ncum = fx[:, 512:768].rearrange("p (j h) -> p j h", j=16)
            nmid = fx[:, 1792:2048].rearrange("p (j h) -> p j h", j=16)
            Rb = special[0:64, 2048:3072].bitcast(BF16)
            for hh in range(2):
                h = 2 * p + hh
                r0 = 32 * hh
                S.op("dve", lambda h=h, r0=r0: nc.vector.tensor_scalar(
                    out=Rb[r0:r0 + 1, :].rearrange("p (j c) -> p j c", j=16),
                    in0=nmid[r0:r0 + 1, :, h:h + 1].to_broadcast([1, 16, 128]), scalar1=-8.0, scalar2=None, op0=ALU.mult),
                    reads=["special"], writes=[("Rb", hh)])
            tiles = [(cq, i) for cq in range(4) for i in range(0, 4 * cq + 4)]
            NT = len(tiles)
            WT = [wb, wb2]
            WB = [[("wb", 0), ("wb", 1)], [("wb2", 0), ("wb2", 1)]]
            ZB = [2, 0]

            def geom(n):
                cq, i = tiles[n]
                off = max(0, (i - 4 * cq) * 128)
                return cq, i, off, slice(off, 512)

            def P1(n):
                cq, i, off, cs = geom(n)
                zb = ZB[n % 2]

                def f():
                    last = None
                    for hh in range(2):
                        nc.tensor.matmul(banks[zb + hh][:, cs], lhsT=kTv[HP[hh], i * 128:(i + 1) * 128],
                                         rhs=qTv[HP[hh], cq * 512 + off:(cq + 1) * 512], start=True, stop=False,
                                         skip_group_check=True)
                    for hh in range(2):
                        r0 = 32 * hh
                        last = nc.tensor.matmul(banks[zb + hh][:, cs], lhsT=onesb[r0:r0 + 1, :],
                                                rhs=Rb[r0:r0 + 1, cq * 512 + off:(cq + 1) * 512], start=False, stop=True,
                                                skip_group_check=True)
                    return last
                S.op("pe", f, reads=[("qT", par, cq), ("kT", par, i // 4), ("Rb", 0), ("Rb", 1), "onesb"], writes=[BK(zb), BK(zb + 1)])

            def SE(n):
                cq, i, off, cs = geom(n)
                zb = ZB[n % 2]
                w_ = WT[n % 2]
                for hh in range(2):
                    h = 2 * p + hh
                    S.op("act", lambda hh=hh, h=h: nc.scalar.activation(out=w_[:, hh, cs], in_=banks[zb + hh][:, cs], func=AF.Exp,
                                                                        scale=0.125, bias=ncum[:, i, h:h + 1]),
                         reads=[BK(zb + hh), "special"], writes=[WB[n % 2][hh]])
                if i >= 4 * cq:
                    S.op("dve", lambda: nc.vector.tensor_tensor(
                        out=w_[:, :, off:off + 128], in0=w_[:, :, off:off + 128],
                        in1=leb.unsqueeze(1).to_broadcast([128, 2, 128]), op=ALU.mult),
                        reads=WB[n % 2] + ["cstb"], writes=WB[n % 2])

            def P3(n):
                cq, i, off, cs = geom(n)
                first = (i == 0)
                w_ = WT[n % 2]

                def f():
                    last = None
                    for hh in range(2):
                        last = nc.tensor.matmul(banks[6][HP[hh], cs], lhsT=vpv[:, i, HP[hh]], rhs=w_[:, hh, cs], start=first,
                                                stop=True, skip_group_check=True)
                    for hh in range(2):
                        last = nc.tensor.matmul(banks[4][HP[hh], cs], lhsT=onesb[:, HP[hh]], rhs=w_[:, hh, cs], start=first,
                                                stop=True, skip_group_check=True)
                    return last
                S.op("pe", f, reads=WB[n % 2] + [("vp", par), "onesb"], writes=[BK(6), BK(4)])
                if i == 4 * cq + 3:
                    normalize(p, cq, None)

            P1(0)
            SE(0)
            for n in range(NT):
                if n + 1 < NT:
                    P1(n + 1)
                    SE(n + 1)
                P3(n)
                if pieces:
                    pieces.pop(0)()
            while pieces:
                pieces.pop(0)()

        def normalize(p, cq, sink_cols):
            if sink_cols is None:
                S.op("act", lambda: nc.scalar.activation(out=gbuf[:, 0, :], in_=banks[4][:, :], func=AF.Ln),
                     reads=[BK(4)], writes=[("gbuf", 0)])
            else:
                S.op("act", lambda: nc.scalar.activation(out=gbuf[:, 0, :], in_=banks[4][:, :], func=AF.Ln,
                                                         bias=small[:, 16 + p:17 + p], scale=1.0),
                     reads=[BK(4), "small"], writes=[("gbuf", 0)])
            S.op("dve", lambda: nc.vector.tensor_copy(out=gbuf[:, 1, :], in_=banks[6][:, :]), reads=[BK(6)], writes=[("gbuf", 1)])
            S.op("act", lambda: nc.scalar.activation(out=gbuf[:, 0, :], in_=gbuf[:, 0, :], func=AF.Exp, scale=-1.0),
                 reads=[("gbuf", 0)], writes=[("gbuf", 0)])
            S.op("dve", lambda: nc.vector.tensor_tensor(out=big[:, p, CH(cq)], in0=gbuf[:, 1, :], in1=gbuf[:, 0, :], op=ALU.mult),
                 reads=[("gbuf", 1), ("gbuf", 0)], writes=[("big", p)])

        def attn_swa(p, par=0, pieces=()):
            pieces = list(pieces)
            qTv = qT if par == 0 else vp2[:, :, :].rearrange("p a b -> p (a b)")
            WT = [wb, wb2]
            WB = [[("wb", 0), ("wb", 1)], [("wb2", 0), ("wb2", 1)]]
            ZB = [2, 0]

            def P1(i):
                nq = 256 if i < 15 else 128
                zb = ZB[i % 2]

                def f():
                    last = None
                    for hh in range(2):
                        last = nc.tensor.matmul(banks[zb + hh][:, 0:nq], lhsT=kT[HP[hh], i * 128:(i + 1) * 128],
                                                rhs=qTv[HP[hh], i * 128:i * 128 + nq], start=True, stop=True)
                    return last
                S.op("pe", f, reads=[("qT", par, n) for n in {i // 4, (i * 128 + nq - 1) // 512}] + [("kT", 0, i // 4)],
                     writes=[BK(zb), BK(zb + 1)])

            def SE(i):
                nq = 256 if i < 15 else 128
                zb = ZB[i % 2]
                w_ = WT[i % 2]
                S.op("act", lambda: nc.scalar.activation(out=w_[:, :, 0:nq], in_=psum[:, zb:zb + 2, 0:nq], func=AF.Exp, scale=0.125),
                     reads=[BK(zb), BK(zb + 1)], writes=WB[i % 2])
                S.op("dve", lambda: nc.vector.tensor_tensor(out=w_[:, :, 0:nq], in0=w_[:, :, 0:nq],
                                                            in1=bandb[:, 0:nq].unsqueeze(1).to_broadcast([128, 2, nq]), op=ALU.mult),
                     reads=WB[i % 2] + ["cstb"], writes=WB[i % 2])

            def pv(i, e0, c0, n, first):
                w_ = WT[i % 2]

                def f():
                    last = None
                    for hh in range(2):
                        last = nc.tensor.matmul(banks[6][HP[hh], c0:c0 + n], lhsT=vp[:, i, 0:64], rhs=w_[:, hh, e0:e0 + n],
                                                start=first, stop=True, skip_group_check=True)
                    for hh in range(2):
                        last = nc.tensor.matmul(banks[4][HP[hh], c0:c0 + n], lhsT=onesb[:, HP[hh]], rhs=w_[:, hh, e0:e0 + n],
                                                start=first, stop=True, skip_group_check=True)
                    return last
                S.op("pe", f, reads=WB[i % 2] + [("vp", 0), "onesb"], writes=[BK(6), BK(4)])

            def P3(i):
                cq = i // 4
                c0 = (i % 4) * 128
                if i % 4 != 3:
                    pv(i, 0, c0, 256, i == 0)
                else:
                    pv(i, 0, c0, 128, False)
                    normalize(p, cq, True)
                    if i < 15:
                        pv(i, 128, 0, 128, True)

            P1(0)
            SE(0)
            for i in range(16):
                if i + 1 < 16:
                    P1(i + 1)
                    SE(i + 1)
                P3(i)
                if pieces:
                    pieces.pop(0)()
            while pieces:
                pieces.pop(0)()

        def fox_prep(wt, wres):
            wv = v3(wt, 8, 512)
            fx = special[:, :]
            nlf = fx[:, 0:256]

            def f():
                last = None
                for i in range(16):
                    last = mm_group(banks[2][:, i * 16:(i + 1) * 16],
                                    [(hT[:, kc, i * 128:(i + 1) * 128], wv[:, kc, 0:16]) for kc in range(8)])
                    last = nc.tensor.matmul(banks[2][:, i * 16:(i + 1) * 16], lhsT=onesf[0:1, :], rhs=rowst[0:1, 16:32],
                                            start=False, stop=True, skip_group_check=True)
                return last
            S.op("pe", f, reads=[wres, "rowst1", "onesf"] + [("hT", c, n) for c in range(8) for n in range(4)],
                 writes=[BK(2)])
            S.op("act", lambda: nc.scalar.activation(out=nlf, in_=banks[2][:, 0:256], func=AF.Exp, scale=-1.0),
                 reads=[BK(2)], writes=["special"])
            S.op("act", lambda: nc.scalar.activation(out=nlf, in_=nlf, func=AF.Ln, bias=1.0, scale=1.0),
                 reads=["special"], writes=["special"])

            def f2():
                last = None
                for j in range(16):
                    for i2 in range(j + 1):
                        last = nc.tensor.matmul(banks[3][:, j * 16:(j + 1) * 16], lhsT=onesf[:, :],
                                                rhs=nlf[:, i2 * 16:(i2 + 1) * 16], start=(i2 == 0), stop=(i2 == j))
                for i in range(16):
                    last = nc.tensor.matmul(banks[4][:, i * 16:(i + 1) * 16], lhsT=lef, rhs=nlf[:, i * 16:(i + 1) * 16],
                                            start=True, stop=True)
                return last
            S.op("pe", f2, reads=["special", "onesf", "cstf"], writes=[BK(3), BK(4)])
            S.op("dve", lambda: nc.vector.tensor_copy(out=fx[:, 256:512], in_=banks[3][:, 0:256]), reads=[BK(3)],
                 writes=["special"])
            S.op("dve", lambda: nc.vector.tensor_copy(out=fx[:, 512:528], in_=banks[4][:, 0:16]), reads=[BK(4)],
                 writes=["special"])
            S.op("dve", lambda: nc.vector.tensor_tensor(out=fx[:, 528:768], in0=banks[4][:, 16:256], in1=fx[:, 256:496],
                                                        op=ALU.add), reads=[BK(4), "special"], writes=["special"])
            S.op("dve", lambda: nc.vector.tensor_scalar(out=fx[:, 1792:1808], in0=fx[:, 256:272], scalar1=0.5, scalar2=None,
                                                        op0=ALU.mult), reads=["special"], writes=["special"])
            S.op("dve", lambda: nc.vector.tensor_tensor(out=fx[:, 1808:2048], in0=fx[:, 256:496], in1=fx[:, 272:512],
                                                        op=ALU.add), reads=["special"], writes=["special"])
            S.op("dve", lambda: nc.vector.tensor_scalar(out=fx[:, 1808:2048], in0=fx[:, 1808:2048], scalar1=0.5, scalar2=None,
                                                        op0=ALU.mult), reads=["special"], writes=["special"])

        def dump_bf(src):
            for c in range(8):
                for n in range(4):
                    S.op("dve", lambda c=c, n=n: nc.vector.tensor_copy(out=xT[:, c, CH(n)], in_=src[:, c, CH(n)]),
                         reads=[("hT", c, n), ("big", c), ("qT", 0, n), ("kT", 0, n), ("vp", 0)], writes=[("xT", c, n)])
            raise _Stop()

        def layer(l, prenormed, next_norm):
            k = KINDS[l]
            if not prenormed:
                norm(3 * l + 0)
            if dbg == "h":
                dump_bf(hT)
            if k == 1:
                rp = special[:, 0:4096].rearrange("p (a t) -> p a t", a=2)
                S.dma("sp", S.new_dma_sem("d_rope"), rp, rope_in, writes=["special"])
                S.op("pe", lambda: nc.tensor.matmul(banks[0][:, 0:16], lhsT=onesf[0:1, :], rhs=rowst[0:1, 0:16],
                                                    start=True, stop=True), reads=["onesf", "rowst0"], writes=[BK(0)])
                S.op("act", lambda: nc.scalar.activation(out=small[:, 0:16], in_=banks[0][:, 0:16], func=AF.Exp),
                     reads=[BK(0)], writes=["small"])
                for hh in range(2):
                    S.op("dve", lambda hh=hh: nc.vector.tensor_copy(
                        out=small[HP[hh], 16:24], in_=small[HP[hh], 0:16].rearrange("p (a b) -> p a b", b=2)[:, :, hh]),
                        reads=["small"], writes=["small"])
            if k == 2:
                wt, wres = ws_next()
                fox_prep(wt, wres)
            def make_pieces(pp, par, overlapped):
                wt, wres = ws_next()
                qd, kd, vd = qkv_views(k, par)
                bank = 7 if overlapped else None
                eng = "dve" if overlapped else "act"
                return (proj_fm(wt, wres, 0, qd, "qT", copy_eng=eng, dpar=par, bank=bank)
                        + proj_fm(wt, wres, 128, kd, "kT", copy_eng="dve", dpar=par, bank=bank)
                        + proj_v(wt, wres, 256, 128, vdst=vd, dpar=par, bank=bank, copy_eng=eng))

            if k in (0, 2):
                for pc in make_pieces(0, 0, False):
                    pc()
            for p in range(8):
                if k in (0, 2):
                    nxt = make_pieces(p + 1, (p + 1) % 2, True) if p + 1 < 8 else []
                    (attn_sb if k == 0 else attn_fox)(p, p % 2, nxt)
                    continue
                if True:
                    if p % 4 == 0:
                        wt, wres = ws_next()
                        swap_cols(wt, wres)
                        for pc in proj_rope(wt, wres, 0, kT, "kT") + proj_v(wt, wres, 256, 64):
                            pc()
                    if p == 0:
                        wt, wres = ws_next()
                        swap_cols(wt, wres)
                        for pc in proj_rope(wt, wres, 0, qT, "qT"):
                            pc()
                    nxt = []
                    if p + 1 < 8:
                        wt, wres = ws_next()
                        swap_cols(wt, wres)
                        npar = (p + 1) % 2
                        qd = qT if npar == 0 else vp2[:, :, :].rearrange("p a b -> p (a b)")
                        nxt = proj_rope(wt, wres, 0, qd, "qT", dpar=npar, bks=(5, 7), tmp=ebuf, tmpn="ebuf")
                    attn_swa(p, p % 2, nxt)
            if dbg == "a":
                dump_bf(big)
            for hf in range(2):
                wt, wres = ws_next()
                wv = v3(wt, 8, 512)
                order = [(ftl, n) for ftl in range(4) for n in range(4)] if hf == 0 else \
                        [(ftl, n) for n in range(4) for ftl in range(4)]
                for (ftl, n) in order:
                    ft = hf * 4 + ftl
                    b = next_bank()
                    S.op("pe", lambda b=b, n=n, ftl=ftl, wv=wv: mm_group(
                        banks[b][:, :], [(wv[:, kc, ftl * 128:(ftl + 1) * 128], big[:, kc, CH(n)]) for kc in range(8)]),
                        reads=[wres] + [("big", kc) for kc in range(8)], writes=[BK(b)])
                    resid_add(ft, n, b)
                    if hf == 1 and ftl == 3 and n >= 1 and dbg is None:
                        norm_chunk(3 * l + 1, n - 1)
            if dbg is None:
                norm_chunk(3 * l + 1, 3)
            if dbg == "xa":
                raise _Stop()
            pT = qk
            PST = [wb[:, :, :].rearrange("p a b -> p (a b)").bitcast(F32), wb2[:, :, :].rearrange("p a b -> p (a b)").bitcast(F32)]
            PSR = [[("wb", 0), ("wb", 1)], [("wb2", 0), ("wb2", 1)]]
            def p_dma(i):
                S.dma("sp", d_st[i % 2], PST[i % 2][:, 0:256], p_in[l, i * 128:(i + 1) * 128, :], writes=PSR[i % 2])

            def p_xpose(i):
                stg = PST[i % 2]

                def f():
                    last = None
                    for kc in range(2):
                        last = nc.tensor.transpose(banks[7][:, kc * 128:(kc + 1) * 128], stg[:, kc * 128:(kc + 1) * 128], identf)
                    return last
                S.op("pe", f, reads=PSR[i % 2] + ["cstf"], writes=[BK(7)])
                src = banks[7][:, 0:256].rearrange("p (a t) -> p a t", a=2)
                S.op("act", lambda: nc.scalar.copy(out=pT[:, :, i * 128:(i + 1) * 128], in_=src),
                     reads=[BK(7)], writes=[("qT", 0, i // 4), ("kT", 0, i // 4)])
            p_dma(0)
            p_dma(1)
            if dbg is not None:
                norm(3 * l + 1)
            for g in range(8):
                wt, wres = ws_next()
                wv = v3(wt, 8, 512)
                ub = g % 2
                uT = big[:, ub * 4:(ub + 1) * 4, :]
                for ftl in range(4):
                    for n in range(4):
                        b = next_bank()
                        S.op("pe", lambda b=b, n=n, ftl=ftl, wv=wv: mm_group(
                            banks[b][:, :], [(wv[:, kc, ftl * 128:(ftl + 1) * 128], hT[:, kc, CH(n)]) for kc in range(8)]),
                            reads=[wres] + HT(n), writes=[BK(b)])
                        tb = ebuf if (ftl * 4 + n) % 2 == 0 else gbuf
                        tr = "ebuf" if (ftl * 4 + n) % 2 == 0 else "gbuf"
                        S.op("act", lambda b=b, tb=tb: nc.scalar.activation(out=tb[:, 0, :], in_=banks[b][:, :], func=AF.Relu),
                             reads=[BK(b)], writes=[(tr, 0)])
                        S.op("dve", lambda tb=tb, n=n, ftl=ftl, uT=uT: nc.vector.tensor_tensor(
                            out=uT[:, ftl, CH(n)], in0=tb[:, 0, :], in1=tb[:, 0, :], op=ALU.mult),
                            reads=[(tr, 0)], writes=[("big", ub * 4 + ftl)])
                for i in (2 * g, 2 * g + 1):
                    p_xpose(i)
                    if i + 2 < 16:
                        p_dma(i + 2)
                wt, wres = ws_next()
                wv = v3(wt, 4, 1024)
                order = [(ft, n) for ft in range(8) for n in range(4)] if g < 7 else \
                        [(ft, n) for n in range(4) for ft in range(8)]
                for (ft, n) in order:
                    b = next_bank()
                    S.op("pe", lambda b=b, n=n, ft=ft, wv=wv, uT=uT: mm_group(
                        banks[b][:, :], [(wv[:, kc, ft * 128:(ft + 1) * 128], uT[:, kc, CH(n)]) for kc in range(4)]),
                        reads=[wres] + [("big", ub * 4 + kc) for kc in range(4)], writes=[BK(b)])
                    resid_add(ft, n, b)
                    if g == 7 and ft == 7 and n >= 1 and dbg is None:
                        norm_chunk(3 * l + 2, n - 1)
            if dbg is None:
                norm_chunk(3 * l + 2, 3)
            if dbg == "xm":
                raise _Stop()
            pT = qk
            if dbg is not None:
                norm(3 * l + 2)
            wgs = [ws_next(), ws_next(hold=True)]
            wpt, wpres = ws_next(hold=True)
            wpv = v3(wpt, 2, 1024)
            gtmp = ebuf[:, 1, :]
            for n in range(4):
                for ft in range(8):
                    wt, wres = wgs[ft // 4]
                    wv = v3(wt, 8, 512)
                    ftl = ft % 4
                    bg = next_bank()
                    S.op("pe", lambda bg=bg, n=n, ftl=ftl, wv=wv: mm_group(
                        banks[bg][:, :], [(wv[:, kc, ftl * 128:(ftl + 1) * 128], hT[:, kc, CH(n)]) for kc in range(8)]),
                        reads=[wres] + HT(n), writes=[BK(bg)])
                    bp = next_bank()
                    S.op("pe", lambda bp=bp, n=n, ft=ft: mm_group(
                        banks[bp][:, :], [(wpv[:, kc, ft * 128:(ft + 1) * 128], pT[:, kc, CH(n)]) for kc in range(2)]),
                        reads=[wpres, ("qT", 0, n), ("kT", 0, n)], writes=[BK(bp)])
                    S.op("act", lambda bg=bg: nc.scalar.activation(out=gtmp, in_=banks[bg][:, :], func=AF.Sigmoid),
                         reads=[BK(bg)], writes=[("ebuf", 1)])
                    S.op("dve", lambda bp=bp: nc.vector.tensor_tensor(out=gtmp, in0=banks[bp][:, :], in1=gtmp,
                                                                      op=ALU.mult), reads=[BK(bp), ("ebuf", 1)], writes=[("ebuf", 1)])
                    S.op("dve", lambda n=n, ft=ft: nc.vector.tensor_tensor(out=xT[:, ft, CH(n)], in0=gtmp,
                                                                           in1=xT[:, ft, CH(n)], op=ALU.add),
                         reads=[("ebuf", 1), ("xT", ft, n)], writes=[("xT", ft, n)])
                if next_norm is not None and n >= 1 and dbg is None:
                    norm_chunk(next_norm[0], n - 1, next_norm[1])
            if next_norm is not None and dbg is None:
                norm_chunk(next_norm[0], 3, next_norm[1])

        def alias_sync(names_from, names_to):
            for a in names_from:
                for b in names_to:
                    if a in S.wr:
                        cur = S.wr.get(b)
                        S.rd.setdefault(b, {})
                        k, v = S.wr[a]
                        S.rd[b][k] = max(S.rd[b].get(k, 0), v)
                    for k, v in S.rd.get(a, {}).items():
                        S.rd.setdefault(b, {})
                        S.rd[b][k] = max(S.rd[b].get(k, 0), v)

        EG = [("ebuf", 0), ("ebuf", 1), ("gbuf", 0), ("gbuf", 1)]
        STG = [("stage", 0), ("stage", 1), ("stage", 0, 0), ("stage", 0, 1), ("stage", 1, 0), ("stage", 1, 1)]
        load_x()
        try:
            for idx, l in enumerate(layers):
                if dbg is not None:
                    nxt = None
                elif idx + 1 < len(layers):
                    nxt = (3 * layers[idx + 1], False)
                else:
                    nxt = (12, True) if final else None
                layer(l, prenormed=(idx > 0 and dbg is None), next_norm=nxt)
        except _Stop:
            pass
        if final and (not layers or dbg is not None):
            norm(12, inplace=True)
        store_x()
        S.wait_all("sp", d_out)
    return nc


_CONSTS = None


def _run(layers, final, x, inputs):
    global _CONSTS
    if _CONSTS is None:
        _CONSTS = host_consts()
    cst, rope = _CONSTS
    nc = build(layers, final)
    gl = []
    for l in range(DEPTH):
        gl += [inputs[f"attn_norm_{l}"], inputs[f"mlp_norm_{l}"], inputs[f"ple_norm_{l}"]]
    gl.append(inputs["final_norm"])
    gains = np.ascontiguousarray(np.stack(gl).astype(np.float32).reshape(13 * 8, 128))
    shared = dict(cst=cst, rope=rope, gains=gains,
                  sinks=np.ascontiguousarray(inputs["sinks_1"].reshape(1, 16)),
                  bfor=np.ascontiguousarray(inputs["b_forget_2"].reshape(1, 16)))
    for l in layers:
        shared[f"w_in_{l}"] = inputs[f"w_in_{l}"]
        shared[f"w_out_{l}"] = inputs[f"w_out_{l}"]
        shared[f"w_up_{l}"] = inputs[f"w_up_{l}"]
        shared[f"w_down_{l}"] = inputs[f"w_down_{l}"]
        shared[f"w_ple_gate_{l}"] = inputs[f"w_ple_gate_{l}"]
        shared[f"w_ple_proj_{l}"] = inputs[f"w_ple_proj_{l}"]
    p = inputs["p"]
    in_maps = []
    for c in range(NCORES):
        m = dict(shared)
        m["x"] = np.ascontiguousarray(x[c])
        m["p"] = np.ascontiguousarray(p[:, c])
        in_maps.append(m)
    res = run_bass_kernel_spmd(nc, in_maps, core_ids=list(range(NCORES)))
    return np.stack([np.asarray(r["y"]) for r in res.results], axis=0).astype(np.float32)


def kernel(**inputs):
    inputs = {k: np.asarray(v) for k, v in inputs.items()}
    x = np.ascontiguousarray(inputs["x"], dtype=np.float32)
    if FUSED:
        return _run(list(range(DEPTH)), True, x, inputs)
    for l in range(DEPTH):
        x = _run([l], l == DEPTH - 1, x, inputs)
    return x
```

```python
import numpy as np
from contextlib import ExitStack
import concourse.bass as bass
import concourse.mybir as mybir
from concourse.bass_utils import run_bass_kernel_spmd

F32 = mybir.dt.float32
BF16 = mybir.dt.bfloat16
AF = mybir.ActivationFunctionType
ALU = mybir.AluOpType

S_LEN = 2048
D = 1024
DEPTH = 4
NCORES = 8
KINDS = (0, 1, 2, 0)
EPS = 1e-6
FUSED = True
DBG_I = 3


class Sched:
    def __init__(self, nc, stack):
        self.nc = nc
        self.stack = stack
        self.engs = {"pe": nc.tensor, "act": nc.scalar, "dve": nc.vector, "pool": nc.gpsimd, "sp": nc.sync}
        self.sem = {k: stack.enter_context(nc.semaphore("s_" + k)) for k in self.engs}
        self.cnt = {k: 0 for k in self.engs}
        self.known = {k: {} for k in self.engs}
        self.wr = {}
        self.rd = {}
        self.semobj = dict(self.sem)
        self.dmacnt = {}

    def new_dma_sem(self, name):
        s = self.stack.enter_context(self.nc.semaphore(name))
        self.semobj[name] = s
        self.dmacnt[name] = 0
        return name

    def _need(self, eng, reads, writes):
        need = {}

        def add(c):
            if c is None:
                return
            k, v = c
            if need.get(k, 0) < v:
                need[k] = v
        for r in reads:
            add(self.wr.get(r))
        for r in writes:
            add(self.wr.get(r))
            for k, v in self.rd.get(r, {}).items():
                add((k, v))
        for k, v in need.items():
            if k == eng and eng == "pe":
                continue
            if self.known[eng].get(k, 0) >= v:
                continue
            self.engs[eng].wait_ge(self.semobj[k], v)
            self.known[eng][k] = v

    def _mark(self, clock, reads, writes):
        k, v = clock
        for r in reads:
            self.rd.setdefault(r, {})[k] = v
        for r in writes:
            self.wr[r] = clock
            self.rd[r] = {}

    def op(self, eng, fn, reads=(), writes=()):
        self._need(eng, reads, writes)
        inst = fn()
        self.cnt[eng] += 1
        inst.then_inc(self.sem[eng], 1)
        self._mark((eng, self.cnt[eng]), reads, writes)
        return inst

    def dma(self, queue, semname, out, in_, reads=(), writes=(), **kw):
        self._need(queue, reads, writes)
        inst = self.engs[queue].dma_start(out=out, in_=in_, **kw)
        self.dmacnt[semname] += 16
        inst.then_inc(self.semobj[semname], 16)
        self._mark((semname, self.dmacnt[semname]), reads, writes)
        return inst

    def wait_all(self, eng, semnames):
        for s in semnames:
            if self.dmacnt[s] > 0:
                self.engs[eng].wait_ge(self.semobj[s], self.dmacnt[s])


def host_consts():
    a = np.arange(128)
    ident = np.eye(128, dtype=np.float32)
    ge = (a[:, None] >= a[None, :]).astype(np.float32)
    lt = (a[:, None] < a[None, :]).astype(np.float32)
    le = (a[:, None] <= a[None, :]).astype(np.float32)
    f = np.arange(256)
    band = ((f[None, :] - a[:, None] >= 0) & (f[None, :] - a[:, None] < 128)).astype(np.float32)
    cst = np.concatenate([ident, ge, lt, le, band], axis=1)
    half = 8
    inv_freq = (np.float32(500000.0) ** (-np.arange(half, dtype=np.float32) / np.float32(half))).astype(np.float32)
    ang = np.arange(S_LEN, dtype=np.float32)[:, None] * inv_freq[None, :]
    cos = np.cos(ang).astype(np.float32).T
    sin = np.sin(ang).astype(np.float32).T
    rope = np.zeros((128, 2, S_LEN), np.float32)
    rope[:, 0, :] = 1.0
    for hh in range(2):
        b = 64 * hh
        rope[b:b + 8, 0] = cos
        rope[b + 8:b + 16, 0] = cos
        rope[b:b + 8, 1] = -sin
        rope[b + 8:b + 16, 1] = sin
    return cst, rope


class _Stop(Exception):
    pass


def build(layers, final, dbg=None):
    nc = bass.Bass("TRN2", target_bir_lowering=False)
    dt_in = lambda n, s: nc.dram_tensor(n, list(s), F32, kind="ExternalInput").ap()
    x_in = dt_in("x", (S_LEN, D))
    p_in = dt_in("p", (DEPTH, S_LEN, 256))
    cst_in = dt_in("cst", (128, 768))
    rope_in = dt_in("rope", (128, 2, S_LEN))
    gains_in = dt_in("gains", (104, 128))
    sinks_in = dt_in("sinks", (1, 16))
    bfor_in = dt_in("bfor", (1, 16))
    W = {}
    for l in layers:
        k = KINDS[l]
        W[l] = dict(
            w_in=dt_in(f"w_in_{l}", (D, (3072, 1280, 3088)[k])),
            w_out=dt_in(f"w_out_{l}", (D, D)),
            w_up=dt_in(f"w_up_{l}", (D, 4096)),
            w_down=dt_in(f"w_down_{l}", (4096, D)),
            w_gate=dt_in(f"w_ple_gate_{l}", (D, D)),
            w_proj=dt_in(f"w_ple_proj_{l}", (256, D)),
        )
    y_out = nc.dram_tensor("y", [S_LEN, D], F32, kind="ExternalOutput").ap()

    with ExitStack() as st:
        S = Sched(nc, st)
        sb = lambda n, s, d: st.enter_context(nc.sbuf_tensor("sb_" + n, list(s), d))
        xT = sb("xT", (128, 8, S_LEN), F32)
        hT = sb("hT", (128, 8, S_LEN), BF16)
        big = sb("big", (128, 8, S_LEN), BF16)
        vp = sb("vp", (128, 16, 128), BF16)
        qk = sb("qk", (128, 2, S_LEN), BF16)
        wsl = [sb(f"ws{i}", (128, 4096), BF16) for i in range(3)]
        ebuf = sb("ebuf", (128, 2, 512), F32)
        gbuf = sb("gbuf", (128, 2, 512), F32)
        spb = sb("spb", (128, 2, 512), BF16)
        wb = sb("wb", (128, 2, 512), BF16)
        special = sb("special", (128, 4608), F32)
        cstf = sb("cstf", (128, 768), F32)
        cstb = sb("cstb", (128, 768), BF16)
        gains = sb("gains", (128, 104), F32)
        gstage = sb("gstage", (104, 128), F32)
        onesf = sb("onesf", (128, 128), F32)
        onesb = sb("onesb", (128, 128), BF16)
        small = sb("small", (128, 64), F32)
        rowst = sb("rowst", (1, 32), F32)
        psum = st.enter_context(nc.psum_tensor("psum", [128, 8, 512], F32))
        banks = [psum[:, i, :] for i in range(8)]
        wb2 = sb("wb2", (128, 2, 512), BF16)
        rstd = gbuf[:, 1, :]
        vp2 = sb("vp2", (128, 16, 128), BF16)
        ebuf3 = special[:, 1536:2560].rearrange("p (a b) -> p a b", a=2)
        ebuf2 = special[:, 0:1024].rearrange("p (a b) -> p a b", a=2)
        spb2 = special[:, 1024:1536].bitcast(BF16).rearrange("p (a b) -> p a b", a=2)
        BK = lambda i: ("bank", i)

        identf = cstf[:, 0:128]
        lef = cstf[:, 384:512]
        geb, ltb, leb, bandb = cstb[:, 128:256], cstb[:, 256:384], cstb[:, 384:512], cstb[:, 512:768]
        stage = [ebuf, gbuf]
        qT, kT = qk[:, 0, :], qk[:, 1, :]
        QKV = {"cur": 0}

        def qkv_views(kind, par):
            if par == 0:
                return qk[:, 0, :], qk[:, 1, :], vp
            if kind == 0:
                a, b = 2560, 3584
            else:
                a, b = 768, 3072
            return (special[:, a:a + 1024].bitcast(BF16), special[:, b:b + 1024].bitcast(BF16), vp2)
        CH = lambda n: slice(n * 512, (n + 1) * 512)

        d_st = [S.new_dma_sem("d_st0"), S.new_dma_sem("d_st1")]
        d_out = [S.new_dma_sem("d_out0"), S.new_dma_sem("d_out1")]
        d_w = [S.new_dma_sem(f"d_w{i}") for i in range(3)]

        S.dma("sp", S.new_dma_sem("d_c0"), cstf[:, :], cst_in, writes=["cstf"])
        S.dma("sp", S.new_dma_sem("d_c1"), gstage[:, :], gains_in, writes=["gstage"])
        S.dma("sp", S.new_dma_sem("d_c2"), rowst[:, 0:16], sinks_in, writes=["rowst0"])
        S.dma("sp", S.new_dma_sem("d_c3"), rowst[:, 16:32], bfor_in, writes=["rowst1"])
        S.op("dve", lambda: nc.vector.memset(onesf[:, :], 1.0), writes=["onesf"])
        S.op("dve", lambda: nc.vector.memset(onesb[:, :], 1.0), writes=["onesb"])
        S.op("dve", lambda: nc.vector.tensor_copy(out=cstb[:, :], in_=cstf[:, :]), reads=["cstf"], writes=["cstb"])
        S.op("pe", lambda: nc.tensor.transpose(banks[7][:, 0:104], gstage[:, :], identf[0:104, 0:104]),
             reads=["gstage", "cstf"], writes=[BK(7)])
        S.op("act", lambda: nc.scalar.copy(out=gains[:, :], in_=banks[7][:, 0:104]), reads=[BK(7)], writes=["gains"])

        blocks = []

        class WS:
            issued = 0
            cur = -1

        def ws_issue(upto):
            while WS.issued <= min(upto, len(blocks) - 1):
                b = WS.issued
                slot = b % 3
                for (dstfn, src) in blocks[b]:
                    S.dma("pool", d_w[slot], dstfn(wsl[slot]), src, writes=[("ws", slot)])
                WS.issued += 1

        def ws_next(hold=False):
            WS.cur += 1
            ws_issue(WS.cur if hold else WS.cur + 2)
            return wsl[WS.cur % 3], ("ws", WS.cur % 3)

        def v3(t, a, b):
            return t[:, 0:a * b].rearrange("p (a b) -> p a b", a=a)

        def wview(wd, c0, ncols):
            return wd.rearrange("(kc p) n -> p kc n", p=128)[:, :, c0:c0 + ncols]

        for l in layers:
            k = KINDS[l]
            wd = W[l]
            if k == 2:
                blocks.append([(lambda t: v3(t, 8, 512)[:, :, 0:16], wview(wd["w_in"], 3072, 16))])
            for p in range(8):
                if k in (0, 2):
                    blocks.append([
                        ((lambda t, o=o: v3(t, 8, 512)[:, :, o * 128:(o + 1) * 128]),
                         wview(wd["w_in"], o * 1024 + p * 128, 128)) for o in range(3)])
                else:
                    def kv_block(g):
                        kc0 = 1024 + 64 * g
                        ent = [(lambda t, o=o: v3(t, 8, 512)[:, :, o:o + 64], wview(wd["w_in"], kc0, 64)) for o in (0, 64)]
                        ent.append((lambda t: v3(t, 8, 512)[:, :, 256:320], wview(wd["w_in"], 1152 + 64 * g, 64)))
                        blocks.append(ent)
                    if p == 0:
                        kv_block(0)
                    blocks.append([(lambda t: v3(t, 8, 512)[:, :, 0:128], wview(wd["w_in"], p * 128, 128))])
                    if p == 4:
                        kv_block(1)
            for hf in range(2):
                blocks.append([(lambda t: v3(t, 8, 512), wview(wd["w_out"], hf * 512, 512))])
            for g in range(8):
                blocks.append([(lambda t: v3(t, 8, 512), wview(wd["w_up"], g * 512, 512))])
                blocks.append([(lambda t: v3(t, 4, 1024),
                               wd["w_down"][g * 512:(g + 1) * 512, :].rearrange("(kc p) n -> p kc n", p=128))])
            for hf in range(2):
                blocks.append([(lambda t: v3(t, 8, 512), wview(wd["w_gate"], hf * 512, 512))])
            blocks.append([(lambda t: v3(t, 2, 1024), wd["w_proj"].rearrange("(kc p) n -> p kc n", p=128))])

        def mm_group(bank_ap, pairs, first=True, skip=False):
            last = None
            n = len(pairs)
            for i, (l_, r_) in enumerate(pairs):
                last = nc.tensor.matmul(bank_ap, lhsT=l_, rhs=r_, start=(first and i == 0), stop=(i == n - 1),
                                        skip_group_check=skip)
            return last

        HT = lambda n: [("hT", c, n) for c in range(8)]
        bank_rr = [0]

        def next_bank(lo=0, hi=4):
            b = lo + bank_rr[0] % (hi - lo)
            bank_rr[0] += 1
            return b

        SR = lambda j: [("ebuf" if j == 0 else "gbuf", 0), ("ebuf" if j == 0 else "gbuf", 1)]

        def load_x():
            for i in range(16):
                stg = stage[i % 2]
                sv = stg[:, :, :].rearrange("p a b -> p (a b)")
                S.dma("sp", d_st[i % 2], sv, x_in[i * 128:(i + 1) * 128, :], writes=SR(i % 2))
                for half in range(2):
                    b = next_bank()

                    def f(b=b, half=half, sv=sv):
                        last = None
                        for cc in range(4):
                            c = half * 4 + cc
                            last = nc.tensor.transpose(banks[b][:, cc * 128:(cc + 1) * 128],
                                                       sv[:, c * 128:(c + 1) * 128], identf)
                        return last
                    S.op("pe", f, reads=SR(i % 2) + ["cstf"], writes=[BK(b)])
                    dst = xT[:, half * 4:half * 4 + 4, i * 128:(i + 1) * 128]
                    src = banks[b][:, :].rearrange("p (c t) -> p c t", c=4)
                    wr = [("xT", c, i // 4) for c in range(half * 4, half * 4 + 4)]
                    if half == 0:
                        S.op("act", lambda dst=dst, src=src: nc.scalar.copy(out=dst, in_=src), reads=[BK(b)], writes=wr)
                    else:
                        S.op("dve", lambda dst=dst, src=src: nc.vector.tensor_copy(out=dst, in_=src), reads=[BK(b)], writes=wr)

        def store_x():
            for i in range(16):
                stg = stage[i % 2]
                sv = stg[:, :, :].rearrange("p a b -> p (a b)")
                for half in range(2):
                    b = next_bank()

                    def f(b=b, half=half):
                        last = None
                        for cc in range(4):
                            c = half * 4 + cc
                            last = nc.tensor.transpose(banks[b][:, cc * 128:(cc + 1) * 128],
                                                       xT[:, c, i * 128:(i + 1) * 128], identf)
                        return last
                    S.op("pe", f, reads=[("xT", c, i // 4) for c in range(half * 4, half * 4 + 4)] + ["cstf"],
                         writes=[BK(b)])
                    dst = sv[:, half * 512:(half + 1) * 512]
                    if half == 0:
                        S.op("act", lambda dst=dst, b=b: nc.scalar.copy(out=dst, in_=banks[b][:, :]),
                             reads=[BK(b)], writes=[SR(i % 2)[0]])
                    else:
                        S.op("dve", lambda dst=dst, b=b: nc.vector.tensor_copy(out=dst, in_=banks[b][:, :]),
                             reads=[BK(b)], writes=[SR(i % 2)[1]])
                S.dma("sp", d_out[i % 2], y_out[i * 128:(i + 1) * 128, :], sv,
                      reads=SR(i % 2))

        def stage_guard():
            pass

        def norm_chunk(gidx, n, inplace=False):
            for c in range(8):
                sq = spb[:, c % 2, :]
                S.op("act", lambda sq=sq, c=c: nc.scalar.activation(out=sq, in_=xT[:, c, CH(n)], func=AF.Square),
                     reads=[("xT", c, n)], writes=[("spb", c % 2)])
                S.op("pe", lambda sq=sq, c=c: nc.tensor.matmul(banks[7][:, :], lhsT=onesb[:, :], rhs=sq,
                                                               start=(c == 0), stop=(c == 7)),
                     reads=[("spb", c % 2), "onesb"], writes=[BK(7)])
            S.op("act", lambda: nc.scalar.activation(out=rstd, in_=banks[7][:, :], func=AF.Ln,
                                                     scale=1.0 / D, bias=epsb[:, 0:1]),
                 reads=[BK(7), "epsb"], writes=[("gbuf", 1)])
            S.op("act", lambda: nc.scalar.activation(out=rstd, in_=rstd, func=AF.Exp, scale=-0.5),
                 reads=[("gbuf", 1)], writes=[("gbuf", 1)])
            for c in range(8):
                dst = xT[:, c, CH(n)] if inplace else hT[:, c, CH(n)]
                wr = [("xT", c, n)] if inplace else [("hT", c, n)]
                S.op("dve", lambda dst=dst, c=c: nc.vector.scalar_tensor_tensor(
                    out=dst, in0=xT[:, c, CH(n)], scalar=gains[:, gidx * 8 + c:gidx * 8 + c + 1], in1=rstd,
                    op0=ALU.mult, op1=ALU.mult), reads=[("xT", c, n), "gains", ("gbuf", 1)], writes=wr)


        def norm(gidx, inplace=False):
            for n in range(4):
                norm_chunk(gidx, n, inplace)

        epsb = sb("epsb", (128, 1), F32)
        S.op("dve", lambda: nc.vector.memset(epsb[:, :], EPS), writes=["epsb"])

        def resid_add(ft, n, b):
            S.op("dve", lambda: nc.vector.tensor_tensor(out=xT[:, ft, CH(n)], in0=banks[b][:, :], in1=xT[:, ft, CH(n)],
                                                        op=ALU.add), reads=[BK(b), ("xT", ft, n)], writes=[("xT", ft, n)])

        def proj_fm(wt, wres, col0, dst, dres, copy_eng="act", dpar=0, bank=None):
            wv = v3(wt, 8, 512)
            pcs = []
            for n in range(4):
                st_ = {}

                def half(h, n=n, st_=st_):
                    if h == 0:
                        st_["b"] = next_bank(0, 2) if bank is None else bank
                    b = st_["b"]
                    S.op("pe", lambda: mm_group(banks[b][:, :], [(wv[:, kc, col0:col0 + 128], hT[:, kc, CH(n)])
                                                                 for kc in range(4 * h, 4 * h + 4)], first=(h == 0), skip=True),
                         reads=[wres] + HT(n), writes=[BK(b)])

                def evac(n=n, st_=st_):
                    b = st_["b"]
                    if copy_eng == "act":
                        S.op("act", lambda: nc.scalar.copy(out=dst[:, CH(n)], in_=banks[b][:, :]),
                             reads=[BK(b)], writes=[(dres, dpar, n)])
                    else:
                        S.op("dve", lambda: nc.vector.tensor_copy(out=dst[:, CH(n)], in_=banks[b][:, :]),
                             reads=[BK(b)], writes=[(dres, dpar, n)])
                pcs += [lambda half=half: half(0), lambda half=half: half(1), evac]
            return pcs

        def proj_v(wt, wres, col0, ncol, vdst=None, dpar=0, bank=None, copy_eng="act"):
            wv = v3(wt, 8, 512)
            vdst = vp if vdst is None else vdst
            pcs = []
            for i4 in range(4):
                st_ = {}

                def half(h, i4=i4, st_=st_):
                    if h == 0:
                        st_["b"] = next_bank(0, 2) if bank is None else bank
                    b = st_["b"]

                    def f():
                        last = None
                        for ii in range(2 * h, 2 * h + 2):
                            i = i4 * 4 + ii
                            last = mm_group(banks[b][:, ii * 128:ii * 128 + ncol],
                                            [(hT[:, kc, i * 128:(i + 1) * 128], wv[:, kc, col0:col0 + ncol]) for kc in range(8)])
                        return last
                    S.op("pe", f, reads=[wres] + HT(i4), writes=[BK(b)])

                def evac(i4=i4, st_=st_):
                    b = st_["b"]
                    src = banks[b][:, :].rearrange("p (a c) -> p a c", a=4)[:, :, 0:ncol]
                    if copy_eng == "act":
                        S.op("act", lambda: nc.scalar.copy(out=vdst[:, i4 * 4:i4 * 4 + 4, 0:ncol], in_=src),
                             reads=[BK(b)], writes=[("vp", dpar)])
                    else:
                        S.op("dve", lambda: nc.vector.tensor_copy(out=vdst[:, i4 * 4:i4 * 4 + 4, 0:ncol], in_=src),
                             reads=[BK(b)], writes=[("vp", dpar)])
                pcs += [lambda half=half: half(0), lambda half=half: half(1), evac]
            return pcs

        def swap_cols(wt, wres):
            wv = v3(wt, 8, 512)
            src = wv[:, :, 0:128].rearrange("p k (h d) -> p k h d", h=2)
            dstv = wv[:, :, 128:256].rearrange("p k (h d) -> p k h d", h=2)
            for (a, b, n) in ((0, 8, 8), (8, 0, 8), (16, 16, 48)):
                S.op("act", lambda a=a, b=b, n=n: nc.scalar.copy(out=dstv[:, :, :, a:a + n], in_=src[:, :, :, b:b + n]),
                     reads=[wres], writes=[wres])

        def proj_rope(wt, wres, col0, dst, dres, dpar=0, bks=(0, 1), tmp=None, tmpn="gbuf"):
            wv = v3(wt, 8, 512)
            tmp = gbuf if tmp is None else tmp
            rp = special[:, 0:4096].rearrange("p (a t) -> p a t", a=2)
            pcs = []
            for n in range(4):
                def pa(n=n):
                    S.op("pe", lambda: mm_group(banks[bks[0]][:, :], [(wv[:, kc, col0:col0 + 128], hT[:, kc, CH(n)])
                                                                      for kc in range(8)]),
                         reads=[wres] + HT(n), writes=[BK(bks[0])])

                def pb(n=n):
                    S.op("pe", lambda: mm_group(banks[bks[1]][:, :], [(wv[:, kc, col0 + 128:col0 + 256], hT[:, kc, CH(n)])
                                                                      for kc in range(8)]),
                         reads=[wres] + HT(n), writes=[BK(bks[1])])

                def pc(n=n):
                    S.op("dve", lambda: nc.vector.tensor_tensor(out=tmp[:, 0, :], in0=banks[bks[0]][:, :], in1=rp[:, 0, CH(n)],
                                                                op=ALU.mult), reads=[BK(bks[0]), "special"], writes=[(tmpn, 0)])
                    S.op("dve", lambda: nc.vector.tensor_tensor(out=tmp[:, 1, :], in0=banks[bks[1]][:, :], in1=rp[:, 1, CH(n)],
                                                                op=ALU.mult), reads=[BK(bks[1]), "special"], writes=[(tmpn, 1)])
                    S.op("dve", lambda: nc.vector.tensor_tensor(out=dst[:, CH(n)], in0=tmp[:, 0, :], in1=tmp[:, 1, :],
                                                                op=ALU.add),
                         reads=[(tmpn, 0), (tmpn, 1)], writes=[(dres, dpar, n)])
                pcs += [pa, pb, pc]
            return pcs

        HP = (slice(0, 64), slice(64, 128))

        def zmm(i, t0, off, nq):
            def f():
                last = None
                for hh in range(2):
                    last = nc.tensor.matmul(banks[2 + hh][:, off:off + nq], lhsT=kT[HP[hh], i * 128:(i + 1) * 128],
                                            rhs=qT[HP[hh], t0 + off:t0 + off + nq], start=True, stop=True)
                return last
            return f

        def attn_sb(make_next):
            tiles = [(pp, cq, i) for pp in range(8) for cq in range(4) for i in range(4 * cq + 3, -1, -1)]
            NT = len(tiles)

            def ctx(n):
                pp = tiles[n][0]
                return (pp, pp % 2) + tuple(qkv_views(0, pp % 2))
            EBT = [ebuf, ebuf2, ebuf3]
            SPT = [spb, spb2]
            EB = [[("ebuf", 0), ("ebuf", 1)], [("eb2", 0), ("eb2", 1)], [("eb3", 0), ("eb3", 1)]]
            SP = [[("spb", 0), ("spb", 1)], [("sp2", 0), ("sp2", 1)]]
            GB = [("gbuf", 0), ("gbuf", 1)]
            WB = [("wb", 0), ("wb", 1)]
            ZB = [2, 0]

            def geom(n):
                _, cq, i = tiles[n]
                off = max(0, (i - 4 * cq) * 128)
                return cq, i, off, slice(off, 512)

            def P1(n):
                cq, i, off, cs = geom(n)
                p, par, qTv, kTv, vpv = ctx(n)
                zb = ZB[n % 2]

                def f():
                    last = None
                    for hh in range(2):
                        last = nc.tensor.matmul(banks[zb + hh][:, cs], lhsT=kTv[HP[hh], i * 128:(i + 1) * 128],
                                                rhs=qTv[HP[hh], cq * 512 + off:(cq + 1) * 512], start=True, stop=True)
                    return last
                S.op("pe", f, reads=[("qT", par, cq), ("kT", par, i // 4)], writes=[BK(zb), BK(zb + 1)])

            def S1(n):
                cq, i, off, cs = geom(n)
                zb = ZB[n % 2]
                e = EBT[n % 3]
                S.op("act", lambda: nc.scalar.activation(out=e[:, :, cs], in_=psum[:, zb:zb + 2, cs], func=AF.Exp, scale=0.125),
                     reads=[BK(zb), BK(zb + 1)], writes=EB[n % 3])
                if i >= 4 * cq:
                    S.op("dve", lambda: nc.vector.tensor_tensor(
                        out=e[:, :, off:off + 128], in0=e[:, :, off:off + 128],
                        in1=ltb.unsqueeze(1).to_broadcast([128, 2, 128]), op=ALU.mult),
                        reads=EB[n % 3] + ["cstb"], writes=EB[n % 3])

            def S2(n):
                cq, i, off, cs = geom(n)
                S.op("act", lambda: nc.scalar.activation(out=SPT[n % 2][:, :, cs], in_=EBT[n % 3][:, :, cs], func=AF.Ln,
                                                         bias=1.0, scale=1.0), reads=EB[n % 3], writes=SP[n % 2])

            def P2(n):
                cq, i, off, cs = geom(n)
                first = (i == 4 * cq + 3)

                def f():
                    last = None
                    for hh in range(2):
                        last = nc.tensor.matmul(banks[4 + hh][:, cs], lhsT=geb, rhs=SPT[n % 2][:, hh, cs], start=first,
                                                stop=True, skip_group_check=True)
                    return last
                S.op("pe", f, reads=SP[n % 2] + ["cstb"], writes=[BK(4), BK(5)])

            def S3(n):
                cq, i, off, cs = geom(n)
                S.op("act", lambda: nc.scalar.activation(out=gbuf[:, :, cs], in_=psum[:, 4:6, cs], func=AF.Exp, scale=-1.0),
                     reads=[BK(4), BK(5)], writes=GB)

            def D2(n):
                cq, i, off, cs = geom(n)
                S.op("dve", lambda: nc.vector.tensor_tensor(out=wb[:, :, cs], in0=EBT[n % 3][:, :, cs], in1=gbuf[:, :, cs],
                                                            op=ALU.mult), reads=EB[n % 3] + GB, writes=WB)

            def LTB(n):
                cq, i, off, cs = geom(n)
                if i > 0:
                    def f():
                        last = None
                        for hh in range(2):
                            last = nc.tensor.matmul(banks[4 + hh][:, cs], lhsT=ltb, rhs=SPT[n % 2][:, hh, cs], start=False,
                                                    stop=True, skip_group_check=True)
                        return last
                    S.op("pe", f, reads=SP[n % 2] + ["cstb"], writes=[BK(4), BK(5)])

            def PV(n):
                cq, i, off, cs = geom(n)
                p, par, qTv, kTv, vpv = ctx(n)
                first = (i == 4 * cq + 3)

                def f2():
                    last = None
                    for hh in range(2):
                        last = nc.tensor.matmul(banks[6][HP[hh], cs], lhsT=vpv[:, i, HP[hh]], rhs=wb[:, hh, cs], start=first,
                                                stop=True, skip_group_check=True)
                    return last
                S.op("pe", f2, reads=WB + [("vp", par)], writes=[BK(6)])
                if i == 0:
                    S.op("dve", lambda: nc.vector.tensor_copy(out=big[:, p, CH(cq)], in_=banks[6][:, :]),
                         reads=[BK(6)], writes=[("big", p)])

            pieces = []
            P1(0)
            S1(0)
            S2(0)
            P1(1)
            P2(0)
            for n in range(NT):
                if n % 40 == 0:
                    while pieces:
                        pieces.pop(0)()
                    pieces = list(make_next(n // 40 + 1))
                if n + 1 < NT:
                    S1(n + 1)
                S3(n)
                if n + 2 < NT:
                    P1(n + 2)
                D2(n)
                if n + 1 < NT:
                    S2(n + 1)
                LTB(n)
                if n + 1 < NT:
                    P2(n + 1)
                PV(n)
                if pieces:
                    pieces.pop(0)()
            while pieces:
                pieces.pop(0)()

        def attn_fox(p, par=0, pieces=()):
            pieces = list(pieces)
            qTv, kTv, vpv = qkv_views(2, par)
            fx = special[:, :]
            nend = fx[:, 256:512].rearrange("p (j h) -> p j h", j=16)
            ncum = fx[:, 512:768].rearrange("p (j h) -> p j h", j=16)
            nmid = fx[:, 1792:2048].rearrange("p (j h) -> p j h", j=16)
            Rb = special[0:64, 2048:3072].bitcast(BF16)
            for hh in range(2):
                h = 2 * p + hh
                r0 = 32 * hh
                S.op("dve", lambda h=h, r0=r0: nc.vector.tensor_scalar(
                    out=Rb[r0:r0 + 1, :].rearrange("p (j c) -> p j c", j=16),
                    in0=nmid[r0:r0 + 1, :, h:h + 1].to_broadcast([1, 16, 128]), scalar1=-8.0, scalar2=None, op0=ALU.mult),
                    reads=["special"], writes=[("Rb", hh)])
            tiles = [(cq, i) for cq in range(4) for i in range(0, 4 * cq + 4)]
            NT = len(tiles)
            WT = [wb, wb2]
            WB = [[("wb", 0), ("wb", 1)], [("wb2", 0), ("wb2", 1)]]
            ZB = [2, 0]

            def geom(n):
                cq, i = tiles[n]
                off = max(0, (i - 4 * cq) * 128)
                return cq, i, off, slice(off, 512)

            def P1(n):
                cq, i, off, cs = geom(n)
                zb = ZB[n % 2]

                def f():
                    last = None
                    for hh in range(2):
                        nc.tensor.matmul(banks[zb + hh][:, cs], lhsT=kTv[HP[hh], i * 128:(i + 1) * 128],
                                         rhs=qTv[HP[hh], cq * 512 + off:(cq + 1) * 512], start=True, stop=False,
                                         skip_group_check=True)
                    for hh in range(2):
                        r0 = 32 * hh
                        last = nc.tensor.matmul(banks[zb + hh][:, cs], lhsT=onesb[r0:r0 + 1, :],
                                                rhs=Rb[r0:r0 + 1, cq * 512 + off:(cq + 1) * 512], start=False, stop=True,
                                                skip_group_check=True)
                    return last
                S.op("pe", f, reads=[("qT", par, cq), ("kT", par, i // 4), ("Rb", 0), ("Rb", 1), "onesb"], writes=[BK(zb), BK(zb + 1)])

            def SE(n):
                cq, i, off, cs = geom(n)
                zb = ZB[n % 2]
                w_ = WT[n % 2]
                for hh in range(2):
                    h = 2 * p + hh
                    S.op("act", lambda hh=hh, h=h: nc.scalar.activation(out=w_[:, hh, cs], in_=banks[zb + hh][:, cs], func=AF.Exp,
                                                                        scale=0.125, bias=ncum[:, i, h:h + 1]),
                         reads=[BK(zb + hh), "special"], writes=[WB[n % 2][hh]])
                if i >= 4 * cq:
                    S.op("dve", lambda: nc.vector.tensor_tensor(
                        out=w_[:, :, off:off + 128], in0=w_[:, :, off:off + 128],
                        in1=leb.unsqueeze(1).to_broadcast([128, 2, 128]), op=ALU.mult),
                        reads=WB[n % 2] + ["cstb"], writes=WB[n % 2])

            def P3(n):
                cq, i, off, cs = geom(n)
                first = (i == 0)
                w_ = WT[n % 2]

                def f():
                    last = None
                    for hh in range(2):
                        last = nc.tensor.matmul(banks[6][HP[hh], cs], lhsT=vpv[:, i, HP[hh]], rhs=w_[:, hh, cs], start=first,
                                                stop=True, skip_group_check=True)
                    for hh in range(2):
                        last = nc.tensor.matmul(banks[4][HP[hh], cs], lhsT=onesb[:, HP[hh]], rhs=w_[:, hh, cs], start=first,
                                                stop=True, skip_group_check=True)
                    return last
                S.op("pe", f, reads=WB[n % 2] + [("vp", par), "onesb"], writes=[BK(6), BK(4)])
                if i == 4 * cq + 3:
                    normalize(p, cq, None)

            P1(0)
            SE(0)
            for n in range(NT):
                if n + 1 < NT:
                    P1(n + 1)
                    SE(n + 1)
                P3(n)
                if pieces:
                    pieces.pop(0)()
            while pieces:
                pieces.pop(0)()

        def normalize(p, cq, sink_cols):
            if sink_cols is None:
                S.op("act", lambda: nc.scalar.activation(out=gbuf[:, 0, :], in_=banks[4][:, :], func=AF.Ln),
                     reads=[BK(4)], writes=[("gbuf", 0)])
            else:
                S.op("act", lambda: nc.scalar.activation(out=gbuf[:, 0, :], in_=banks[4][:, :], func=AF.Ln,
                                                         bias=small[:, 16 + p:17 + p], scale=1.0),
                     reads=[BK(4), "small"], writes=[("gbuf", 0)])
            S.op("dve", lambda: nc.vector.tensor_copy(out=gbuf[:, 1, :], in_=banks[6][:, :]), reads=[BK(6)], writes=[("gbuf", 1)])
            S.op("act", lambda: nc.scalar.activation(out=gbuf[:, 0, :], in_=gbuf[:, 0, :], func=AF.Exp, scale=-1.0),
                 reads=[("gbuf", 0)], writes=[("gbuf", 0)])
            S.op("dve", lambda: nc.vector.tensor_tensor(out=big[:, p, CH(cq)], in0=gbuf[:, 1, :], in1=gbuf[:, 0, :], op=ALU.mult),
                 reads=[("gbuf", 1), ("gbuf", 0)], writes=[("big", p)])

        def attn_swa(p, par=0, pieces=()):
            pieces = list(pieces)
            qTv = qT if par == 0 else vp2[:, :, :].rearrange("p a b -> p (a b)")
            WT = [wb, wb2]
            WB = [[("wb", 0), ("wb", 1)], [("wb2", 0), ("wb2", 1)]]
            ZB = [2, 0]

            def P1(i):
                nq = 256 if i < 15 else 128
                zb = ZB[i % 2]

                def f():
                    last = None
                    for hh in range(2):
                        last = nc.tensor.matmul(banks[zb + hh][:, 0:nq], lhsT=kT[HP[hh], i * 128:(i + 1) * 128],
                                                rhs=qTv[HP[hh], i * 128:i * 128 + nq], start=True, stop=True)
                    return last
                S.op("pe", f, reads=[("qT", par, n) for n in {i // 4, (i * 128 + nq - 1) // 512}] + [("kT", 0, i // 4)],
                     writes=[BK(zb), BK(zb + 1)])

            def SE(i):
                nq = 256 if i < 15 else 128
                zb = ZB[i % 2]
                w_ = WT[i % 2]
                S.op("act", lambda: nc.scalar.activation(out=w_[:, :, 0:nq], in_=psum[:, zb:zb + 2, 0:nq], func=AF.Exp, scale=0.125),
                     reads=[BK(zb), BK(zb + 1)], writes=WB[i % 2])
                S.op("dve", lambda: nc.vector.tensor_tensor(out=w_[:, :, 0:nq], in0=w_[:, :, 0:nq],
                                                            in1=bandb[:, 0:nq].unsqueeze(1).to_broadcast([128, 2, nq]), op=ALU.mult),
                     reads=WB[i % 2] + ["cstb"], writes=WB[i % 2])

            def pv(i, e0, c0, n, first):
                w_ = WT[i % 2]

                def f():
                    last = None
                    for hh in range(2):
                        last = nc.tensor.matmul(banks[6][HP[hh], c0:c0 + n], lhsT=vp[:, i, 0:64], rhs=w_[:, hh, e0:e0 + n],
                                                start=first, stop=True, skip_group_check=True)
                    for hh in range(2):
                        last = nc.tensor.matmul(banks[4][HP[hh], c0:c0 + n], lhsT=onesb[:, HP[hh]], rhs=w_[:, hh, e0:e0 + n],
                                                start=first, stop=True, skip_group_check=True)
                    return last
                S.op("pe", f, reads=WB[i % 2] + [("vp", 0), "onesb"], writes=[BK(6), BK(4)])

            def P3(i):
                cq = i // 4
                c0 = (i % 4) * 128
                if i % 4 != 3:
                    pv(i, 0, c0, 256, i == 0)
                else:
                    pv(i, 0, c0, 128, False)
                    normalize(p, cq, True)
                    if i < 15:
                        pv(i, 128, 0, 128, True)

            P1(0)
            SE(0)
            for i in range(16):
                if i + 1 < 16:
                    P1(i + 1)
                    SE(i + 1)
                P3(i)
                if pieces:
                    pieces.pop(0)()
            while pieces:
                pieces.pop(0)()

        def fox_prep(wt, wres):
            wv = v3(wt, 8, 512)
            fx = special[:, :]
            nlf = fx[:, 0:256]

            def f():
                last = None
                for i in range(16):
                    last = mm_group(banks[2][:, i * 16:(i + 1) * 16],
                                    [(hT[:, kc, i * 128:(i + 1) * 128], wv[:, kc, 0:16]) for kc in range(8)])
                    last = nc.tensor.matmul(banks[2][:, i * 16:(i + 1) * 16], lhsT=onesf[0:1, :], rhs=rowst[0:1, 16:32],
                                            start=False, stop=True, skip_group_check=True)
                return last
            S.op("pe", f, reads=[wres, "rowst1", "onesf"] + [("hT", c, n) for c in range(8) for n in range(4)],
                 writes=[BK(2)])
            S.op("act", lambda: nc.scalar.activation(out=nlf, in_=banks[2][:, 0:256], func=AF.Exp, scale=-1.0),
                 reads=[BK(2)], writes=["special"])
            S.op("act", lambda: nc.scalar.activation(out=nlf, in_=nlf, func=AF.Ln, bias=1.0, scale=1.0),
                 reads=["special"], writes=["special"])

            def f2():
                last = None
                for j in range(16):
                    for i2 in range(j + 1):
                        last = nc.tensor.matmul(banks[3][:, j * 16:(j + 1) * 16], lhsT=onesf[:, :],
                                                rhs=nlf[:, i2 * 16:(i2 + 1) * 16], start=(i2 == 0), stop=(i2 == j))
                for i in range(16):
                    last = nc.tensor.matmul(banks[4][:, i * 16:(i + 1) * 16], lhsT=lef, rhs=nlf[:, i * 16:(i + 1) * 16],
                                            start=True, stop=True)
                return last
            S.op("pe", f2, reads=["special", "onesf", "cstf"], writes=[BK(3), BK(4)])
            S.op("dve", lambda: nc.vector.tensor_copy(out=fx[:, 256:512], in_=banks[3][:, 0:256]), reads=[BK(3)],
                 writes=["special"])
            S.op("dve", lambda: nc.vector.tensor_copy(out=fx[:, 512:528], in_=banks[4][:, 0:16]), reads=[BK(4)],
                 writes=["special"])
            S.op("dve", lambda: nc.vector.tensor_tensor(out=fx[:, 528:768], in0=banks[4][:, 16:256], in1=fx[:, 256:496],
                                                        op=ALU.add), reads=[BK(4), "special"], writes=["special"])
            S.op("dve", lambda: nc.vector.tensor_scalar(out=fx[:, 1792:1808], in0=fx[:, 256:272], scalar1=0.5, scalar2=None,
                                                        op0=ALU.mult), reads=["special"], writes=["special"])
            S.op("dve", lambda: nc.vector.tensor_tensor(out=fx[:, 1808:2048], in0=fx[:, 256:496], in1=fx[:, 272:512],
                                                        op=ALU.add), reads=["special"], writes=["special"])
            S.op("dve", lambda: nc.vector.tensor_scalar(out=fx[:, 1808:2048], in0=fx[:, 1808:2048], scalar1=0.5, scalar2=None,
                                                        op0=ALU.mult), reads=["special"], writes=["special"])

        def dump_bf(src):
            for c in range(8):
                for n in range(4):
                    S.op("dve", lambda c=c, n=n: nc.vector.tensor_copy(out=xT[:, c, CH(n)], in_=src[:, c, CH(n)]),
                         reads=[("hT", c, n), ("big", c), ("qT", 0, n), ("kT", 0, n), ("vp", 0)], writes=[("xT", c, n)])
            raise _Stop()

        def layer(l, prenormed, next_norm):
            k = KINDS[l]
            if not prenormed:
                norm(3 * l + 0)
            if dbg == "h":
                dump_bf(hT)
            if k == 1:
                rp = special[:, 0:4096].rearrange("p (a t) -> p a t", a=2)
                S.dma("sp", S.new_dma_sem("d_rope"), rp, rope_in, writes=["special"])
                S.op("pe", lambda: nc.tensor.matmul(banks[0][:, 0:16], lhsT=onesf[0:1, :], rhs=rowst[0:1, 0:16],
                                                    start=True, stop=True), reads=["onesf", "rowst0"], writes=[BK(0)])
                S.op("act", lambda: nc.scalar.activation(out=small[:, 0:16], in_=banks[0][:, 0:16], func=AF.Exp),
                     reads=[BK(0)], writes=["small"])
                for hh in range(2):
                    S.op("dve", lambda hh=hh: nc.vector.tensor_copy(
                        out=small[HP[hh], 16:24], in_=small[HP[hh], 0:16].rearrange("p (a b) -> p a b", b=2)[:, :, hh]),
                        reads=["small"], writes=["small"])
            if k == 2:
                wt, wres = ws_next()
                fox_prep(wt, wres)
            def make_pieces(pp, par, overlapped):
                wt, wres = ws_next()
                qd, kd, vd = qkv_views(k, par)
                bank = 7 if overlapped else None
                eng = "dve" if overlapped else "act"
                return (proj_fm(wt, wres, 0, qd, "qT", copy_eng=eng, dpar=par, bank=bank)
                        + proj_fm(wt, wres, 128, kd, "kT", copy_eng="dve", dpar=par, bank=bank)
                        + proj_v(wt, wres, 256, 128, vdst=vd, dpar=par, bank=bank, copy_eng=eng))

            if k in (0, 2):
                for pc in make_pieces(0, 0, False):
                    pc()
            if k == 0:
                attn_sb(lambda pp: make_pieces(pp, pp % 2, True) if pp < 8 else [])
            for p in range(8):
                if k == 0:
                    break
                if k == 2:
                    nxt = make_pieces(p + 1, (p + 1) % 2, True) if p + 1 < 8 else []
                    attn_fox(p, p % 2, nxt)
                    continue
                if True:
                    if p % 4 == 0:
                        wt, wres = ws_next()
                        swap_cols(wt, wres)
                        for pc in proj_rope(wt, wres, 0, kT, "kT") + proj_v(wt, wres, 256, 64):
                            pc()
                    if p == 0:
                        wt, wres = ws_next()
                        swap_cols(wt, wres)
                        for pc in proj_rope(wt, wres, 0, qT, "qT"):
                            pc()
                    nxt = []
                    if p + 1 < 8:
                        wt, wres = ws_next()
                        swap_cols(wt, wres)
                        npar = (p + 1) % 2
                        qd = qT if npar == 0 else vp2[:, :, :].rearrange("p a b -> p (a b)")
                        nxt = proj_rope(wt, wres, 0, qd, "qT", dpar=npar, bks=(5, 7), tmp=ebuf, tmpn="ebuf")
                    attn_swa(p, p % 2, nxt)
            if dbg == "a":
                dump_bf(big)
            for hf in range(2):
                wt, wres = ws_next()
                wv = v3(wt, 8, 512)
                order = [(ftl, n) for ftl in range(4) for n in range(4)] if hf == 0 else \
                        [(ftl, n) for n in range(4) for ftl in range(4)]
                for (ftl, n) in order:
                    ft = hf * 4 + ftl
                    b = next_bank()
                    S.op("pe", lambda b=b, n=n, ftl=ftl, wv=wv: mm_group(
                        banks[b][:, :], [(wv[:, kc, ftl * 128:(ftl + 1) * 128], big[:, kc, CH(n)]) for kc in range(8)]),
                        reads=[wres] + [("big", kc) for kc in range(8)], writes=[BK(b)])
                    resid_add(ft, n, b)
                    if hf == 1 and ftl == 3 and n >= 1 and dbg is None:
                        norm_chunk(3 * l + 1, n - 1)
            if dbg is None:
                norm_chunk(3 * l + 1, 3)
            if dbg == "xa":
                raise _Stop()
            pT = qk
            PST = [wb[:, :, :].rearrange("p a b -> p (a b)").bitcast(F32), wb2[:, :, :].rearrange("p a b -> p (a b)").bitcast(F32)]
            PSR = [[("wb", 0), ("wb", 1)], [("wb2", 0), ("wb2", 1)]]
            def p_dma(i):
                S.dma("sp", d_st[i % 2], PST[i % 2][:, 0:256], p_in[l, i * 128:(i + 1) * 128, :], writes=PSR[i % 2])

            def p_xpose(i):
                stg = PST[i % 2]

                def f():
                    last = None
                    for kc in range(2):
                        last = nc.tensor.transpose(banks[7][:, kc * 128:(kc + 1) * 128], stg[:, kc * 128:(kc + 1) * 128], identf)
                    return last
                S.op("pe", f, reads=PSR[i % 2] + ["cstf"], writes=[BK(7)])
                src = banks[7][:, 0:256].rearrange("p (a t) -> p a t", a=2)
                S.op("act", lambda: nc.scalar.copy(out=pT[:, :, i * 128:(i + 1) * 128], in_=src),
                     reads=[BK(7)], writes=[("qT", 0, i // 4), ("kT", 0, i // 4)])
            p_dma(0)
            p_dma(1)
            if dbg is not None:
                norm(3 * l + 1)
            for g in range(8):
                wt, wres = ws_next()
                wv = v3(wt, 8, 512)
                ub = g % 2
                uT = big[:, ub * 4:(ub + 1) * 4, :]
                for ftl in range(4):
                    for n in range(4):
                        b = next_bank()
                        S.op("pe", lambda b=b, n=n, ftl=ftl, wv=wv: mm_group(
                            banks[b][:, :], [(wv[:, kc, ftl * 128:(ftl + 1) * 128], hT[:, kc, CH(n)]) for kc in range(8)]),
                            reads=[wres] + HT(n), writes=[BK(b)])
                        tb = ebuf if (ftl * 4 + n) % 2 == 0 else gbuf
                        tr = "ebuf" if (ftl * 4 + n) % 2 == 0 else "gbuf"
                        S.op("act", lambda b=b, tb=tb: nc.scalar.activation(out=tb[:, 0, :], in_=banks[b][:, :], func=AF.Relu),
                             reads=[BK(b)], writes=[(tr, 0)])
                        S.op("dve", lambda tb=tb, n=n, ftl=ftl, uT=uT: nc.vector.tensor_tensor(
                            out=uT[:, ftl, CH(n)], in0=tb[:, 0, :], in1=tb[:, 0, :], op=ALU.mult),
                            reads=[(tr, 0)], writes=[("big", ub * 4 + ftl)])
                for i in (2 * g, 2 * g + 1):
                    p_xpose(i)
                    if i + 2 < 16:
                        p_dma(i + 2)
                wt, wres = ws_next()
                wv = v3(wt, 4, 1024)
                order = [(ft, n) for ft in range(8) for n in range(4)] if g < 7 else \
                        [(ft, n) for n in range(4) for ft in range(8)]
                for (ft, n) in order:
                    b = next_bank()
                    S.op("pe", lambda b=b, n=n, ft=ft, wv=wv, uT=uT: mm_group(
                        banks[b][:, :], [(wv[:, kc, ft * 128:(ft + 1) * 128], uT[:, kc, CH(n)]) for kc in range(4)]),
                        reads=[wres] + [("big", ub * 4 + kc) for kc in range(4)], writes=[BK(b)])
                    resid_add(ft, n, b)
                    if g == 7 and ft == 7 and n >= 1 and dbg is None:
                        norm_chunk(3 * l + 2, n - 1)
            if dbg is None:
                norm_chunk(3 * l + 2, 3)
            if dbg == "xm":
                raise _Stop()
            pT = qk
            if dbg is not None:
                norm(3 * l + 2)
            wgs = [ws_next(), ws_next(hold=True)]
            wpt, wpres = ws_next(hold=True)
            wpv = v3(wpt, 2, 1024)
            gtmp = ebuf[:, 1, :]
            for n in range(4):
                for ft in range(8):
                    wt, wres = wgs[ft // 4]
                    wv = v3(wt, 8, 512)
                    ftl = ft % 4
                    bg = next_bank()
                    S.op("pe", lambda bg=bg, n=n, ftl=ftl, wv=wv: mm_group(
                        banks[bg][:, :], [(wv[:, kc, ftl * 128:(ftl + 1) * 128], hT[:, kc, CH(n)]) for kc in range(8)]),
                        reads=[wres] + HT(n), writes=[BK(bg)])
                    bp = next_bank()
                    S.op("pe", lambda bp=bp, n=n, ft=ft: mm_group(
                        banks[bp][:, :], [(wpv[:, kc, ft * 128:(ft + 1) * 128], pT[:, kc, CH(n)]) for kc in range(2)]),
                        reads=[wpres, ("qT", 0, n), ("kT", 0, n)], writes=[BK(bp)])
                    S.op("act", lambda bg=bg: nc.scalar.activation(out=gtmp, in_=banks[bg][:, :], func=AF.Sigmoid),
                         reads=[BK(bg)], writes=[("ebuf", 1)])
                    S.op("dve", lambda bp=bp: nc.vector.tensor_tensor(out=gtmp, in0=banks[bp][:, :], in1=gtmp,
                                                                      op=ALU.mult), reads=[BK(bp), ("ebuf", 1)], writes=[("ebuf", 1)])
                    S.op("dve", lambda n=n, ft=ft: nc.vector.tensor_tensor(out=xT[:, ft, CH(n)], in0=gtmp,
                                                                           in1=xT[:, ft, CH(n)], op=ALU.add),
                         reads=[("ebuf", 1), ("xT", ft, n)], writes=[("xT", ft, n)])
                if next_norm is not None and n >= 1 and dbg is None:
                    norm_chunk(next_norm[0], n - 1, next_norm[1])
            if next_norm is not None and dbg is None:
                norm_chunk(next_norm[0], 3, next_norm[1])

        def alias_sync(names_from, names_to):
            for a in names_from:
                for b in names_to:
                    if a in S.wr:
                        cur = S.wr.get(b)
                        S.rd.setdefault(b, {})
                        k, v = S.wr[a]
                        S.rd[b][k] = max(S.rd[b].get(k, 0), v)
                    for k, v in S.rd.get(a, {}).items():
                        S.rd.setdefault(b, {})
                        S.rd[b][k] = max(S.rd[b].get(k, 0), v)

        EG = [("ebuf", 0), ("ebuf", 1), ("gbuf", 0), ("gbuf", 1)]
        STG = [("stage", 0), ("stage", 1), ("stage", 0, 0), ("stage", 0, 1), ("stage", 1, 0), ("stage", 1, 1)]
        load_x()
        try:
            for idx, l in enumerate(layers):
                if dbg is not None:
                    nxt = None
                elif idx + 1 < len(layers):
                    nxt = (3 * layers[idx + 1], False)
                else:
                    nxt = (12, True) if final else None
                layer(l, prenormed=(idx > 0 and dbg is None), next_norm=nxt)
        except _Stop:
            pass
        if final and (not layers or dbg is not None):
            norm(12, inplace=True)
        store_x()
        S.wait_all("sp", d_out)
    return nc


_CONSTS = None


def _run(layers, final, x, inputs):
    global _CONSTS
    if _CONSTS is None:
        _CONSTS = host_consts()
    cst, rope = _CONSTS
    nc = build(layers, final)
    gl = []
    for l in range(DEPTH):
        gl += [inputs[f"attn_norm_{l}"], inputs[f"mlp_norm_{l}"], inputs[f"ple_norm_{l}"]]
    gl.append(inputs["final_norm"])
    gains = np.ascontiguousarray(np.stack(gl).astype(np.float32).reshape(13 * 8, 128))
    shared = dict(cst=cst, rope=rope, gains=gains,
                  sinks=np.ascontiguousarray(inputs["sinks_1"].reshape(1, 16)),
                  bfor=np.ascontiguousarray(inputs["b_forget_2"].reshape(1, 16)))
    for l in layers:
        shared[f"w_in_{l}"] = inputs[f"w_in_{l}"]
        shared[f"w_out_{l}"] = inputs[f"w_out_{l}"]
        shared[f"w_up_{l}"] = inputs[f"w_up_{l}"]
        shared[f"w_down_{l}"] = inputs[f"w_down_{l}"]
        shared[f"w_ple_gate_{l}"] = inputs[f"w_ple_gate_{l}"]
        shared[f"w_ple_proj_{l}"] = inputs[f"w_ple_proj_{l}"]
    p = inputs["p"]
    in_maps = []
    for c in range(NCORES):
        m = dict(shared)
        m["x"] = np.ascontiguousarray(x[c])
        m["p"] = np.ascontiguousarray(p[:, c])
        in_maps.append(m)
    res = run_bass_kernel_spmd(nc, in_maps, core_ids=list(range(NCORES)))
    return np.stack([np.asarray(r["y"]) for r in res.results], axis=0).astype(np.float32)


def kernel(**inputs):
    inputs = {k: np.asarray(v) for k, v in inputs.items()}
    x = np.ascontiguousarray(inputs["x"], dtype=np.float32)
    if FUSED:
        return _run(list(range(DEPTH)), True, x, inputs)
    for l in range(DEPTH):
        x = _run([l], l == DEPTH - 1, x, inputs)
    return x
```

```python
import numpy as np
from contextlib import ExitStack
import concourse.bass as bass
import concourse.mybir as mybir
from concourse.bass_utils import run_bass_kernel_spmd

F32 = mybir.dt.float32
BF16 = mybir.dt.bfloat16
AF = mybir.ActivationFunctionType
ALU = mybir.AluOpType

S_LEN = 2048
D = 1024
DEPTH = 4
NCORES = 8
KINDS = (0, 1, 2, 0)
EPS = 1e-6
FUSED = True
DBG_I = 3


class Sched:
    def __init__(self, nc, stack):
        self.nc = nc
        self.stack = stack
        self.engs = {"pe": nc.tensor, "act": nc.scalar, "dve": nc.vector, "pool": nc.gpsimd, "sp": nc.sync}
        self.sem = {k: stack.enter_context(nc.semaphore("s_" + k)) for k in self.engs}
        self.cnt = {k: 0 for k in self.engs}
        self.known = {k: {} for k in self.engs}
        self.wr = {}
        self.rd = {}
        self.semobj = dict(self.sem)
        self.dmacnt = {}

    def new_dma_sem(self, name):
        s = self.stack.enter_context(self.nc.semaphore(name))
        self.semobj[name] = s
        self.dmacnt[name] = 0
        return name

    def _need(self, eng, reads, writes):
        need = {}

        def add(c):
            if c is None:
                return
            k, v = c
            if need.get(k, 0) < v:
                need[k] = v
        for r in reads:
            add(self.wr.get(r))
        for r in writes:
            add(self.wr.get(r))
            for k, v in self.rd.get(r, {}).items():
                add((k, v))
        for k, v in need.items():
            if k == eng and eng == "pe":
                continue
            if self.known[eng].get(k, 0) >= v:
                continue
            self.engs[eng].wait_ge(self.semobj[k], v)
            self.known[eng][k] = v

    def _mark(self, clock, reads, writes):
        k, v = clock
        for r in reads:
            self.rd.setdefault(r, {})[k] = v
        for r in writes:
            self.wr[r] = clock
            self.rd[r] = {}

    def op(self, eng, fn, reads=(), writes=()):
        self._need(eng, reads, writes)
        inst = fn()
        self.cnt[eng] += 1
        inst.then_inc(self.sem[eng], 1)
        self._mark((eng, self.cnt[eng]), reads, writes)
        return inst

    def dma(self, queue, semname, out, in_, reads=(), writes=(), **kw):
        self._need(queue, reads, writes)
        inst = self.engs[queue].dma_start(out=out, in_=in_, **kw)
        self.dmacnt[semname] += 16
        inst.then_inc(self.semobj[semname], 16)
        self._mark((semname, self.dmacnt[semname]), reads, writes)
        return inst

    def wait_all(self, eng, semnames):
        for s in semnames:
            if self.dmacnt[s] > 0:
                self.engs[eng].wait_ge(self.semobj[s], self.dmacnt[s])


def host_consts():
    a = np.arange(128)
    ident = np.eye(128, dtype=np.float32)
    ge = (a[:, None] >= a[None, :]).astype(np.float32)
    lt = (a[:, None] < a[None, :]).astype(np.float32)
    le = (a[:, None] <= a[None, :]).astype(np.float32)
    f = np.arange(256)
    band = ((f[None, :] - a[:, None] >= 0) & (f[None, :] - a[:, None] < 128)).astype(np.float32)
    cst = np.concatenate([ident, ge, lt, le, band], axis=1)
    half = 8
    inv_freq = (np.float32(500000.0) ** (-np.arange(half, dtype=np.float32) / np.float32(half))).astype(np.float32)
    ang = np.arange(S_LEN, dtype=np.float32)[:, None] * inv_freq[None, :]
    cos = np.cos(ang).astype(np.float32).T
    sin = np.sin(ang).astype(np.float32).T
    rope = np.zeros((128, 2, S_LEN), np.float32)
    rope[:, 0, :] = 1.0
    for hh in range(2):
        b = 64 * hh
        rope[b:b + 8, 0] = cos
        rope[b + 8:b + 16, 0] = cos
        rope[b:b + 8, 1] = -sin
        rope[b + 8:b + 16, 1] = sin
    return cst, rope


class _Stop(Exception):
    pass


def build(layers, final, dbg=None):
    nc = bass.Bass("TRN2", target_bir_lowering=False)
    dt_in = lambda n, s: nc.dram_tensor(n, list(s), F32, kind="ExternalInput").ap()
    x_in = dt_in("x", (S_LEN, D))
    p_in = dt_in("p", (DEPTH, S_LEN, 256))
    cst_in = dt_in("cst", (128, 768))
    rope_in = dt_in("rope", (128, 2, S_LEN))
    gains_in = dt_in("gains", (104, 128))
    sinks_in = dt_in("sinks", (1, 16))
    bfor_in = dt_in("bfor", (1, 16))
    W = {}
    for l in layers:
        k = KINDS[l]
        W[l] = dict(
            w_in=dt_in(f"w_in_{l}", (D, (3072, 1280, 3088)[k])),
            w_out=dt_in(f"w_out_{l}", (D, D)),
            w_up=dt_in(f"w_up_{l}", (D, 4096)),
            w_down=dt_in(f"w_down_{l}", (4096, D)),
            w_gate=dt_in(f"w_ple_gate_{l}", (D, D)),
            w_proj=dt_in(f"w_ple_proj_{l}", (256, D)),
        )
    y_out = nc.dram_tensor("y", [S_LEN, D], F32, kind="ExternalOutput").ap()

    with ExitStack() as st:
        S = Sched(nc, st)
        sb = lambda n, s, d: st.enter_context(nc.sbuf_tensor("sb_" + n, list(s), d))
        xT = sb("xT", (128, 8, S_LEN), F32)
        hT = sb("hT", (128, 8, S_LEN), BF16)
        big = sb("big", (128, 8, S_LEN), BF16)
        vp = sb("vp", (128, 16, 128), BF16)
        qk = sb("qk", (128, 2, S_LEN), BF16)
        wsl = [sb(f"ws{i}", (128, 4096), BF16) for i in range(3)]
        ebuf = sb("ebuf", (128, 2, 512), F32)
        gbuf = sb("gbuf", (128, 2, 512), F32)
        spb = sb("spb", (128, 2, 512), BF16)
        wb = sb("wb", (128, 2, 512), BF16)
        special = sb("special", (128, 4608), F32)
        cstf = sb("cstf", (128, 768), F32)
        cstb = sb("cstb", (128, 768), BF16)
        gains = sb("gains", (128, 104), F32)
        gstage = sb("gstage", (104, 128), F32)
        onesf = sb("onesf", (128, 128), F32)
        onesb = sb("onesb", (128, 128), BF16)
        small = sb("small", (128, 64), F32)
        rowst = sb("rowst", (1, 32), F32)
        psum = st.enter_context(nc.psum_tensor("psum", [128, 8, 512], F32))
        banks = [psum[:, i, :] for i in range(8)]
        wb2 = sb("wb2", (128, 2, 512), BF16)
        rstd = gbuf[:, 1, :]
        vp2 = sb("vp2", (128, 16, 128), BF16)
        ebuf3 = special[:, 1536:2560].rearrange("p (a b) -> p a b", a=2)
        ebuf2 = special[:, 0:1024].rearrange("p (a b) -> p a b", a=2)
        spb2 = special[:, 1024:1536].bitcast(BF16).rearrange("p (a b) -> p a b", a=2)
        BK = lambda i: ("bank", i)

        identf = cstf[:, 0:128]
        lef = cstf[:, 384:512]
        geb, ltb, leb, bandb = cstb[:, 128:256], cstb[:, 256:384], cstb[:, 384:512], cstb[:, 512:768]
        stage = [ebuf, gbuf]
        qT, kT = qk[:, 0, :], qk[:, 1, :]
        QKV = {"cur": 0}
        RBV = [special[0:64, 2048:3072].bitcast(BF16),
               ebuf[0:64, :, :].rearrange("p a b -> p (a b)").bitcast(BF16)]

        def qkv_views(kind, par):
            if par == 0:
                return qk[:, 0, :], qk[:, 1, :], vp
            if kind == 0:
                a, b = 2560, 3584
            else:
                a, b = 768, 3072
            return (special[:, a:a + 1024].bitcast(BF16), special[:, b:b + 1024].bitcast(BF16), vp2)
        CH = lambda n: slice(n * 512, (n + 1) * 512)

        d_st = [S.new_dma_sem("d_st0"), S.new_dma_sem("d_st1")]
        d_out = [S.new_dma_sem("d_out0"), S.new_dma_sem("d_out1")]
        d_w = [S.new_dma_sem(f"d_w{i}") for i in range(3)]

        S.dma("sp", S.new_dma_sem("d_c0"), cstf[:, :], cst_in, writes=["cstf"])
        S.dma("sp", S.new_dma_sem("d_c1"), gstage[:, :], gains_in, writes=["gstage"])
        S.dma("sp", S.new_dma_sem("d_c2"), rowst[:, 0:16], sinks_in, writes=["rowst0"])
        S.dma("sp", S.new_dma_sem("d_c3"), rowst[:, 16:32], bfor_in, writes=["rowst1"])
        S.op("dve", lambda: nc.vector.memset(onesf[:, :], 1.0), writes=["onesf"])
        S.op("dve", lambda: nc.vector.memset(onesb[:, :], 1.0), writes=["onesb"])
        S.op("dve", lambda: nc.vector.tensor_copy(out=cstb[:, :], in_=cstf[:, :]), reads=["cstf"], writes=["cstb"])
        S.op("pe", lambda: nc.tensor.transpose(banks[7][:, 0:104], gstage[:, :], identf[0:104, 0:104]),
             reads=["gstage", "cstf"], writes=[BK(7)])
        S.op("act", lambda: nc.scalar.copy(out=gains[:, :], in_=banks[7][:, 0:104]), reads=[BK(7)], writes=["gains"])

        blocks = []

        class WS:
            issued = 0
            cur = -1

        def ws_issue(upto):
            while WS.issued <= min(upto, len(blocks) - 1):
                b = WS.issued
                slot = b % 3
                for (dstfn, src) in blocks[b]:
                    S.dma("pool", d_w[slot], dstfn(wsl[slot]), src, writes=[("ws", slot)])
                WS.issued += 1

        def ws_next(hold=False):
            WS.cur += 1
            ws_issue(WS.cur if hold else WS.cur + 2)
            return wsl[WS.cur % 3], ("ws", WS.cur % 3)

        def v3(t, a, b):
            return t[:, 0:a * b].rearrange("p (a b) -> p a b", a=a)

        def wview(wd, c0, ncols):
            return wd.rearrange("(kc p) n -> p kc n", p=128)[:, :, c0:c0 + ncols]

        for l in layers:
            k = KINDS[l]
            wd = W[l]
            if k == 2:
                blocks.append([(lambda t: v3(t, 8, 512)[:, :, 0:16], wview(wd["w_in"], 3072, 16))])
            for p in range(8):
                if k in (0, 2):
                    blocks.append([
                        ((lambda t, o=o: v3(t, 8, 512)[:, :, o * 128:(o + 1) * 128]),
                         wview(wd["w_in"], o * 1024 + p * 128, 128)) for o in range(3)])
                else:
                    def kv_block(g):
                        kc0 = 1024 + 64 * g
                        ent = [(lambda t, o=o: v3(t, 8, 512)[:, :, o:o + 64], wview(wd["w_in"], kc0, 64)) for o in (0, 64)]
                        ent.append((lambda t: v3(t, 8, 512)[:, :, 256:320], wview(wd["w_in"], 1152 + 64 * g, 64)))
                        blocks.append(ent)
                    if p == 0:
                        kv_block(0)
                    blocks.append([(lambda t: v3(t, 8, 512)[:, :, 0:128], wview(wd["w_in"], p * 128, 128))])
                    if p == 4:
                        kv_block(1)
            for hf in range(2):
                blocks.append([(lambda t: v3(t, 8, 512), wview(wd["w_out"], hf * 512, 512))])
            for g in range(8):
                blocks.append([(lambda t: v3(t, 8, 512), wview(wd["w_up"], g * 512, 512))])
                blocks.append([(lambda t: v3(t, 4, 1024),
                               wd["w_down"][g * 512:(g + 1) * 512, :].rearrange("(kc p) n -> p kc n", p=128))])
            for hf in range(2):
                blocks.append([(lambda t: v3(t, 8, 512), wview(wd["w_gate"], hf * 512, 512))])
            blocks.append([(lambda t: v3(t, 2, 1024), wd["w_proj"].rearrange("(kc p) n -> p kc n", p=128))])

        def mm_group(bank_ap, pairs, first=True, skip=False):
            last = None
            n = len(pairs)
            for i, (l_, r_) in enumerate(pairs):
                last = nc.tensor.matmul(bank_ap, lhsT=l_, rhs=r_, start=(first and i == 0), stop=(i == n - 1),
                                        skip_group_check=skip)
            return last

        HT = lambda n: [("hT", c, n) for c in range(8)]
        bank_rr = [0]

        def next_bank(lo=0, hi=4):
            b = lo + bank_rr[0] % (hi - lo)
            bank_rr[0] += 1
            return b

        SR = lambda j: [("ebuf" if j == 0 else "gbuf", 0), ("ebuf" if j == 0 else "gbuf", 1)]

        def load_x():
            for i in range(16):
                stg = stage[i % 2]
                sv = stg[:, :, :].rearrange("p a b -> p (a b)")
                S.dma("sp", d_st[i % 2], sv, x_in[i * 128:(i + 1) * 128, :], writes=SR(i % 2))
                for half in range(2):
                    b = next_bank()

                    def f(b=b, half=half, sv=sv):
                        last = None
                        for cc in range(4):
                            c = half * 4 + cc
                            last = nc.tensor.transpose(banks[b][:, cc * 128:(cc + 1) * 128],
                                                       sv[:, c * 128:(c + 1) * 128], identf)
                        return last
                    S.op("pe", f, reads=SR(i % 2) + ["cstf"], writes=[BK(b)])
                    dst = xT[:, half * 4:half * 4 + 4, i * 128:(i + 1) * 128]
                    src = banks[b][:, :].rearrange("p (c t) -> p c t", c=4)
                    wr = [("xT", c, i // 4) for c in range(half * 4, half * 4 + 4)]
                    if half == 0:
                        S.op("act", lambda dst=dst, src=src: nc.scalar.copy(out=dst, in_=src), reads=[BK(b)], writes=wr)
                    else:
                        S.op("dve", lambda dst=dst, src=src: nc.vector.tensor_copy(out=dst, in_=src), reads=[BK(b)], writes=wr)

        def store_x():
            for i in range(16):
                stg = stage[i % 2]
                sv = stg[:, :, :].rearrange("p a b -> p (a b)")
                for half in range(2):
                    b = next_bank()

                    def f(b=b, half=half):
                        last = None
                        for cc in range(4):
                            c = half * 4 + cc
                            last = nc.tensor.transpose(banks[b][:, cc * 128:(cc + 1) * 128],
                                                       xT[:, c, i * 128:(i + 1) * 128], identf)
                        return last
                    S.op("pe", f, reads=[("xT", c, i // 4) for c in range(half * 4, half * 4 + 4)] + ["cstf"],
                         writes=[BK(b)])
                    dst = sv[:, half * 512:(half + 1) * 512]
                    if half == 0:
                        S.op("act", lambda dst=dst, b=b: nc.scalar.copy(out=dst, in_=banks[b][:, :]),
                             reads=[BK(b)], writes=[SR(i % 2)[0]])
                    else:
                        S.op("dve", lambda dst=dst, b=b: nc.vector.tensor_copy(out=dst, in_=banks[b][:, :]),
                             reads=[BK(b)], writes=[SR(i % 2)[1]])
                S.dma("sp", d_out[i % 2], y_out[i * 128:(i + 1) * 128, :], sv,
                      reads=SR(i % 2))

        def stage_guard():
            pass

        def norm_chunk(gidx, n, inplace=False):
            for c in range(8):
                sq = spb[:, c % 2, :]
                S.op("act", lambda sq=sq, c=c: nc.scalar.activation(out=sq, in_=xT[:, c, CH(n)], func=AF.Square),
                     reads=[("xT", c, n)], writes=[("spb", c % 2)])
                S.op("pe", lambda sq=sq, c=c: nc.tensor.matmul(banks[7][:, :], lhsT=onesb[:, :], rhs=sq,
                                                               start=(c == 0), stop=(c == 7)),
                     reads=[("spb", c % 2), "onesb"], writes=[BK(7)])
            S.op("act", lambda: nc.scalar.activation(out=rstd, in_=banks[7][:, :], func=AF.Ln,
                                                     scale=1.0 / D, bias=epsb[:, 0:1]),
                 reads=[BK(7), "epsb"], writes=[("gbuf", 1)])
            S.op("act", lambda: nc.scalar.activation(out=rstd, in_=rstd, func=AF.Exp, scale=-0.5),
                 reads=[("gbuf", 1)], writes=[("gbuf", 1)])
            for c in range(8):
                dst = xT[:, c, CH(n)] if inplace else hT[:, c, CH(n)]
                wr = [("xT", c, n)] if inplace else [("hT", c, n)]
                S.op("dve", lambda dst=dst, c=c: nc.vector.scalar_tensor_tensor(
                    out=dst, in0=xT[:, c, CH(n)], scalar=gains[:, gidx * 8 + c:gidx * 8 + c + 1], in1=rstd,
                    op0=ALU.mult, op1=ALU.mult), reads=[("xT", c, n), "gains", ("gbuf", 1)], writes=wr)


        def norm(gidx, inplace=False):
            for n in range(4):
                norm_chunk(gidx, n, inplace)

        epsb = sb("epsb", (128, 1), F32)
        S.op("dve", lambda: nc.vector.memset(epsb[:, :], EPS), writes=["epsb"])

        def resid_add(ft, n, b):
            S.op("dve", lambda: nc.vector.tensor_tensor(out=xT[:, ft, CH(n)], in0=banks[b][:, :], in1=xT[:, ft, CH(n)],
                                                        op=ALU.add), reads=[BK(b), ("xT", ft, n)], writes=[("xT", ft, n)])

        def proj_fm(wt, wres, col0, dst, dres, copy_eng="act", dpar=0, bank=None):
            wv = v3(wt, 8, 512)
            pcs = []
            for n in range(4):
                st_ = {}

                def half(h, n=n, st_=st_):
                    if h == 0:
                        st_["b"] = next_bank(0, 2) if bank is None else bank
                    b = st_["b"]
                    S.op("pe", lambda: mm_group(banks[b][:, :], [(wv[:, kc, col0:col0 + 128], hT[:, kc, CH(n)])
                                                                 for kc in range(4 * h, 4 * h + 4)], first=(h == 0), skip=True),
                         reads=[wres] + HT(n), writes=[BK(b)])

                def evac(n=n, st_=st_):
                    b = st_["b"]
                    if copy_eng == "act":
                        S.op("act", lambda: nc.scalar.copy(out=dst[:, CH(n)], in_=banks[b][:, :]),
                             reads=[BK(b)], writes=[(dres, dpar, n)])
                    else:
                        S.op("dve", lambda: nc.vector.tensor_copy(out=dst[:, CH(n)], in_=banks[b][:, :]),
                             reads=[BK(b)], writes=[(dres, dpar, n)])
                pcs += [lambda half=half: half(0), lambda half=half: half(1), evac]
            return pcs

        def proj_v(wt, wres, col0, ncol, vdst=None, dpar=0, bank=None, copy_eng="act"):
            wv = v3(wt, 8, 512)
            vdst = vp if vdst is None else vdst
            pcs = []
            for i4 in range(4):
                st_ = {}

                def half(h, i4=i4, st_=st_):
                    if h == 0:
                        st_["b"] = next_bank(0, 2) if bank is None else bank
                    b = st_["b"]

                    def f():
                        last = None
                        for ii in range(2 * h, 2 * h + 2):
                            i = i4 * 4 + ii
                            last = mm_group(banks[b][:, ii * 128:ii * 128 + ncol],
                                            [(hT[:, kc, i * 128:(i + 1) * 128], wv[:, kc, col0:col0 + ncol]) for kc in range(8)])
                        return last
                    S.op("pe", f, reads=[wres] + HT(i4), writes=[BK(b)])

                def evac(i4=i4, st_=st_):
                    b = st_["b"]
                    src = banks[b][:, :].rearrange("p (a c) -> p a c", a=4)[:, :, 0:ncol]
                    if copy_eng == "act":
                        S.op("act", lambda: nc.scalar.copy(out=vdst[:, i4 * 4:i4 * 4 + 4, 0:ncol], in_=src),
                             reads=[BK(b)], writes=[("vp", dpar)])
                    else:
                        S.op("dve", lambda: nc.vector.tensor_copy(out=vdst[:, i4 * 4:i4 * 4 + 4, 0:ncol], in_=src),
                             reads=[BK(b)], writes=[("vp", dpar)])
                pcs += [lambda half=half: half(0), lambda half=half: half(1), evac]
            return pcs

        def swap_cols(wt, wres):
            wv = v3(wt, 8, 512)
            src = wv[:, :, 0:128].rearrange("p k (h d) -> p k h d", h=2)
            dstv = wv[:, :, 128:256].rearrange("p k (h d) -> p k h d", h=2)
            for (a, b, n) in ((0, 8, 8), (8, 0, 8), (16, 16, 48)):
                S.op("act", lambda a=a, b=b, n=n: nc.scalar.copy(out=dstv[:, :, :, a:a + n], in_=src[:, :, :, b:b + n]),
                     reads=[wres], writes=[wres])

        def proj_rope(wt, wres, col0, dst, dres, dpar=0, bks=(0, 1), tmp=None, tmpn="gbuf"):
            wv = v3(wt, 8, 512)
            tmp = gbuf if tmp is None else tmp
            rp = special[:, 0:4096].rearrange("p (a t) -> p a t", a=2)
            pcs = []
            for n in range(4):
                def pa(n=n):
                    S.op("pe", lambda: mm_group(banks[bks[0]][:, :], [(wv[:, kc, col0:col0 + 128], hT[:, kc, CH(n)])
                                                                      for kc in range(8)]),
                         reads=[wres] + HT(n), writes=[BK(bks[0])])

                def pb(n=n):
                    S.op("pe", lambda: mm_group(banks[bks[1]][:, :], [(wv[:, kc, col0 + 128:col0 + 256], hT[:, kc, CH(n)])
                                                                      for kc in range(8)]),
                         reads=[wres] + HT(n), writes=[BK(bks[1])])

                def pc(n=n):
                    S.op("dve", lambda: nc.vector.tensor_tensor(out=tmp[:, 0, :], in0=banks[bks[0]][:, :], in1=rp[:, 0, CH(n)],
                                                                op=ALU.mult), reads=[BK(bks[0]), "special"], writes=[(tmpn, 0)])
                    S.op("dve", lambda: nc.vector.tensor_tensor(out=tmp[:, 1, :], in0=banks[bks[1]][:, :], in1=rp[:, 1, CH(n)],
                                                                op=ALU.mult), reads=[BK(bks[1]), "special"], writes=[(tmpn, 1)])
                    S.op("dve", lambda: nc.vector.tensor_tensor(out=dst[:, CH(n)], in0=tmp[:, 0, :], in1=tmp[:, 1, :],
                                                                op=ALU.add),
                         reads=[(tmpn, 0), (tmpn, 1)], writes=[(dres, dpar, n)])
                pcs += [pa, pb, pc]
            return pcs

        HP = (slice(0, 64), slice(64, 128))

        def zmm(i, t0, off, nq):
            def f():
                last = None
                for hh in range(2):
                    last = nc.tensor.matmul(banks[2 + hh][:, off:off + nq], lhsT=kT[HP[hh], i * 128:(i + 1) * 128],
                                            rhs=qT[HP[hh], t0 + off:t0 + off + nq], start=True, stop=True)
                return last
            return f

        def attn_sb(make_next):
            tiles = [(pp, cq, i) for pp in range(8) for cq in range(4) for i in range(4 * cq + 3, -1, -1)]
            NT = len(tiles)

            def ctx(n):
                pp = tiles[n][0]
                return (pp, pp % 2) + tuple(qkv_views(0, pp % 2))
            EBT = [ebuf, ebuf2, ebuf3]
            SPT = [spb, spb2]
            EB = [[("ebuf", 0), ("ebuf", 1)], [("eb2", 0), ("eb2", 1)], [("eb3", 0), ("eb3", 1)]]
            SP = [[("spb", 0), ("spb", 1)], [("sp2", 0), ("sp2", 1)]]
            GB = [("gbuf", 0), ("gbuf", 1)]
            WB = [("wb", 0), ("wb", 1)]
            ZB = [2, 0]

            def geom(n):
                _, cq, i = tiles[n]
                off = max(0, (i - 4 * cq) * 128)
                return cq, i, off, slice(off, 512)

            def P1(n):
                cq, i, off, cs = geom(n)
                p, par, qTv, kTv, vpv = ctx(n)
                zb = ZB[n % 2]

                def f():
                    last = None
                    for hh in range(2):
                        last = nc.tensor.matmul(banks[zb + hh][:, cs], lhsT=kTv[HP[hh], i * 128:(i + 1) * 128],
                                                rhs=qTv[HP[hh], cq * 512 + off:(cq + 1) * 512], start=True, stop=True)
                    return last
                S.op("pe", f, reads=[("qT", par, cq), ("kT", par, i // 4)], writes=[BK(zb), BK(zb + 1)])

            def S1(n):
                cq, i, off, cs = geom(n)
                zb = ZB[n % 2]
                e = EBT[n % 3]
                S.op("act", lambda: nc.scalar.activation(out=e[:, :, cs], in_=psum[:, zb:zb + 2, cs], func=AF.Exp, scale=0.125),
                     reads=[BK(zb), BK(zb + 1)], writes=EB[n % 3])
                if i >= 4 * cq:
                    S.op("dve", lambda: nc.vector.tensor_tensor(
                        out=e[:, :, off:off + 128], in0=e[:, :, off:off + 128],
                        in1=ltb.unsqueeze(1).to_broadcast([128, 2, 128]), op=ALU.mult),
                        reads=EB[n % 3] + ["cstb"], writes=EB[n % 3])

            def S2(n):
                cq, i, off, cs = geom(n)
                S.op("act", lambda: nc.scalar.activation(out=SPT[n % 2][:, :, cs], in_=EBT[n % 3][:, :, cs], func=AF.Ln,
                                                         bias=1.0, scale=1.0), reads=EB[n % 3], writes=SP[n % 2])

            def P2(n):
                cq, i, off, cs = geom(n)
                first = (i == 4 * cq + 3)

                def f():
                    last = None
                    for hh in range(2):
                        last = nc.tensor.matmul(banks[4 + hh][:, cs], lhsT=geb, rhs=SPT[n % 2][:, hh, cs], start=first,
                                                stop=True, skip_group_check=True)
                    return last
                S.op("pe", f, reads=SP[n % 2] + ["cstb"], writes=[BK(4), BK(5)])

            def S3(n):
                cq, i, off, cs = geom(n)
                S.op("act", lambda: nc.scalar.activation(out=gbuf[:, :, cs], in_=psum[:, 4:6, cs], func=AF.Exp, scale=-1.0),
                     reads=[BK(4), BK(5)], writes=GB)

            def D2(n):
                cq, i, off, cs = geom(n)
                S.op("dve", lambda: nc.vector.tensor_tensor(out=wb[:, :, cs], in0=EBT[n % 3][:, :, cs], in1=gbuf[:, :, cs],
                                                            op=ALU.mult), reads=EB[n % 3] + GB, writes=WB)

            def LTB(n):
                cq, i, off, cs = geom(n)
                if i > 0:
                    def f():
                        last = None
                        for hh in range(2):
                            last = nc.tensor.matmul(banks[4 + hh][:, cs], lhsT=ltb, rhs=SPT[n % 2][:, hh, cs], start=False,
                                                    stop=True, skip_group_check=True)
                        return last
                    S.op("pe", f, reads=SP[n % 2] + ["cstb"], writes=[BK(4), BK(5)])

            def PV(n):
                cq, i, off, cs = geom(n)
                p, par, qTv, kTv, vpv = ctx(n)
                first = (i == 4 * cq + 3)

                def f2():
                    last = None
                    for hh in range(2):
                        last = nc.tensor.matmul(banks[6][HP[hh], cs], lhsT=vpv[:, i, HP[hh]], rhs=wb[:, hh, cs], start=first,
                                                stop=True, skip_group_check=True)
                    return last
                S.op("pe", f2, reads=WB + [("vp", par)], writes=[BK(6)])
                if i == 0:
                    S.op("dve", lambda: nc.vector.tensor_copy(out=big[:, p, CH(cq)], in_=banks[6][:, :]),
                         reads=[BK(6)], writes=[("big", p)])

            pieces = []
            P1(0)
            S1(0)
            S2(0)
            P1(1)
            P2(0)
            for n in range(NT):
                if n % 40 == 0:
                    while pieces:
                        pieces.pop(0)()
                    pieces = list(make_next(n // 40 + 1))
                if n + 1 < NT:
                    S1(n + 1)
                S3(n)
                if n + 2 < NT:
                    P1(n + 2)
                D2(n)
                if n + 1 < NT:
                    S2(n + 1)
                LTB(n)
                if n + 1 < NT:
                    P2(n + 1)
                PV(n)
                if pieces:
                    pieces.pop(0)()
            while pieces:
                pieces.pop(0)()

        def fox_rb(pp):
            par = pp % 2
            fx = special[:, :]
            nmid = fx[:, 1792:2048].rearrange("p (j h) -> p j h", j=16)
            Rb = RBV[par]
            pcs = []
            for hh in range(2):
                h = 2 * pp + hh
                r0 = 32 * hh

                def pc(h=h, r0=r0, hh=hh):
                    S.op("dve", lambda: nc.vector.tensor_scalar(
                        out=Rb[r0:r0 + 1, :].rearrange("p (j c) -> p j c", j=16),
                        in0=nmid[r0:r0 + 1, :, h:h + 1].to_broadcast([1, 16, 128]), scalar1=-8.0, scalar2=None, op0=ALU.mult),
                        reads=["special"], writes=[("Rb", par, hh)] + ([("ebuf", 0), ("ebuf", 1)] if par == 1 else []))
                pcs.append(pc)
            return pcs

        def attn_fox(make_next):
            fx = special[:, :]
            ncum = fx[:, 512:768].rearrange("p (j h) -> p j h", j=16)

            def ctx(n):
                pp = tiles[n][0]
                return (pp, pp % 2) + tuple(qkv_views(2, pp % 2))
            tiles = [(pp, cq, i) for pp in range(8) for cq in range(4) for i in range(0, 4 * cq + 4)]
            NT = len(tiles)
            WT = [wb, wb2]
            WB = [[("wb", 0), ("wb", 1)], [("wb2", 0), ("wb2", 1)]]
            ZB = [2, 0]

            def geom(n):
                _, cq, i = tiles[n]
                off = max(0, (i - 4 * cq) * 128)
                return cq, i, off, slice(off, 512)

            def P1(n):
                cq, i, off, cs = geom(n)
                p, par, qTv, kTv, vpv = ctx(n)
                Rb = RBV[par]
                zb = ZB[n % 2]

                def f():
                    last = None
                    for hh in range(2):
                        nc.tensor.matmul(banks[zb + hh][:, cs], lhsT=kTv[HP[hh], i * 128:(i + 1) * 128],
                                         rhs=qTv[HP[hh], cq * 512 + off:(cq + 1) * 512], start=True, stop=False,
                                         skip_group_check=True)
                    for hh in range(2):
                        r0 = 32 * hh
                        last = nc.tensor.matmul(banks[zb + hh][:, cs], lhsT=onesb[r0:r0 + 1, :],
                                                rhs=Rb[r0:r0 + 1, cq * 512 + off:(cq + 1) * 512], start=False, stop=True,
                                                skip_group_check=True)
                    return last
                S.op("pe", f, reads=[("qT", par, cq), ("kT", par, i // 4), ("Rb", par, 0), ("Rb", par, 1), "onesb"], writes=[BK(zb), BK(zb + 1)])

            def SE(n):
                cq, i, off, cs = geom(n)
                p, par, qTv, kTv, vpv = ctx(n)
                Rb = RBV[par]
                zb = ZB[n % 2]
                w_ = WT[n % 2]
                for hh in range(2):
                    h = 2 * p + hh
                    S.op("act", lambda hh=hh, h=h: nc.scalar.activation(out=w_[:, hh, cs], in_=banks[zb + hh][:, cs], func=AF.Exp,
                                                                        scale=0.125, bias=ncum[:, i, h:h + 1]),
                         reads=[BK(zb + hh), "special"], writes=[WB[n % 2][hh]])
                if i >= 4 * cq:
                    S.op("dve", lambda: nc.vector.tensor_tensor(
                        out=w_[:, :, off:off + 128], in0=w_[:, :, off:off + 128],
                        in1=leb.unsqueeze(1).to_broadcast([128, 2, 128]), op=ALU.mult),
                        reads=WB[n % 2] + ["cstb"], writes=WB[n % 2])

            def P3(n):
                cq, i, off, cs = geom(n)
                p, par, qTv, kTv, vpv = ctx(n)
                Rb = RBV[par]
                first = (i == 0)
                w_ = WT[n % 2]

                def f():
                    last = None
                    for hh in range(2):
                        last = nc.tensor.matmul(banks[6][HP[hh], cs], lhsT=vpv[:, i, HP[hh]], rhs=w_[:, hh, cs], start=first,
                                                stop=True, skip_group_check=True)
                    for hh in range(2):
                        last = nc.tensor.matmul(banks[4][HP[hh], cs], lhsT=onesb[:, HP[hh]], rhs=w_[:, hh, cs], start=first,
                                                stop=True, skip_group_check=True)
                    return last
                S.op("pe", f, reads=WB[n % 2] + [("vp", par), "onesb"], writes=[BK(6), BK(4)])
                if i == 4 * cq + 3:
                    normalize(p, cq, None)

            pieces = []
            P1(0)
            SE(0)
            for n in range(NT):
                if n % 40 == 0:
                    while pieces:
                        pieces.pop(0)()
                    pieces = list(make_next(n // 40 + 1))
                if n + 1 < NT:
                    P1(n + 1)
                    SE(n + 1)
                P3(n)
                if pieces:
                    pieces.pop(0)()
            while pieces:
                pieces.pop(0)()

        def normalize(p, cq, sink_cols):
            if sink_cols is None:
                S.op("act", lambda: nc.scalar.activation(out=gbuf[:, 0, :], in_=banks[4][:, :], func=AF.Ln),
                     reads=[BK(4)], writes=[("gbuf", 0)])
            else:
                S.op("act", lambda: nc.scalar.activation(out=gbuf[:, 0, :], in_=banks[4][:, :], func=AF.Ln,
                                                         bias=small[:, 16 + p:17 + p], scale=1.0),
                     reads=[BK(4), "small"], writes=[("gbuf", 0)])
            S.op("dve", lambda: nc.vector.tensor_copy(out=gbuf[:, 1, :], in_=banks[6][:, :]), reads=[BK(6)], writes=[("gbuf", 1)])
            S.op("act", lambda: nc.scalar.activation(out=gbuf[:, 0, :], in_=gbuf[:, 0, :], func=AF.Exp, scale=-1.0),
                 reads=[("gbuf", 0)], writes=[("gbuf", 0)])
            S.op("dve", lambda: nc.vector.tensor_tensor(out=big[:, p, CH(cq)], in0=gbuf[:, 1, :], in1=gbuf[:, 0, :], op=ALU.mult),
                 reads=[("gbuf", 1), ("gbuf", 0)], writes=[("big", p)])

        def attn_swa(p, par=0, pieces=()):
            pieces = list(pieces)
            qTv = qT if par == 0 else vp2[:, :, :].rearrange("p a b -> p (a b)")
            WT = [wb, wb2]
            WB = [[("wb", 0), ("wb", 1)], [("wb2", 0), ("wb2", 1)]]
            ZB = [2, 0]

            def P1(i):
                nq = 256 if i < 15 else 128
                zb = ZB[i % 2]

                def f():
                    last = None
                    for hh in range(2):
                        last = nc.tensor.matmul(banks[zb + hh][:, 0:nq], lhsT=kT[HP[hh], i * 128:(i + 1) * 128],
                                                rhs=qTv[HP[hh], i * 128:i * 128 + nq], start=True, stop=True)
                    return last
                S.op("pe", f, reads=[("qT", par, n) for n in {i // 4, (i * 128 + nq - 1) // 512}] + [("kT", 0, i // 4)],
                     writes=[BK(zb), BK(zb + 1)])

            def SE(i):
                nq = 256 if i < 15 else 128
                zb = ZB[i % 2]
                w_ = WT[i % 2]
                S.op("act", lambda: nc.scalar.activation(out=w_[:, :, 0:nq], in_=psum[:, zb:zb + 2, 0:nq], func=AF.Exp, scale=0.125),
                     reads=[BK(zb), BK(zb + 1)], writes=WB[i % 2])
                S.op("dve", lambda: nc.vector.tensor_tensor(out=w_[:, :, 0:nq], in0=w_[:, :, 0:nq],
                                                            in1=bandb[:, 0:nq].unsqueeze(1).to_broadcast([128, 2, nq]), op=ALU.mult),
                     reads=WB[i % 2] + ["cstb"], writes=WB[i % 2])

            def pv(i, e0, c0, n, first):
                w_ = WT[i % 2]

                def f():
                    last = None
                    for hh in range(2):
                        last = nc.tensor.matmul(banks[6][HP[hh], c0:c0 + n], lhsT=vp[:, i, 0:64], rhs=w_[:, hh, e0:e0 + n],
                                                start=first, stop=True, skip_group_check=True)
                    for hh in range(2):
                        last = nc.tensor.matmul(banks[4][HP[hh], c0:c0 + n], lhsT=onesb[:, HP[hh]], rhs=w_[:, hh, e0:e0 + n],
                                                start=first, stop=True, skip_group_check=True)
                    return last
                S.op("pe", f, reads=WB[i % 2] + [("vp", 0), "onesb"], writes=[BK(6), BK(4)])

            def P3(i):
                cq = i // 4
                c0 = (i % 4) * 128
                if i % 4 != 3:
                    pv(i, 0, c0, 256, i == 0)
                else:
                    pv(i, 0, c0, 128, False)
                    normalize(p, cq, True)
                    if i < 15:
                        pv(i, 128, 0, 128, True)

            P1(0)
            SE(0)
            for i in range(16):
                if i + 1 < 16:
                    P1(i + 1)
                    SE(i + 1)
                P3(i)
                if pieces:
                    pieces.pop(0)()
            while pieces:
                pieces.pop(0)()

        def fox_prep(wt, wres):
            wv = v3(wt, 8, 512)
            fx = special[:, :]
            nlf = fx[:, 0:256]

            def f():
                last = None
                for i in range(16):
                    last = mm_group(banks[2][:, i * 16:(i + 1) * 16],
                                    [(hT[:, kc, i * 128:(i + 1) * 128], wv[:, kc, 0:16]) for kc in range(8)])
                    last = nc.tensor.matmul(banks[2][:, i * 16:(i + 1) * 16], lhsT=onesf[0:1, :], rhs=rowst[0:1, 16:32],
                                            start=False, stop=True, skip_group_check=True)
                return last
            S.op("pe", f, reads=[wres, "rowst1", "onesf"] + [("hT", c, n) for c in range(8) for n in range(4)],
                 writes=[BK(2)])
            S.op("act", lambda: nc.scalar.activation(out=nlf, in_=banks[2][:, 0:256], func=AF.Exp, scale=-1.0),
                 reads=[BK(2)], writes=["special"])
            S.op("act", lambda: nc.scalar.activation(out=nlf, in_=nlf, func=AF.Ln, bias=1.0, scale=1.0),
                 reads=["special"], writes=["special"])

            def f2():
                last = None
                for j in range(16):
                    for i2 in range(j + 1):
                        last = nc.tensor.matmul(banks[3][:, j * 16:(j + 1) * 16], lhsT=onesf[:, :],
                                                rhs=nlf[:, i2 * 16:(i2 + 1) * 16], start=(i2 == 0), stop=(i2 == j))
                for i in range(16):
                    last = nc.tensor.matmul(banks[4][:, i * 16:(i + 1) * 16], lhsT=lef, rhs=nlf[:, i * 16:(i + 1) * 16],
                                            start=True, stop=True)
                return last
            S.op("pe", f2, reads=["special", "onesf", "cstf"], writes=[BK(3), BK(4)])
            S.op("dve", lambda: nc.vector.tensor_copy(out=fx[:, 256:512], in_=banks[3][:, 0:256]), reads=[BK(3)],
                 writes=["special"])
            S.op("dve", lambda: nc.vector.tensor_copy(out=fx[:, 512:528], in_=banks[4][:, 0:16]), reads=[BK(4)],
                 writes=["special"])
            S.op("dve", lambda: nc.vector.tensor_tensor(out=fx[:, 528:768], in0=banks[4][:, 16:256], in1=fx[:, 256:496],
                                                        op=ALU.add), reads=[BK(4), "special"], writes=["special"])
            S.op("dve", lambda: nc.vector.tensor_scalar(out=fx[:, 1792:1808], in0=fx[:, 256:272], scalar1=0.5, scalar2=None,
                                                        op0=ALU.mult), reads=["special"], writes=["special"])
            S.op("dve", lambda: nc.vector.tensor_tensor(out=fx[:, 1808:2048], in0=fx[:, 256:496], in1=fx[:, 272:512],
                                                        op=ALU.add), reads=["special"], writes=["special"])
            S.op("dve", lambda: nc.vector.tensor_scalar(out=fx[:, 1808:2048], in0=fx[:, 1808:2048], scalar1=0.5, scalar2=None,
                                                        op0=ALU.mult), reads=["special"], writes=["special"])

        def dump_bf(src):
            for c in range(8):
                for n in range(4):
                    S.op("dve", lambda c=c, n=n: nc.vector.tensor_copy(out=xT[:, c, CH(n)], in_=src[:, c, CH(n)]),
                         reads=[("hT", c, n), ("big", c), ("qT", 0, n), ("kT", 0, n), ("vp", 0)], writes=[("xT", c, n)])
            raise _Stop()

        def layer(l, prenormed, next_norm):
            k = KINDS[l]
            if not prenormed:
                norm(3 * l + 0)
            if dbg == "h":
                dump_bf(hT)
            if k == 1:
                rp = special[:, 0:4096].rearrange("p (a t) -> p a t", a=2)
                S.dma("sp", S.new_dma_sem("d_rope"), rp, rope_in, writes=["special"])
                S.op("pe", lambda: nc.tensor.matmul(banks[0][:, 0:16], lhsT=onesf[0:1, :], rhs=rowst[0:1, 0:16],
                                                    start=True, stop=True), reads=["onesf", "rowst0"], writes=[BK(0)])
                S.op("act", lambda: nc.scalar.activation(out=small[:, 0:16], in_=banks[0][:, 0:16], func=AF.Exp),
                     reads=[BK(0)], writes=["small"])
                for hh in range(2):
                    S.op("dve", lambda hh=hh: nc.vector.tensor_copy(
                        out=small[HP[hh], 16:24], in_=small[HP[hh], 0:16].rearrange("p (a b) -> p a b", b=2)[:, :, hh]),
                        reads=["small"], writes=["small"])
            if k == 2:
                wt, wres = ws_next()
                fox_prep(wt, wres)
            def make_pieces(pp, par, overlapped):
                wt, wres = ws_next()
                qd, kd, vd = qkv_views(k, par)
                bank = 7 if overlapped else None
                eng = "dve" if overlapped else "act"
                return (proj_fm(wt, wres, 0, qd, "qT", copy_eng=eng, dpar=par, bank=bank)
                        + proj_fm(wt, wres, 128, kd, "kT", copy_eng="dve", dpar=par, bank=bank)
                        + proj_v(wt, wres, 256, 128, vdst=vd, dpar=par, bank=bank, copy_eng=eng))

            if k in (0, 2):
                for pc in make_pieces(0, 0, False):
                    pc()
            if k == 0:
                attn_sb(lambda pp: make_pieces(pp, pp % 2, True) if pp < 8 else [])
            if k == 2:
                for pc in fox_rb(0):
                    pc()
                attn_fox(lambda pp: (fox_rb(pp) + make_pieces(pp, pp % 2, True)) if pp < 8 else [])
            for p in range(8):
                if k == 0:
                    break
                if k == 2:
                    break
                if True:
                    if p % 4 == 0:
                        wt, wres = ws_next()
                        swap_cols(wt, wres)
                        for pc in proj_rope(wt, wres, 0, kT, "kT") + proj_v(wt, wres, 256, 64):
                            pc()
                    if p == 0:
                        wt, wres = ws_next()
                        swap_cols(wt, wres)
                        for pc in proj_rope(wt, wres, 0, qT, "qT"):
                            pc()
                    nxt = []
                    if p + 1 < 8:
                        wt, wres = ws_next()
                        swap_cols(wt, wres)
                        npar = (p + 1) % 2
                        qd = qT if npar == 0 else vp2[:, :, :].rearrange("p a b -> p (a b)")
                        nxt = proj_rope(wt, wres, 0, qd, "qT", dpar=npar, bks=(5, 7), tmp=ebuf, tmpn="ebuf")
                    attn_swa(p, p % 2, nxt)
            if dbg == "a":
                dump_bf(big)
            for hf in range(2):
                wt, wres = ws_next()
                wv = v3(wt, 8, 512)
                order = [(ftl, n) for ftl in range(4) for n in range(4)] if hf == 0 else \
                        [(ftl, n) for n in range(4) for ftl in range(4)]
                for (ftl, n) in order:
                    ft = hf * 4 + ftl
                    b = next_bank()
                    S.op("pe", lambda b=b, n=n, ftl=ftl, wv=wv: mm_group(
                        banks[b][:, :], [(wv[:, kc, ftl * 128:(ftl + 1) * 128], big[:, kc, CH(n)]) for kc in range(8)]),
                        reads=[wres] + [("big", kc) for kc in range(8)], writes=[BK(b)])
                    resid_add(ft, n, b)
                    if hf == 1 and ftl == 3 and n >= 1 and dbg is None:
                        norm_chunk(3 * l + 1, n - 1)
            if dbg is None:
                norm_chunk(3 * l + 1, 3)
            if dbg == "xa":
                raise _Stop()
            pT = qk
            PST = [wb[:, :, :].rearrange("p a b -> p (a b)").bitcast(F32), wb2[:, :, :].rearrange("p a b -> p (a b)").bitcast(F32)]
            PSR = [[("wb", 0), ("wb", 1)], [("wb2", 0), ("wb2", 1)]]
            def p_dma(i):
                S.dma("sp", d_st[i % 2], PST[i % 2][:, 0:256], p_in[l, i * 128:(i + 1) * 128, :], writes=PSR[i % 2])

            def p_xpose(i):
                stg = PST[i % 2]

                def f():
                    last = None
                    for kc in range(2):
                        last = nc.tensor.transpose(banks[7][:, kc * 128:(kc + 1) * 128], stg[:, kc * 128:(kc + 1) * 128], identf)
                    return last
                S.op("pe", f, reads=PSR[i % 2] + ["cstf"], writes=[BK(7)])
                src = banks[7][:, 0:256].rearrange("p (a t) -> p a t", a=2)
                S.op("act", lambda: nc.scalar.copy(out=pT[:, :, i * 128:(i + 1) * 128], in_=src),
                     reads=[BK(7)], writes=[("qT", 0, i // 4), ("kT", 0, i // 4)])
            p_dma(0)
            p_dma(1)
            if dbg is not None:
                norm(3 * l + 1)
            for g in range(8):
                wt, wres = ws_next()
                wv = v3(wt, 8, 512)
                ub = g % 2
                uT = big[:, ub * 4:(ub + 1) * 4, :]
                for ftl in range(4):
                    for n in range(4):
                        b = next_bank()
                        S.op("pe", lambda b=b, n=n, ftl=ftl, wv=wv: mm_group(
                            banks[b][:, :], [(wv[:, kc, ftl * 128:(ftl + 1) * 128], hT[:, kc, CH(n)]) for kc in range(8)]),
                            reads=[wres] + HT(n), writes=[BK(b)])
                        tb = ebuf if (ftl * 4 + n) % 2 == 0 else gbuf
                        tr = "ebuf" if (ftl * 4 + n) % 2 == 0 else "gbuf"
                        S.op("act", lambda b=b, tb=tb: nc.scalar.activation(out=tb[:, 0, :], in_=banks[b][:, :], func=AF.Relu),
                             reads=[BK(b)], writes=[(tr, 0)])
                        S.op("dve", lambda tb=tb, n=n, ftl=ftl, uT=uT: nc.vector.tensor_tensor(
                            out=uT[:, ftl, CH(n)], in0=tb[:, 0, :], in1=tb[:, 0, :], op=ALU.mult),
                            reads=[(tr, 0)], writes=[("big", ub * 4 + ftl)])
                for i in (2 * g, 2 * g + 1):
                    p_xpose(i)
                    if i + 2 < 16:
                        p_dma(i + 2)
                wt, wres = ws_next()
                wv = v3(wt, 4, 1024)
                order = [(ft, n) for ft in range(8) for n in range(4)] if g < 7 else \
                        [(ft, n) for n in range(4) for ft in range(8)]
                for (ft, n) in order:
                    b = next_bank()
                    S.op("pe", lambda b=b, n=n, ft=ft, wv=wv, uT=uT: mm_group(
                        banks[b][:, :], [(wv[:, kc, ft * 128:(ft + 1) * 128], uT[:, kc, CH(n)]) for kc in range(4)]),
                        reads=[wres] + [("big", ub * 4 + kc) for kc in range(4)], writes=[BK(b)])
                    resid_add(ft, n, b)
                    if g == 7 and ft == 7 and n >= 1 and dbg is None:
                        norm_chunk(3 * l + 2, n - 1)
            if dbg is None:
                norm_chunk(3 * l + 2, 3)
            if dbg == "xm":
                raise _Stop()
            pT = qk
            if dbg is not None:
                norm(3 * l + 2)
            wgs = [ws_next(), ws_next(hold=True)]
            wpt, wpres = ws_next(hold=True)
            wpv = v3(wpt, 2, 1024)
            gtmp = ebuf[:, 1, :]
            for n in range(4):
                for ft in range(8):
                    wt, wres = wgs[ft // 4]
                    wv = v3(wt, 8, 512)
                    ftl = ft % 4
                    bg = next_bank()
                    S.op("pe", lambda bg=bg, n=n, ftl=ftl, wv=wv: mm_group(
                        banks[bg][:, :], [(wv[:, kc, ftl * 128:(ftl + 1) * 128], hT[:, kc, CH(n)]) for kc in range(8)]),
                        reads=[wres] + HT(n), writes=[BK(bg)])
                    bp = next_bank()
                    S.op("pe", lambda bp=bp, n=n, ft=ft: mm_group(
                        banks[bp][:, :], [(wpv[:, kc, ft * 128:(ft + 1) * 128], pT[:, kc, CH(n)]) for kc in range(2)]),
                        reads=[wpres, ("qT", 0, n), ("kT", 0, n)], writes=[BK(bp)])
                    S.op("act", lambda bg=bg: nc.scalar.activation(out=gtmp, in_=banks[bg][:, :], func=AF.Sigmoid),
                         reads=[BK(bg)], writes=[("ebuf", 1)])
                    S.op("dve", lambda bp=bp: nc.vector.tensor_tensor(out=gtmp, in0=banks[bp][:, :], in1=gtmp,
                                                                      op=ALU.mult), reads=[BK(bp), ("ebuf", 1)], writes=[("ebuf", 1)])
                    S.op("dve", lambda n=n, ft=ft: nc.vector.tensor_tensor(out=xT[:, ft, CH(n)], in0=gtmp,
                                                                           in1=xT[:, ft, CH(n)], op=ALU.add),
                         reads=[("ebuf", 1), ("xT", ft, n)], writes=[("xT", ft, n)])
                if next_norm is not None and n >= 1 and dbg is None:
                    norm_chunk(next_norm[0], n - 1, next_norm[1])
            if next_norm is not None and dbg is None:
                norm_chunk(next_norm[0], 3, next_norm[1])

        def alias_sync(names_from, names_to):
            for a in names_from:
                for b in names_to:
                    if a in S.wr:
                        cur = S.wr.get(b)
                        S.rd.setdefault(b, {})
                        k, v = S.wr[a]
                        S.rd[b][k] = max(S.rd[b].get(k, 0), v)
                    for k, v in S.rd.get(a, {}).items():
                        S.rd.setdefault(b, {})
                        S.rd[b][k] = max(S.rd[b].get(k, 0), v)

        EG = [("ebuf", 0), ("ebuf", 1), ("gbuf", 0), ("gbuf", 1)]
        STG = [("stage", 0), ("stage", 1), ("stage", 0, 0), ("stage", 0, 1), ("stage", 1, 0), ("stage", 1, 1)]
        load_x()
        try:
            for idx, l in enumerate(layers):
                if dbg is not None:
                    nxt = None
                elif idx + 1 < len(layers):
                    nxt = (3 * layers[idx + 1], False)
                else:
                    nxt = (12, True) if final else None
                layer(l, prenormed=(idx > 0 and dbg is None), next_norm=nxt)
        except _Stop:
            pass
        if final and (not layers or dbg is not None):
            norm(12, inplace=True)
        store_x()
        S.wait_all("sp", d_out)
    return nc


_CONSTS = None


def _run(layers, final, x, inputs):
    global _CONSTS
    if _CONSTS is None:
        _CONSTS = host_consts()
    cst, rope = _CONSTS
    nc = build(layers, final)
    gl = []
    for l in range(DEPTH):
        gl += [inputs[f"attn_norm_{l}"], inputs[f"mlp_norm_{l}"], inputs[f"ple_norm_{l}"]]
    gl.append(inputs["final_norm"])
    gains = np.ascontiguousarray(np.stack(gl).astype(np.float32).reshape(13 * 8, 128))
    shared = dict(cst=cst, rope=rope, gains=gains,
                  sinks=np.ascontiguousarray(inputs["sinks_1"].reshape(1, 16)),
                  bfor=np.ascontiguousarray(inputs["b_forget_2"].reshape(1, 16)))
    for l in layers:
        shared[f"w_in_{l}"] = inputs[f"w_in_{l}"]
        shared[f"w_out_{l}"] = inputs[f"w_out_{l}"]
        shared[f"w_up_{l}"] = inputs[f"w_up_{l}"]
        shared[f"w_down_{l}"] = inputs[f"w_down_{l}"]
        shared[f"w_ple_gate_{l}"] = inputs[f"w_ple_gate_{l}"]
        shared[f"w_ple_proj_{l}"] = inputs[f"w_ple_proj_{l}"]
    p = inputs["p"]
    in_maps = []
    for c in range(NCORES):
        m = dict(shared)
        m["x"] = np.ascontiguousarray(x[c])
        m["p"] = np.ascontiguousarray(p[:, c])
        in_maps.append(m)
    res = run_bass_kernel_spmd(nc, in_maps, core_ids=list(range(NCORES)))
    return np.stack([np.asarray(r["y"]) for r in res.results], axis=0).astype(np.float32)


def kernel(**inputs):
    inputs = {k: np.asarray(v) for k, v in inputs.items()}
    x = np.ascontiguousarray(inputs["x"], dtype=np.float32)
    if FUSED:
        return _run(list(range(DEPTH)), True, x, inputs)
    for l in range(DEPTH):
        x = _run([l], l == DEPTH - 1, x, inputs)
    return x
```
